# Optimizing a Trainium2 kernel written in Bass

```python
import math
import jax, jax.numpy as jnp
from jax import lax
import numpy as np

D_MODEL = 1024
BATCH = 32
SEQ = 2048
DEPTH = 4
DEC_BATCH = 1
DEC_SEQ = 16384
PAST_LEN = 128

N_MEM = 256
GRID_W = 64
CHUNK = 128
RET_HEADS = 8
RET_DK = 128
RET_DV = 128
RET_W = RET_HEADS * RET_DV
SG_GROUPS = 4
SG_W = 1024
ATT_HEADS = 8
ATT_KV_HEADS = 2
ATT_HD = 128
ATT_W = ATT_HEADS * ATT_HD
ATT_KV_W = ATT_KV_HEADS * ATT_HD
X_HEADS = 4
X_HD = D_MODEL // X_HEADS
D_FF = -(-8 * D_MODEL // (3 * 256)) * 256
N_BRANCH = 3
ALPHA = (2 * DEPTH) ** 0.25
BETA = (8 * DEPTH) ** -0.25
ROPE_BASE = 10000.0
LN_EPS = 1e-5
RMS_EPS = 1e-6
SPLIT_POINTS = (RET_W, 2 * RET_W, 3 * RET_W, 4 * RET_W,
                4 * RET_W + SG_W, 4 * RET_W + 2 * SG_W,
                4 * RET_W + 2 * SG_W + ATT_W,
                4 * RET_W + 2 * SG_W + ATT_W + ATT_KV_W,
                4 * RET_W + 2 * SG_W + ATT_W + 2 * ATT_KV_W)
D_IN = 4 * RET_W + 2 * SG_W + ATT_W + 2 * ATT_KV_W + N_BRANCH * D_MODEL

kernel_name = "hybrid_gated_encoder"


def layer_norm(x, w, b):
    xf = x.astype(jnp.float32)
    mu = jnp.mean(xf, axis=-1, keepdims=True)
    var = jnp.mean(jnp.square(xf - mu), axis=-1, keepdims=True)
    y = (xf - mu) * lax.rsqrt(var + LN_EPS)
    return (y * w.astype(jnp.float32) + b.astype(jnp.float32)).astype(x.dtype)


def rms_norm(x, w):
    xf = x.astype(jnp.float32)
    y = xf * lax.rsqrt(jnp.mean(jnp.square(xf), axis=-1, keepdims=True) + RMS_EPS)
    return (y * w.astype(jnp.float32)).astype(x.dtype)


def rope_cos_sin(pos, dim):
    inv_freq = ROPE_BASE ** (-jnp.arange(0, dim, 2, dtype=jnp.float32) / dim)
    ang = pos.astype(jnp.float32)[:, None] * inv_freq[None, :]
    return jnp.cos(ang), jnp.sin(ang)


def apply_rope(x, cos, sin):
    x1, x2 = jnp.split(x, 2, axis=-1)
    c = cos.astype(x.dtype)
    s = sin.astype(x.dtype)
    return jnp.concatenate([x1 * c - x2 * s, x1 * s + x2 * c], axis=-1)


def axial_rope(x, cos_r, sin_r, cos_c, sin_c):
    half = x.shape[-1] // 2
    xr = apply_rope(x[..., :half], cos_r[:, None], sin_r[:, None])
    xc = apply_rope(x[..., half:], cos_c[:, None], sin_c[:, None])
    return jnp.concatenate([xr, xc], axis=-1)


def retention_direction(q, k, v, log_gamma, strict):
    B, H, S, dk = q.shape
    dv = v.shape[-1]
    n = S // CHUNK
    qc = q.reshape(B, H, n, CHUNK, dk)
    kc = k.reshape(B, H, n, CHUNK, dk)
    vc = v.reshape(B, H, n, CHUNK, dv)
    idx = jnp.arange(CHUNK, dtype=jnp.float32)
    diff = idx[:, None] - idx[None, :]
    lg = log_gamma.astype(jnp.float32)
    allowed = (diff > 0) if strict else (diff >= 0)
    dmat = jnp.where(allowed[None], jnp.exp(lg[:, None, None] * jnp.maximum(diff, 0.0)[None]), 0.0)
    dmat = dmat.astype(q.dtype)
    scores = jnp.einsum('bhncd,bhnmd->bhncm', qc, kc) * dmat[None, :, None]
    intra = jnp.einsum('bhncm,bhnme->bhnce', scores, vc)
    zeta = jnp.exp(lg[:, None] * (CHUNK - 1 - idx)[None]).astype(q.dtype)
    xi = jnp.exp(lg[:, None] * (idx + 1.0)[None]).astype(q.dtype)
    kv = jnp.einsum('bhncd,bhnce->nbhde', kc * zeta[None, :, None, :, None], vc)
    chunk_decay = jnp.exp(lg * CHUNK).astype(q.dtype)[None, :, None, None]

    def step(state, kv_i):
        return state * chunk_decay + kv_i, state

    _, prev = lax.scan(step, jnp.zeros((B, H, dk, dv), kv.dtype), kv)
    inter = jnp.einsum('bhncd,nbhde->bhnce', qc * xi[None, :, None, :, None], prev)
    return (intra + inter).reshape(B, H, S, dv)


def bidirectional_retention(q, k, v, log_gamma_f, log_gamma_b):
    fwd = retention_direction(q, k, v, log_gamma_f, False)
    flip = lambda t: jnp.flip(t, axis=2)
    bwd = flip(retention_direction(flip(q), flip(k), flip(v), log_gamma_b, True))
    return fwd + bwd


def head_group_norm(o, w):
    B, H, S, dv = o.shape
    of = o.astype(jnp.float32)
    mu = jnp.mean(of, axis=-1, keepdims=True)
    var = jnp.mean(jnp.square(of - mu), axis=-1, keepdims=True)
    y = ((of - mu) * lax.rsqrt(var + LN_EPS)).transpose(0, 2, 1, 3).reshape(B, S, H * dv)
    return (y * w.astype(jnp.float32)).astype(o.dtype)


def gqa_block_attention(q, k, v):
    B, S, H, hd = q.shape
    G = H // ATT_KV_HEADS
    qb = q.reshape(B, S // CHUNK, CHUNK, ATT_KV_HEADS, G, hd).transpose(1, 0, 3, 4, 2, 5)
    kt = k.transpose(0, 2, 1, 3)
    vt = v.transpose(0, 2, 1, 3)
    scale = hd ** -0.5

    def block(qi):
        s = jnp.einsum('bkgqd,bksd->bkgqs', qi, kt).astype(jnp.float32) * scale
        p = jax.nn.softmax(s, axis=-1).astype(vt.dtype)
        return jnp.einsum('bkgqs,bksd->bkgqd', p, vt)

    out = lax.map(block, qb)
    return out.transpose(1, 0, 4, 2, 3, 5).reshape(B, S, H * hd)


def token_mixer(x, l, p, rope):
    B, S, _ = x.shape
    cos_t, sin_t, cos_r, sin_r, cos_c, sin_c = rope
    h = x @ p['w_in'][l]
    rq, rk, rv, rg, su, sv, aq, ak, av, gate_logits = jnp.split(h, SPLIT_POINTS, axis=-1)

    heads = lambda t: t.reshape(B, S, RET_HEADS, -1).transpose(0, 2, 1, 3)
    rq = apply_rope(heads(rq), cos_t, sin_t)
    rk = apply_rope(heads(rk), cos_t, sin_t) * (RET_DK ** -0.5)
    ro = bidirectional_retention(rq, rk, heads(rv),
                                 jax.nn.log_sigmoid(p['ret_decay_f'][l].astype(jnp.float32)),
                                 jax.nn.log_sigmoid(p['ret_decay_b'][l].astype(jnp.float32)))
    ro = head_group_norm(ro, p['ret_gn_w'][l])
    ret_out = (jax.nn.silu(rg) * ro) @ p['ret_wo'][l]

    su = jax.nn.gelu(su)
    sv = layer_norm(jax.nn.gelu(sv), p['sg_ln_w'][l], p['sg_ln_b'][l])
    svc = sv.reshape(B, S // CHUNK, CHUNK, SG_GROUPS, SG_W // SG_GROUPS)
    mixed = jnp.einsum('gcm,bnmgd->bncgd', p['sg_ws'][l], svc) + p['sg_b'][l].T[:, :, None]
    sg_out = (su * mixed.reshape(B, S, SG_W)) @ p['sg_wo'][l]

    aq = rms_norm(aq.reshape(B, S, ATT_HEADS, ATT_HD), p['att_qn_w'][l])
    ak = rms_norm(ak.reshape(B, S, ATT_KV_HEADS, ATT_HD), p['att_kn_w'][l])
    av = av.reshape(B, S, ATT_KV_HEADS, ATT_HD)
    aq = axial_rope(aq, cos_r, sin_r, cos_c, sin_c)
    ak = axial_rope(ak, cos_r, sin_r, cos_c, sin_c)
    att_out = gqa_block_attention(aq, ak, av) @ p['att_wo'][l]

    gates = jax.nn.sigmoid(gate_logits + p['b_gate'][l]).reshape(B, S, N_BRANCH, D_MODEL)
    merged = gates[:, :, 0] * ret_out + gates[:, :, 1] * sg_out + gates[:, :, 2] * att_out
    return merged @ p['w_out'][l]


def cross_attention(x, mem, wq, wkv, wo):
    B, S, _ = x.shape
    M = mem.shape[1]
    q = (x @ wq).reshape(B, S, X_HEADS, X_HD)
    k, v = jnp.split(mem @ wkv, 2, axis=-1)
    k = k.reshape(B, M, X_HEADS, X_HD)
    v = v.reshape(B, M, X_HEADS, X_HD)
    s = jnp.einsum('bshd,bmhd->bhsm', q, k).astype(jnp.float32) * (X_HD ** -0.5)
    pr = jax.nn.softmax(s, axis=-1).astype(v.dtype)
    o = jnp.einsum('bhsm,bmhd->bshd', pr, v).reshape(B, S, D_MODEL)
    return o @ wo


def swiglu(x, w_in, w_out):
    a, b = jnp.split(x @ w_in, 2, axis=-1)
    return (jax.nn.silu(a) * b) @ w_out


def encoder(x, mem, p):
    S = x.shape[1]
    rows = S // GRID_W
    grid_r, grid_c = jnp.meshgrid(jnp.arange(rows), jnp.arange(GRID_W), indexing='ij')
    row = grid_r.reshape(S)
    col = grid_c.reshape(S)
    t = jnp.arange(S)
    cos_t, sin_t = rope_cos_sin(t, RET_DK)
    cos_r, sin_r = rope_cos_sin(row, ATT_HD // 2)
    cos_c, sin_c = rope_cos_sin(col, ATT_HD // 2)
    rope = (cos_t, sin_t, cos_r, sin_r, cos_c, sin_c)
    x = layer_norm(x, p['in_ln_w'], p['in_ln_b'])
    for l in range(DEPTH):
        x = layer_norm(ALPHA * x + token_mixer(x, l, p, rope), p['ln_w'][l, 0], p['ln_b'][l, 0])
        x = layer_norm(ALPHA * x + cross_attention(x, mem, p['xa_wq'][l], p['xa_wkv'][l], p['xa_wo'][l]),
                       p['ln_w'][l, 1], p['ln_b'][l, 1])
        x = layer_norm(ALPHA * x + swiglu(x, p['ffn_w_in'][l], p['ffn_w_out'][l]),
                       p['ln_w'][l, 2], p['ln_b'][l, 2])
    return x


def setup_inputs(seed: int = 0) -> dict:
    key = jax.random.key(seed)
    ks = jax.random.split(key, 32)
    nrm = lambda k, shape, scale: jax.random.normal(k, shape, jnp.float32) * scale
    base_logit = jnp.log(jnp.exp2(5.0 + jnp.arange(RET_HEADS, dtype=jnp.float32)) - 1.0)
    return {
        "x_prompt": nrm(ks[0], (BATCH, SEQ, D_MODEL), 1.0),
        "x_sample": nrm(ks[1], (DEC_BATCH, DEC_SEQ, D_MODEL), 1.0),
        "mem_prompt": nrm(ks[2], (BATCH, N_MEM, D_MODEL), 1.0),
        "mem_sample": nrm(ks[3], (DEC_BATCH, N_MEM, D_MODEL), 1.0),
        "in_ln_w": 1.0 + nrm(ks[4], (D_MODEL,), 0.02),
        "in_ln_b": nrm(ks[5], (D_MODEL,), 0.02),
        "w_in": nrm(ks[6], (DEPTH, D_MODEL, D_IN), D_MODEL ** -0.5),
        "b_gate": nrm(ks[7], (DEPTH, N_BRANCH * D_MODEL), 0.02),
        "ret_decay_f": base_logit[None] + nrm(ks[8], (DEPTH, RET_HEADS), 0.1),
        "ret_decay_b": base_logit[None] + nrm(ks[9], (DEPTH, RET_HEADS), 0.1),
        "ret_gn_w": 1.0 + nrm(ks[10], (DEPTH, RET_W), 0.02),
        "ret_wo": nrm(ks[11], (DEPTH, RET_W, D_MODEL), BETA * RET_W ** -0.5),
        "sg_ln_w": 1.0 + nrm(ks[12], (DEPTH, SG_W), 0.02),
        "sg_ln_b": nrm(ks[13], (DEPTH, SG_W), 0.02),
        "sg_ws": nrm(ks[14], (DEPTH, SG_GROUPS, CHUNK, CHUNK), CHUNK ** -0.5),
        "sg_b": 1.0 + nrm(ks[15], (DEPTH, SG_GROUPS, CHUNK), 0.02),
        "sg_wo": nrm(ks[16], (DEPTH, SG_W, D_MODEL), BETA * SG_W ** -0.5),
        "att_qn_w": 1.0 + nrm(ks[17], (DEPTH, ATT_HD), 0.02),
        "att_kn_w": 1.0 + nrm(ks[18], (DEPTH, ATT_HD), 0.02),
        "att_wo": nrm(ks[19], (DEPTH, ATT_W, D_MODEL), BETA * ATT_W ** -0.5),
        "w_out": nrm(ks[20], (DEPTH, D_MODEL, D_MODEL), BETA * D_MODEL ** -0.5),
        "ln_w": 1.0 + nrm(ks[21], (DEPTH, 3, D_MODEL), 0.02),
        "ln_b": nrm(ks[22], (DEPTH, 3, D_MODEL), 0.02),
        "xa_wq": nrm(ks[23], (DEPTH, D_MODEL, D_MODEL), D_MODEL ** -0.5),
        "xa_wkv": nrm(ks[24], (DEPTH, D_MODEL, 2 * D_MODEL), D_MODEL ** -0.5),
        "xa_wo": nrm(ks[25], (DEPTH, D_MODEL, D_MODEL), BETA * D_MODEL ** -0.5),
        "ffn_w_in": nrm(ks[26], (DEPTH, D_MODEL, 2 * D_FF), D_MODEL ** -0.5),
        "ffn_w_out": nrm(ks[27], (DEPTH, D_FF, D_MODEL), BETA * D_FF ** -0.5),
    }


def reference(x_prompt, x_sample, mem_prompt, mem_sample, in_ln_w, in_ln_b, w_in, b_gate,
              ret_decay_f, ret_decay_b, ret_gn_w, ret_wo, sg_ln_w, sg_ln_b, sg_ws, sg_b, sg_wo,
              att_qn_w, att_kn_w, att_wo, w_out, ln_w, ln_b, xa_wq, xa_wkv, xa_wo,
              ffn_w_in, ffn_w_out):
    p = dict(in_ln_w=in_ln_w, in_ln_b=in_ln_b, w_in=w_in, b_gate=b_gate,
             ret_decay_f=ret_decay_f, ret_decay_b=ret_decay_b, ret_gn_w=ret_gn_w, ret_wo=ret_wo,
             sg_ln_w=sg_ln_w, sg_ln_b=sg_ln_b, sg_ws=sg_ws, sg_b=sg_b, sg_wo=sg_wo,
             att_qn_w=att_qn_w, att_kn_w=att_kn_w, att_wo=att_wo, w_out=w_out,
             ln_w=ln_w, ln_b=ln_b, xa_wq=xa_wq, xa_wkv=xa_wkv, xa_wo=xa_wo,
             ffn_w_in=ffn_w_in, ffn_w_out=ffn_w_out)
    y_prompt = encoder(x_prompt, mem_prompt, p)
    y_sample = encoder(x_sample, mem_sample, p)
    return (y_prompt, y_sample)
```

```python
import math
from contextlib import ExitStack

import numpy as np
import concourse.bass as bass
import concourse.mybir as mybir
from concourse.bass_utils import run_bass_kernel_spmd

F32 = mybir.dt.float32
BF16 = mybir.dt.bfloat16
AF = mybir.ActivationFunctionType
ALU = mybir.AluOpType
AX = mybir.AxisListType

D = 1024
N_MEM = 256
GRID_W = 64
D_FF = 2816
D_IN = 10752
LN_EPS = 1e-5
RMS_EPS = 1e-6
ROPE_BASE = 10000.0


class Cfg:
    def __init__(self, nseg_p=4, seq=2048, sseg=2048, depth=4, ncores=8):
        self.NP = nseg_p
        self.SEQ = seq
        self.SSEG = sseg
        self.DEPTH = depth
        self.NC = ncores
        self.TP = seq // 128
        self.TS = sseg // 128
        self.NT = nseg_p * self.TP + self.TS
        self.T = self.NT * 128
        self.DSEQ = sseg * ncores
        self.ALPHA = (2 * depth) ** 0.25
        self.NSEG = nseg_p + 1

    def seg_tiles(self, s):
        if s < self.NP:
            return list(range(s * self.TP, (s + 1) * self.TP))
        return list(range(self.NP * self.TP, self.NT))

    def tile_seg(self, i):
        return min(i // self.TP, self.NP) if self.TP > 0 else self.NP

    def tab_row(self, i):
        if i < self.NP * self.TP:
            return i % self.TP
        return self.TP + (i - self.NP * self.TP)


class Tracker:
    NQ = 12

    def __init__(self, nc):
        self.nc = nc
        self.engs = {"pe": nc.tensor, "act": nc.scalar, "dve": nc.vector, "pool": nc.gpsimd, "sp": nc.sync}
        self.semh = {}
        self.ccnt = {}
        self.opidx = {}
        for e in ("pe", "act", "dve", "pool"):
            self.semh[("c", e)] = nc.alloc_semaphore("c_" + e)
            self.ccnt[e] = 0
            self.opidx[e] = 0
        self.dcnt = {}
        self.drr = {}
        for q in ("sp", "pool"):
            self.dcnt[q] = [0] * self.NQ
            self.drr[q] = 0
            for k in range(self.NQ):
                self.semh[("d", q, k)] = nc.alloc_semaphore("d_%s%d" % (q, k))
        self.semh[("cc",)] = nc.alloc_semaphore("ccsem")
        self.cccnt = 0
        self.waited = {e: {} for e in self.engs}
        self.lastw = {}
        self.readers = {}
        self.ninstr = 0

    def _wait(self, eng, t):
        semkey, val, teng, tidx = t
        if val <= 0:
            return
        if teng is not None and teng == eng:
            if eng == "pe":
                return
            if self.opidx[eng] - tidx > 2:
                return
        if self.waited[eng].get(semkey, 0) >= val:
            return
        self.engs[eng].wait_ge(self.semh[semkey], val)
        self.ninstr += 1
        self.waited[eng][semkey] = val

    def _deps(self, eng, reads, writes):
        for c in reads:
            t = self.lastw.get(c)
            if t is not None:
                self._wait(eng, t)
        for c in writes:
            t = self.lastw.get(c)
            if t is not None:
                self._wait(eng, t)
            rd = self.readers.get(c)
            if rd:
                for t in rd.values():
                    self._wait(eng, t)

    def _commit(self, t, reads, writes):
        for c in reads:
            self.readers.setdefault(c, {})[t[0]] = t
        for c in writes:
            self.lastw[c] = t
            self.readers[c] = {}

    def op(self, eng, reads, writes, fn):
        ps = [c for c in reads if c[0] in ("PA", "PT", "PB")]
        if ps:
            reads = [c for c in reads if c[0] not in ("PA", "PT", "PB")]
            writes = list(writes) + ps
        self._deps(eng, reads, writes)
        ins = fn(self.engs[eng])
        self.ccnt[eng] += 1
        ins.then_inc(self.semh[("c", eng)], 1)
        t = (("c", eng), self.ccnt[eng], eng, self.opidx[eng])
        self.opidx[eng] += 1
        self.ninstr += 1
        self._commit(t, reads, writes)

    def dma(self, q, reads, writes, fn):
        self._deps(q, reads, writes)
        k = self.drr[q]
        self.drr[q] = (k + 1) % self.NQ
        semkey = ("d", q, k)
        self._wait(q, (semkey, self.dcnt[q][k], None, 0))
        inss = fn(self.engs[q])
        if not isinstance(inss, (list, tuple)):
            inss = [inss]
        for ins in inss:
            ins.then_inc(self.semh[semkey], 16)
        self.dcnt[q][k] += 16 * len(inss)
        self.ninstr += len(inss)
        t = (semkey, self.dcnt[q][k], None, 0)
        self._commit(t, reads, writes)

    def collective(self, reads, writes, fn):
        self._deps("pool", reads, writes)
        ins = fn(self.engs["pool"])
        ins.then_inc(self.semh[("cc",)], 1)
        self.cccnt += 1
        t = (("cc",), self.cccnt, None, 0)
        self._commit(t, reads, writes)

    def barrier(self):
        for e in self.engs:
            for ce in ("pe", "act", "dve", "pool"):
                self._wait(e, (("c", ce), self.ccnt[ce], None, 0))
            for q in ("sp", "pool"):
                for k in range(self.NQ):
                    self._wait(e, (("d", q, k), self.dcnt[q][k], None, 0))
            self._wait(e, (("cc",), self.cccnt, None, 0))
        self.lastw = {}
        self.readers = {}


class Buf:
    def __init__(self, tensors, name):
        self.ts = tensors
        self.name = name
        self.n = len(tensors)

    def t(self, i=0):
        return self.ts[i % self.n]

    def c(self, i=0, sub=0):
        return (self.name, i % self.n, sub)


class Stage:
    def __init__(self, nc, tr, tag):
        self.nc = nc
        self.tr = tr
        self.tag = tag
        self.es = ExitStack()

    def sb(self, name, shape, dtype, nslots=1):
        ts = [self.es.enter_context(self.nc.sbuf_tensor("%s_%s_%d" % (self.tag, name, k), list(shape), dtype))
              for k in range(nslots)]
        return Buf(ts, self.tag + name)

    def close(self):
        self.tr.barrier()
        self.es.close()


def build(cfg, debug=(), stop_after=None):
    nc = bass.Bass("TRN2", target_bir_lowering=False)
    tr = Tracker(nc)
    T, NT, L = cfg.T, cfg.NT, cfg.DEPTH
    NSEG = cfg.NSEG
    ALPHA = cfg.ALPHA

    def din(name, shape, dt=F32):
        return nc.dram_tensor(name, list(shape), dt, kind="ExternalInput").ap()

    def dscr(name, shape, dt):
        kind = "ExternalOutput" if name in debug else "Internal"
        return nc.dram_tensor(name, list(shape), dt, kind=kind).ap()

    x_in = din("x", [T, D])
    mem_in = din("mem", [NSEG * N_MEM, D])
    in_ln_w = din("in_ln_w", [1, D])
    in_ln_b = din("in_ln_b", [1, D])
    w_in = din("w_in", [L, D, D_IN])
    b_gate = din("b_gate", [L, 3 * D])
    ret_decay = din("ret_decay", [L, 16])
    ret_gn_w = din("ret_gn_w", [L, D])
    ret_wo = din("ret_wo", [L, D, D])
    sg_ln_w = din("sg_ln_w", [L, D])
    sg_ln_b = din("sg_ln_b", [L, D])
    sg_ws = din("sg_ws", [L, 4, 128, 128])
    sg_b = din("sg_b", [L, 4, 128])
    sg_wo = din("sg_wo", [L, D, D])
    att_qn_w = din("att_qn_w", [L, 128])
    att_kn_w = din("att_kn_w", [L, 128])
    att_wo = din("att_wo", [L, D, D])
    w_out = din("w_out", [L, D, D])
    ln_w = din("ln_w", [L, 3, D])
    ln_b = din("ln_b", [L, 3, D])
    xa_wq = din("xa_wq", [L, D, D])
    xa_wkv = din("xa_wkv", [L, D, 2 * D])
    xa_wo = din("xa_wo", [L, D, D])
    ffn_w_in = din("ffn_w_in", [L, D, 2 * D_FF])
    ffn_w_out = din("ffn_w_out", [L, D_FF, D])
    NTAB = cfg.TP + cfg.TS
    tab_r = din("tab_r", [NTAB * 128, 512])
    tab_a = din("tab_a", [NTAB * 128, 256])
    ctab = din("ctab", [128, 1024])
    cctab = din("cctab", [128, 32])
    ident_in = din("ident", [128, 128])

    y_out = nc.dram_tensor("y", [T, D], F32, kind="ExternalOutput").ap()

    X = dscr("X", [T, D], F32)
    X1 = dscr("X1", [T, D], F32)
    X2 = dscr("X2", [T, D], F32)
    RQ = dscr("RQ", [T, D], BF16)
    RK = dscr("RK", [T, D], BF16)
    RV = dscr("RV", [T, D], BF16)
    RG = dscr("RG", [T, D], BF16)
    SU = dscr("SU", [T, D], BF16)
    SV = dscr("SV", [T, D], BF16)
    AQ = dscr("AQ", [T, D], BF16)
    AKV = dscr("AKV", [T, 512], BF16)
    GT = dscr("GT", [T, 3 * D], BF16)
    SF = dscr("STF", [NT * 128, D], BF16)
    SB = dscr("STB", [NT * 128, D], BF16)
    YR = dscr("YR", [T, D], BF16)
    YS = dscr("YS", [T, D], BF16)
    YAT = dscr("YAT", [NT * 128, D], BF16)
    MK = dscr("MK", [NSEG * 2 * 128, D], BF16)
    MV = dscr("MV", [NSEG * 2 * 128, D], BF16)
    CR = max(cfg.SSEG, 1024)
    CCin = nc.dram_tensor("CCin", [CR, 256], F32)
    CCout = nc.dram_tensor("CCout", [CR * cfg.NC, 256], F32)
    CCin_bf = CCin.ap().bitcast(BF16)
    CCout_bf = CCout.ap().bitcast(BF16)
    Eloc_v = CCin.ap()[0:1024, :].rearrange("(j q) c -> j (q c)", q=4)

    def Eall_v(c2):
        return CCout.ap()[c2 * CR:c2 * CR + 1024, :].rearrange("(j q) c -> j (q c)", q=4)

    def allgather():
        tr.collective([("ccin",)], [("ccout",)], lambda e: e.collective_compute(
            "AllGather", ALU.bypass, replica_groups=[list(range(cfg.NC))],
            ins=[CCin.ap().opt()], outs=[CCout.ap().opt()]))

    gs = ExitStack()

    def gsb(name, shape, dt):
        return gs.enter_context(nc.sbuf_tensor(name, list(shape), dt))

    ident = gsb("identb", [128, 128], BF16)
    ones = gsb("onesb", [128, 128], BF16)
    ctb = gsb("ctb", [128, 1024], F32)
    cct = gsb("cct", [128, 32], F32)
    pa = [gs.enter_context(nc.psum_tensor("pa%d" % k, [128, 1024], F32)) for k in range(2)]
    pt = [gs.enter_context(nc.psum_tensor("pt%d" % k, [128, 1024], BF16)) for k in range(2)]
    pb = [gs.enter_context(nc.psum_tensor("pb%d" % k, [128, 512], F32)) for k in range(2)]
    PA = Buf(pa, "PA")
    PT = Buf(pt, "PT")
    PB = Buf(pb, "PB")
    C_PF, C_MF, C_PB, C_MB, C_RP1, C_RM = 0, 128, 256, 384, 512, 640
    C_Z = 768
    C_I4 = 772

    tr.dma("pool", [], [("ident",)], lambda e: e.dma_start(out=ident[:], in_=ident_in))
    tr.dma("sp", [], [("ctb",)], lambda e: e.dma_start(out=ctb[:], in_=ctab))
    tr.dma("sp", [], [("cct",)], lambda e: e.dma_start(out=cct[:], in_=cctab))
    tr.op("dve", [], [("ones",)], lambda e: e.memset(ones[:], 1.0))
    tr.barrier()

    def V(reads, writes, fn):
        tr.op("dve", reads, writes, fn)

    def A(reads, writes, fn):
        tr.op("act", reads, writes, fn)

    def G(reads, writes, fn):
        tr.op("pool", reads, writes, fn)

    def P(reads, writes, fn):
        tr.op("pe", reads, writes, fn)

    def LD(reads, writes, fn):
        tr.dma("sp", reads, writes, fn)

    def STO(reads, writes, fn):
        tr.dma("sp", reads, writes, fn)

    def rows(ap2d, i):
        return ap2d[i * 128:(i + 1) * 128, :]

    def load_w(st, name, src2d, nk, ncols, col0=0):
        buf = st.sb(name, [128, nk, ncols], BF16)
        w = buf.t()
        step = max(1, 4096 // ncols)
        k = 0
        while k < nk:
            k2 = min(nk, k + step)
            tr.dma("pool", [], [buf.c()],
                   lambda e, k=k, k2=k2: e.dma_start(
                       out=w[:, k:k2, :],
                       in_=src2d[k * 128:k2 * 128, col0:col0 + ncols].rearrange("(k p) c -> p k c", p=128)))
            k = k2
        return buf

    def load_bcast(st, name, src_row, n, dt=F32):
        buf = st.sb(name, [128, n], dt)
        q = "sp" if dt == F32 else "pool"
        tr.dma(q, [], [buf.c()], lambda e: e.dma_start(out=buf.t()[:], in_=src_row.to_broadcast([128, n])))
        return buf

    def transposes(src_fn, n, ptslot, reads):
        def f(e):
            ins = None
            for j in range(n):
                ins = e.transpose(PT.t(ptslot)[:, j * 128:(j + 1) * 128], src_fn(j), ident[:])
            return ins
        P(reads + [("ident",)], [PT.c(ptslot)], f)

    def gemm(lhs_fn, nk, wbuf, col0, ncols, out_ap, reads, writes, extra=None):
        def f(e):
            ins = None
            for k in range(nk):
                ins = e.matmul(out_ap, lhsT=lhs_fn(k), rhs=wbuf.t()[:, k, col0:col0 + ncols],
                               start=(k == 0), stop=(k == nk - 1 and extra is None))
            if extra is not None:
                ins = e.matmul(out_ap, lhsT=extra[0], rhs=extra[1], start=False, stop=True)
            return ins
        P(reads + [wbuf.c()], writes, f)

    pt_rr = [0]
    pa_rr = [0]

    def next_pt():
        pt_rr[0] += 1
        return pt_rr[0]

    def next_pa():
        pa_rr[0] += 1
        return pa_rr[0]

    def to_T(src_buf, slot, nblk, dst_buf, dslot, copy_eng="dve"):
        p = next_pt()
        src = src_buf.t(slot)
        transposes(lambda j: src[:, j * 128:(j + 1) * 128], nblk, p, [src_buf.c(slot)])
        dst = dst_buf.t(dslot)
        fn = lambda e: e.tensor_copy(out=dst[:].rearrange("p a b -> p (a b)")[:, 0:nblk * 128],
                                     in_=PT.t(p)[:, 0:nblk * 128])
        if copy_eng == "act":
            fn = lambda e: e.copy(out=dst[:].rearrange("p a b -> p (a b)")[:, 0:nblk * 128],
                                  in_=PT.t(p)[:, 0:nblk * 128])
        tr.op(copy_eng, [PT.c(p)], [dst_buf.c(dslot)], fn)

    def rsqrt(ap, cells, premul, eps):
        V(cells, cells, lambda e: e.tensor_scalar(out=ap, in0=ap, scalar1=premul, scalar2=eps, op0=ALU.mult, op1=ALU.add))
        A(cells, cells, lambda e: e.activation(out=ap, in_=ap, func=AF.Sqrt))
        V(cells, cells, lambda e: e.reciprocal(out=ap, in_=ap))

    def layer_norm(src_ap, src_cells, st_bufs, slot, wrep, brep, dst_ap, dst_cells, tmp_ap=None, tmp_cells=None):
        stats, mv, rs = st_bufs
        s_t, mv_t, rs_t = stats.t(slot), mv.t(slot), rs.t(slot)
        V(src_cells, [stats.c(slot)], lambda e: (
            e.bn_stats(out=s_t[:, 0:6], in_=src_ap[:, 0:512]),
            e.bn_stats(out=s_t[:, 6:12], in_=src_ap[:, 512:1024]))[-1])
        V([stats.c(slot)], [mv.c(slot)], lambda e: e.bn_aggr(out=mv_t[:, 0:2], in_=s_t[:, 0:12]))
        V([mv.c(slot)], [rs.c(slot)], lambda e: e.tensor_copy(out=rs_t[:, 0:1], in_=mv_t[:, 1:2]))
        rsqrt(rs_t[:, 0:1], [rs.c(slot)], 1.0, LN_EPS)
        t_ap = tmp_ap if tmp_ap is not None else src_ap
        t_cells = tmp_cells if tmp_cells is not None else src_cells
        V(src_cells + [mv.c(slot), rs.c(slot)], t_cells, lambda e: e.tensor_scalar(
            out=t_ap, in0=src_ap, scalar1=mv_t[:, 0:1], scalar2=rs_t[:, 0:1], op0=ALU.subtract, op1=ALU.mult))
        G(t_cells + [wrep.c()], t_cells, lambda e: e.tensor_tensor(out=t_ap, in0=t_ap, in1=wrep.t()[:], op=ALU.mult))
        G(t_cells + [brep.c()], dst_cells, lambda e: e.tensor_tensor(out=dst_ap, in0=t_ap, in1=brep.t()[:], op=ALU.add))

    def ln_bufs(st, tag, nslots=2):
        return (st.sb(tag + "st", [128, 12], F32, nslots), st.sb(tag + "mv", [128, 2], F32, nslots),
                st.sb(tag + "rs", [128, 1], F32, nslots))

    st = Stage(nc, tr, "l0")
    wrep = load_bcast(st, "w", in_ln_w, D)
    brep = load_bcast(st, "b", in_ln_b, D)
    xin = st.sb("xin", [128, D], F32, 3)
    xo = st.sb("xo", [128, D], F32, 2)
    lb = ln_bufs(st, "ln")
    for i in range(NT):
        LD([], [xin.c(i)], lambda e, i=i: e.dma_start(out=xin.t(i)[:], in_=rows(x_in, i)))
        layer_norm(xin.t(i)[:], [xin.c(i)], lb, i, wrep, brep, xo.t(i)[:], [xo.c(i)])
        STO([xo.c(i)], [], lambda e, i=i: e.dma_start(out=rows(X, i), in_=xo.t(i)[:]))
    st.close()
    if stop_after is not None and stop_after == ("l0"):
        tr.barrier()
        return nc, tr

    def load_xT(st, Xsrc, i, xin, xb, xT):
        LD([], [xin.c(i)], lambda e: e.dma_start(out=xin.t(i)[:], in_=rows(Xsrc, i)))
        A([xin.c(i)], [xb.c(i)], lambda e: e.copy(out=xb.t(i)[:], in_=xin.t(i)[:]))
        to_T(xb, i, 8, xT, i)

    for l in range(L):
        lt = "y%d" % l
        ls = Stage(nc, tr, lt + "p")
        lgt = ls.sb("lg", [128, 16], F32)
        tmp16 = ls.sb("t16", [128, 16], F32)
        tr.dma("sp", [], [tmp16.c()], lambda e: e.dma_start(
            out=tmp16.t()[:], in_=ret_decay[l:l + 1, :].to_broadcast([128, 16])))
        A([tmp16.c()], [lgt.c()], lambda e: e.activation(out=lgt.t()[:], in_=tmp16.t()[:], func=AF.Exp, scale=-1.0))
        V([lgt.c()], [tmp16.c()], lambda e: e.tensor_scalar(
            out=tmp16.t()[:], in0=lgt.t()[:], scalar1=1.0, scalar2=None, op0=ALU.add))
        A([tmp16.c()], [lgt.c()], lambda e: e.activation(out=lgt.t()[:], in_=tmp16.t()[:], func=AF.Ln))
        V([lgt.c()], [lgt.c()], lambda e: e.tensor_scalar(
            out=lgt.t()[:], in0=lgt.t()[:], scalar1=-1.0, scalar2=None, op0=ALU.mult))
        lg = lgt.t()
        DTb = ls.sb("DT", [128, 8, 128], F32)
        dtmp = ls.sb("dtmp", [128, 8, 128], F32)
        for h in range(8):
            A([lgt.c(), ("ctb",)], [DTb.c()], lambda e, h=h: e.activation(
                out=DTb.t()[:, h, :], in_=ctb[:, C_PF:C_PF + 128], func=AF.Exp, scale=lg[:, h:h + 1]))
            A([lgt.c(), ("ctb",)], [dtmp.c()], lambda e, h=h: e.activation(
                out=dtmp.t()[:, h, :], in_=ctb[:, C_PB:C_PB + 128], func=AF.Exp, scale=lg[:, 8 + h:9 + h]))
        V([DTb.c(), ("ctb",)], [DTb.c()], lambda e: e.tensor_tensor(
            out=DTb.t()[:], in0=DTb.t()[:], in1=ctb[:, C_MF:C_MF + 128].unsqueeze(1).to_broadcast([128, 8, 128]),
            op=ALU.mult))
        V([dtmp.c(), ("ctb",)], [dtmp.c()], lambda e: e.tensor_tensor(
            out=dtmp.t()[:], in0=dtmp.t()[:], in1=ctb[:, C_MB:C_MB + 128].unsqueeze(1).to_broadcast([128, 8, 128]),
            op=ALU.mult))
        V([DTb.c(), dtmp.c()], [DTb.c()], lambda e: e.tensor_tensor(
            out=DTb.t()[:], in0=DTb.t()[:], in1=dtmp.t()[:], op=ALU.add))
        XIF = ls.sb("XIF", [128, 8, 128], F32)
        XIB = ls.sb("XIB", [128, 8, 128], F32)
        for h in range(8):
            A([lgt.c(), ("ctb",)], [XIF.c()], lambda e, h=h: e.activation(
                out=XIF.t()[:, h, :], in_=ctb[:, C_RP1:C_RP1 + 128], func=AF.Exp, scale=lg[:, h:h + 1]))
            A([lgt.c(), ("ctb",)], [XIB.c()], lambda e, h=h: e.activation(
                out=XIB.t()[:, h, :], in_=ctb[:, C_RM:C_RM + 128], func=AF.Exp, scale=lg[:, 8 + h:9 + h]))
        ZT = ls.sb("ZT", [128, 16], F32)
        CDt = ls.sb("CD", [128, 16], F32)
        A([lgt.c(), ("ctb",)], [ZT.c()], lambda e: e.activation(
            out=ZT.t()[:, 0:8], in_=lg[:, 0:8], func=AF.Exp, scale=ctb[:, C_Z:C_Z + 1]))
        A([lgt.c(), ("ctb",)], [ZT.c()], lambda e: e.activation(
            out=ZT.t()[:, 8:16], in_=lg[:, 8:16], func=AF.Exp, scale=ctb[:, C_Z + 1:C_Z + 2]))
        A([lgt.c()], [CDt.c()], lambda e: e.activation(out=CDt.t()[:], in_=lg[:], func=AF.Exp, scale=128.0))
        CF = ls.sb("CF", [128, 8, 8], F32)
        CB = ls.sb("CB", [128, 8, 8], F32)
        for c2 in range(cfg.NC):
            A([lgt.c(), ("cct",)], [CF.c()], lambda e, c2=c2: e.activation(
                out=CF.t()[:, c2, :], in_=lg[:, 0:8], func=AF.Exp, scale=cct[:, c2:c2 + 1]))
            A([lgt.c(), ("cct",)], [CB.c()], lambda e, c2=c2: e.activation(
                out=CB.t()[:, c2, :], in_=lg[:, 8:16], func=AF.Exp, scale=cct[:, 8 + c2:9 + c2]))
        V([CF.c(), ("cct",)], [CF.c()], lambda e: e.tensor_tensor(
            out=CF.t()[:], in0=CF.t()[:], in1=cct[:, 16:24].unsqueeze(2).to_broadcast([128, 8, 8]), op=ALU.mult))
        V([CB.c(), ("cct",)], [CB.c()], lambda e: e.tensor_tensor(
            out=CB.t()[:], in0=CB.t()[:], in1=cct[:, 24:32].unsqueeze(2).to_broadcast([128, 8, 8]), op=ALU.mult))
        wq_rep = ls.sb("wqr", [128, 128], F32)
        wk_rep = ls.sb("wkr", [128, 128], F32)
        tr.dma("sp", [], [wq_rep.c()], lambda e: e.dma_start(
            out=wq_rep.t()[:], in_=att_qn_w[l:l + 1, :].to_broadcast([128, 128])))
        tr.dma("sp", [], [wk_rep.c()], lambda e: e.dma_start(
            out=wk_rep.t()[:], in_=att_kn_w[l:l + 1, :].to_broadcast([128, 128])))
        mx = ls.sb("mx", [128, 4], F32)
        V([wq_rep.c()], [mx.c(0, 1)], lambda e: e.tensor_reduce(
            out=mx.t()[:, 0:1], in_=wq_rep.t()[:], axis=AX.X, op=ALU.max, apply_absolute_value=True))
        V([wk_rep.c()], [mx.c(0, 2)], lambda e: e.tensor_reduce(
            out=mx.t()[:, 1:2], in_=wk_rep.t()[:], axis=AX.X, op=ALU.max, apply_absolute_value=True))
        V([mx.c(0, 1), mx.c(0, 2)], [mx.c(0, 3)], lambda e: e.tensor_scalar(
            out=mx.t()[:, 2:3], in0=mx.t()[:, 0:1], scalar1=mx.t()[:, 1:2], scalar2=-math.sqrt(128.0),
            op0=ALU.mult, op1=ALU.mult))
        negM = mx.t()[:, 2:3]
        negM_c = mx.c(0, 3)
        wq_sw = ls.sb("wqs", [128, 128], F32)
        wk_sw = ls.sb("wks", [128, 128], F32)
        for (src, dst) in ((wq_rep, wq_sw), (wk_rep, wk_sw)):
            s4 = src.t()[:].rearrange("p (a h b) -> p a h b", a=2, h=2)
            d4 = dst.t()[:].rearrange("p (a h b) -> p a h b", a=2, h=2)
            V([src.c()], [dst.c()], lambda e, s4=s4, d4=d4: (
                e.tensor_copy(out=d4[:, :, 0, :], in_=s4[:, :, 1, :]),
                e.tensor_copy(out=d4[:, :, 1, :], in_=s4[:, :, 0, :]))[-1])
        tr.barrier()

        st = Stage(nc, tr, lt + "m")
        wkv = load_w(st, "wkv", xa_wkv[l], 8, 2 * D)
        xin = st.sb("xin", [128, D], F32, 2)
        xb = st.sb("xb", [128, D], BF16, 2)
        xT = st.sb("xT", [128, 8, 128], BF16, 2)
        kb_ = st.sb("kb", [128, D], BF16, 2)
        kT = st.sb("kT", [128, 8, 128], BF16, 2)
        vb_ = st.sb("vb", [128, D], BF16, 2)
        for i in range(NSEG * 2):
            load_xT(st, mem_in, i, xin, xb, xT)
            for half, (dst, eng) in enumerate(((kb_, "act"), (vb_, "dve"))):
                p = next_pa()
                for cbk in range(2):
                    gemm(lambda k: xT.t(i)[:, k, :], 8, wkv, half * D + cbk * 512, 512,
                         PA.t(p)[:, cbk * 512:(cbk + 1) * 512], [xT.c(i)], [PA.c(p, cbk)])
                if eng == "act":
                    A([PA.c(p, 0), PA.c(p, 1)], [dst.c(i)], lambda e, p=p, dst=dst: e.copy(out=dst.t(i)[:], in_=PA.t(p)[:]))
                else:
                    V([PA.c(p, 0), PA.c(p, 1)], [dst.c(i)], lambda e, p=p, dst=dst: e.tensor_copy(out=dst.t(i)[:], in_=PA.t(p)[:]))
            to_T(kb_, i, 8, kT, i, copy_eng="act")
            STO([kT.c(i)], [], lambda e, i=i: e.dma_start(out=rows(MK, i), in_=kT.t(i)[:].rearrange("p a b -> p (a b)")))
            STO([vb_.c(i)], [], lambda e, i=i: e.dma_start(out=rows(MV, i), in_=vb_.t(i)[:]))
        st.close()
        if stop_after is not None and stop_after == (lt + "m"):
            tr.barrier()
            return nc, tr

        st = Stage(nc, tr, lt + "a")
        wa = load_w(st, "w", w_in[l], 8, 4096, 0)
        xin = st.sb("xin", [128, D], F32, 2)
        xb = st.sb("xb", [128, D], BF16, 2)
        xT = st.sb("xT", [128, 8, 128], BF16, 2)
        tabr = st.sb("tabr", [128, 512], F32, 2)
        xs = st.sb("xs", [128, 8, 128], F32, 2)
        t1 = st.sb("t1", [128, 8, 128], F32, 2)
        u = st.sb("u", [128, 8, 128], F32, 2)
        ob = st.sb("ob", [128, D], BF16, 4)
        oc = [0]

        def rope_ret(i, p, tab_off, dstD):
            k = oc[0]
            oc[0] += 1
            tb = tabr.t(i)
            Cb = tb[:, tab_off:tab_off + 128].unsqueeze(1).to_broadcast([128, 8, 128])
            Slo = tb[:, tab_off + 128:tab_off + 192].unsqueeze(1).to_broadcast([128, 8, 64])
            Shi = tb[:, tab_off + 192:tab_off + 256].unsqueeze(1).to_broadcast([128, 8, 64])
            A([PA.c(p, 0), PA.c(p, 1)], [xs.c(k)], lambda e: e.copy(
                out=xs.t(k)[:].rearrange("p a b -> p (a b)"), in_=PA.t(p)[:]))
            V([xs.c(k), tabr.c(i)], [t1.c(k)], lambda e: e.tensor_tensor(out=t1.t(k)[:], in0=xs.t(k)[:], in1=Cb, op=ALU.mult))
            G([xs.c(k), tabr.c(i)], [u.c(k)], lambda e: (
                e.tensor_tensor(out=u.t(k)[:, :, 0:64], in0=xs.t(k)[:, :, 64:128], in1=Slo, op=ALU.mult),
                e.tensor_tensor(out=u.t(k)[:, :, 64:128], in0=xs.t(k)[:, :, 0:64], in1=Shi, op=ALU.mult))[-1])
            V([t1.c(k), u.c(k)], [ob.c(k)], lambda e: e.tensor_tensor(
                out=ob.t(k)[:].rearrange("p (a b) -> p a b", a=8), in0=t1.t(k)[:], in1=u.t(k)[:], op=ALU.add))
            STO([ob.c(k)], [], lambda e: e.dma_start(out=rows(dstD, i), in_=ob.t(k)[:]))

        for i in range(NT):
            load_xT(st, X, i, xin, xb, xT)
            LD([], [tabr.c(i)], lambda e, i=i: e.dma_start(out=tabr.t(i)[:], in_=rows(tab_r, cfg.tab_row(i))))
            for grp in range(4):
                p = next_pa()
                for cbk in range(2):
                    gemm(lambda k: xT.t(i)[:, k, :], 8, wa, grp * 1024 + cbk * 512, 512,
                         PA.t(p)[:, cbk * 512:(cbk + 1) * 512], [xT.c(i)], [PA.c(p, cbk)])
                if grp == 0:
                    rope_ret(i, p, 0, RQ)
                elif grp == 1:
                    rope_ret(i, p, 256, RK)
                else:
                    k = oc[0]
                    oc[0] += 1
                    fn = AF.Copy if grp == 2 else AF.Silu
                    A([PA.c(p, 0), PA.c(p, 1)], [ob.c(k)], lambda e, p=p, k=k, fn=fn: e.activation(
                        out=ob.t(k)[:], in_=PA.t(p)[:], func=fn))
                    dstD = RV if grp == 2 else RG
                    STO([ob.c(k)], [], lambda e, k=k, dstD=dstD, i=i: e.dma_start(out=rows(dstD, i), in_=ob.t(k)[:]))
        st.close()
        if stop_after is not None and stop_after == (lt + "a"):
            tr.barrier()
            return nc, tr

        st = Stage(nc, tr, lt + "b")
        wb = load_w(st, "w", w_in[l], 8, 3584, 4096)
        sgw = load_bcast(st, "sgw", sg_ln_w[l:l + 1, :], D)
        sgb = load_bcast(st, "sgb", sg_ln_b[l:l + 1, :], D)
        xin = st.sb("xin", [128, D], F32, 2)
        xb = st.sb("xb", [128, D], BF16, 2)
        xT = st.sb("xT", [128, 8, 128], BF16, 2)
        taba = st.sb("taba", [128, 256], F32, 2)
        cw = st.sb("cw", [128, 4, 128], F32, 2)
        xs = st.sb("xs", [128, 8, 128], F32, 2)
        sq = st.sb("sq", [128, 8, 128], F32, 2)
        t1 = st.sb("t1", [128, 8, 128], F32, 2)
        u = st.sb("u", [128, 8, 128], F32, 2)
        ss = st.sb("ss", [128, 16], F32, 2)
        svf = st.sb("svf", [128, D], F32, 2)
        ob = st.sb("ob", [128, D], BF16, 4)
        okv = st.sb("okv", [128, 512], BF16, 2)
        lb = ln_bufs(st, "ln")
        oc = [0]

        def axial(i, k, nh, Ci, Si, src3, dst3, rst, rs_off, cells_in, cells_out):
            cwt = cw.t(i)
            G(cells_in, [sq.c(k)], lambda e: e.tensor_tensor(out=sq.t(k)[:, 0:nh, :], in0=src3, in1=src3, op=ALU.mult))
            V([sq.c(k)], [ss.c(k, rs_off)], lambda e: e.tensor_reduce(
                out=rst[:, rs_off:rs_off + nh], in_=sq.t(k)[:, 0:nh, :], axis=AX.X, op=ALU.add))
            rsqrt(rst[:, rs_off:rs_off + nh], [ss.c(k, rs_off)], 1.0 / 128.0, RMS_EPS)
            Cb = cwt[:, Ci, :].unsqueeze(1).to_broadcast([128, nh, 128])
            V(cells_in + [cw.c(i)], [t1.c(k)], lambda e: e.tensor_tensor(out=t1.t(k)[:, 0:nh, :], in0=src3, in1=Cb, op=ALU.mult))
            s5 = src3.rearrange("p h (a c b) -> p h a c b", a=2, c=2)
            u5 = u.t(k)[:, 0:nh, :].rearrange("p h (a c b) -> p h a c b", a=2, c=2)
            S4 = cwt[:, Si, :].rearrange("p (a c b) -> p a c b", a=2, c=2)

            def fu2(e):
                ins = None
                for a in range(2):
                    for c in range(2):
                        ins = e.tensor_tensor(out=u5[:, :, a, c, :], in0=s5[:, :, a, 1 - c, :],
                                              in1=S4[:, a, c, :].unsqueeze(1).to_broadcast([128, nh, 32]), op=ALU.mult)
                return ins
            G(cells_in + [cw.c(i)], [u.c(k)], fu2)
            V([t1.c(k), u.c(k)], [t1.c(k)], lambda e: e.tensor_tensor(
                out=t1.t(k)[:, 0:nh, :], in0=t1.t(k)[:, 0:nh, :], in1=u.t(k)[:, 0:nh, :], op=ALU.add))
            V([t1.c(k), ss.c(k, rs_off)], cells_out, lambda e: e.tensor_tensor(
                out=dst3, in0=t1.t(k)[:, 0:nh, :],
                in1=rst[:, rs_off:rs_off + nh].unsqueeze(2).to_broadcast([128, nh, 128]), op=ALU.mult))

        for i in range(NT):
            load_xT(st, X, i, xin, xb, xT)
            LD([], [taba.c(i)], lambda e, i=i: e.dma_start(out=taba.t(i)[:], in_=rows(tab_a, cfg.tab_row(i))))
            ta = taba.t(i)
            G([taba.c(i), wq_rep.c(), wq_sw.c(), wk_rep.c(), wk_sw.c()], [cw.c(i)], lambda e, i=i, ta=ta: (
                e.tensor_tensor(out=cw.t(i)[:, 0, :], in0=ta[:, 0:128], in1=wq_rep.t()[:], op=ALU.mult),
                e.tensor_tensor(out=cw.t(i)[:, 1, :], in0=ta[:, 128:256], in1=wq_sw.t()[:], op=ALU.mult),
                e.tensor_tensor(out=cw.t(i)[:, 2, :], in0=ta[:, 0:128], in1=wk_rep.t()[:], op=ALU.mult),
                e.tensor_tensor(out=cw.t(i)[:, 3, :], in0=ta[:, 128:256], in1=wk_sw.t()[:], op=ALU.mult))[-1])
            p = next_pa()
            for cbk in range(2):
                gemm(lambda k: xT.t(i)[:, k, :], 8, wb, cbk * 512, 512,
                     PA.t(p)[:, cbk * 512:(cbk + 1) * 512], [xT.c(i)], [PA.c(p, cbk)])
            k = oc[0]; oc[0] += 1
            A([PA.c(p, 0), PA.c(p, 1)], [ob.c(k)], lambda e, p=p, k=k: e.activation(
                out=ob.t(k)[:], in_=PA.t(p)[:], func=AF.Gelu_apprx_tanh))
            STO([ob.c(k)], [], lambda e, k=k, i=i: e.dma_start(out=rows(SU, i), in_=ob.t(k)[:]))
            p = next_pa()
            for cbk in range(2):
                gemm(lambda k: xT.t(i)[:, k, :], 8, wb, 1024 + cbk * 512, 512,
                     PA.t(p)[:, cbk * 512:(cbk + 1) * 512], [xT.c(i)], [PA.c(p, cbk)])
            A([PA.c(p, 0), PA.c(p, 1)], [svf.c(i)], lambda e, p=p, i=i: e.activation(
                out=svf.t(i)[:], in_=PA.t(p)[:], func=AF.Gelu_apprx_tanh))
            k = oc[0]; oc[0] += 1
            layer_norm(svf.t(i)[:], [svf.c(i)], lb, i, sgw, sgb, ob.t(k)[:], [ob.c(k)])
            STO([ob.c(k)], [], lambda e, k=k, i=i: e.dma_start(out=rows(SV, i), in_=ob.t(k)[:]))
            p = next_pa()
            for cbk in range(2):
                gemm(lambda k: xT.t(i)[:, k, :], 8, wb, 2048 + cbk * 512, 512,
                     PA.t(p)[:, cbk * 512:(cbk + 1) * 512], [xT.c(i)], [PA.c(p, cbk)])
            k = oc[0]; oc[0] += 1
            A([PA.c(p, 0), PA.c(p, 1)], [xs.c(k)], lambda e, p=p, k=k: e.copy(
                out=xs.t(k)[:].rearrange("p a b -> p (a b)"), in_=PA.t(p)[:]))
            axial(i, k, 8, 0, 1, xs.t(k)[:], ob.t(k)[:].rearrange("p (a b) -> p a b", a=8), ss.t(k), 0,
                  [xs.c(k)], [ob.c(k)])
            STO([ob.c(k)], [], lambda e, k=k, i=i: e.dma_start(out=rows(AQ, i), in_=ob.t(k)[:]))
            p = next_pa()
            gemm(lambda k: xT.t(i)[:, k, :], 8, wb, 3072, 512, PA.t(p)[:, 0:512], [xT.c(i)], [PA.c(p, 0)])
            k2 = oc[0]; oc[0] += 1
            A([PA.c(p, 0)], [xs.c(k2)], lambda e, p=p, k2=k2: e.copy(
                out=xs.t(k2)[:].rearrange("p a b -> p (a b)")[:, 0:512], in_=PA.t(p)[:, 0:512]))
            axial(i, k2, 2, 2, 3, xs.t(k2)[:, 0:2, :], okv.t(i)[:, 0:256].rearrange("p (a b) -> p a b", a=2), ss.t(k2), 8,
                  [xs.c(k2)], [okv.c(i, 0)])
            V([xs.c(k2)], [okv.c(i, 1)], lambda e, k2=k2, i=i: e.tensor_copy(
                out=okv.t(i)[:, 256:512], in_=xs.t(k2)[:, 2:4, :].rearrange("p a b -> p (a b)")))
            STO([okv.c(i, 0), okv.c(i, 1)], [], lambda e, i=i: e.dma_start(out=rows(AKV, i), in_=okv.t(i)[:]))
        st.close()
        if stop_after is not None and stop_after == (lt + "b"):
            tr.barrier()
            return nc, tr

        st = Stage(nc, tr, lt + "g")
        wg = load_w(st, "w", w_in[l], 8, 3 * D, 4096 + 3584)
        bgr = st.sb("bg", [1, 3 * D], BF16)
        tr.dma("pool", [], [bgr.c()], lambda e: e.dma_start(out=bgr.t()[:], in_=b_gate[l:l + 1, :]))
        xin = st.sb("xin", [128, D], F32, 2)
        xb = st.sb("xb", [128, D], BF16, 2)
        xT = st.sb("xT", [128, 8, 128], BF16, 2)
        og = st.sb("og", [128, 3 * D], BF16, 2)
        for i in range(NT):
            load_xT(st, X, i, xin, xb, xT)
            for g3 in range(3):
                p = next_pa()
                for cbk in range(2):
                    c0 = g3 * 1024 + cbk * 512
                    gemm(lambda k: xT.t(i)[:, k, :], 8, wg, c0, 512,
                         PA.t(p)[:, cbk * 512:(cbk + 1) * 512], [xT.c(i), bgr.c(), ("ones",)], [PA.c(p, cbk)],
                         extra=(ones[0:1, :], bgr.t()[0:1, c0:c0 + 512]))
                A([PA.c(p, 0), PA.c(p, 1)], [og.c(i, g3)], lambda e, p=p, i=i, g3=g3: e.activation(
                    out=og.t(i)[:, g3 * 1024:(g3 + 1) * 1024], in_=PA.t(p)[:], func=AF.Sigmoid))
            STO([og.c(i, 0), og.c(i, 1), og.c(i, 2)], [], lambda e, i=i: e.dma_start(out=rows(GT, i), in_=og.t(i)[:]))
        st.close()
        if stop_after is not None and stop_after == (lt + "g"):
            tr.barrier()
            return nc, tr

        st = Stage(nc, tr, lt + "s")
        rk_t = st.sb("rk", [128, 8, 128], BF16, 2)
        rv_t = st.sb("rv", [128, 8, 128], BF16, 2)
        kz = st.sb("kz", [128, 8, 128], BF16, 2)
        S = st.sb("S", [128, 8, 128], F32, 1)
        Sb16 = st.sb("Sb", [128, 8, 128], BF16, 2)
        eall = st.sb("eall", [128, 8, 128], F32, 2)
        it = [0]

        def scan(tiles, direction, store, init_from=None):
            order = tiles if direction == 0 else tiles[::-1]
            zoff = 8 * direction
            Sd = SF if direction == 0 else SB
            S3 = S.t()[:]
            if init_from is None:
                V([], [S.c()], lambda e: e.memset(S3.rearrange("p a b -> p (a b)"), 0.0))
            for i in order:
                j = it[0]; it[0] += 1
                LD([], [rk_t.c(j)], lambda e: e.dma_start(out=rk_t.t(j)[:].rearrange("p a b -> p (a b)"), in_=rows(RK, i)))
                LD([], [rv_t.c(j)], lambda e: e.dma_start(out=rv_t.t(j)[:].rearrange("p a b -> p (a b)"), in_=rows(RV, i)))
                if store:
                    A([S.c()], [Sb16.c(j)], lambda e: e.copy(out=Sb16.t(j)[:], in_=S3))
                    STO([Sb16.c(j)], [], lambda e: e.dma_start(out=rows(Sd, i), in_=Sb16.t(j)[:].rearrange("p a b -> p (a b)")))
                V([rk_t.c(j), ZT.c()], [kz.c(j)], lambda e: e.tensor_tensor(
                    out=kz.t(j)[:], in0=rk_t.t(j)[:],
                    in1=ZT.t()[:, zoff:zoff + 8].unsqueeze(2).to_broadcast([128, 8, 128]), op=ALU.mult))
                p = next_pa()

                def f(e):
                    ins = None
                    for h in range(8):
                        ins = e.matmul(PA.t(p)[:, h * 128:(h + 1) * 128], lhsT=kz.t(j)[:, h, :], rhs=rv_t.t(j)[:, h, :],
                                       start=True, stop=True)
                    return ins
                P([kz.c(j), rv_t.c(j)], [PA.c(p, 0), PA.c(p, 1)], f)
                V([S.c(), CDt.c()], [S.c()], lambda e: e.tensor_tensor(
                    out=S3, in0=S3, in1=CDt.t()[:, zoff:zoff + 8].unsqueeze(2).to_broadcast([128, 8, 128]), op=ALU.mult))
                V([S.c(), PA.c(p, 0), PA.c(p, 1)], [S.c()], lambda e: e.tensor_tensor(
                    out=S3, in0=S3, in1=PA.t(p)[:].rearrange("p (a b) -> p a b", a=8), op=ALU.add))

        for s in range(cfg.NP):
            scan(cfg.seg_tiles(s), 0, True)
            scan(cfg.seg_tiles(s), 1, True)
        stl = cfg.seg_tiles(cfg.NP)
        import os as _os
        DBG = _os.environ.get("DBGSKIP", "")
        for d in range(2):
            if "pre" in DBG:
                break
            scan(stl, d, False)
            tr.dma("sp", [S.c()], [("ccin",)], lambda e, d=d: e.dma_start(
                out=Eloc_v[d * 128:(d + 1) * 128, :], in_=S.t()[:].rearrange("p a b -> p (a b)")))
        if "cc" not in DBG:
            allgather()
        for d in range(2):
            if "post" in DBG:
                break
            S3 = S.t()[:]
            V([], [S.c()], lambda e: e.memset(S3.rearrange("p a b -> p (a b)"), 0.0))
            CO = CF if d == 0 else CB
            for c2 in range(cfg.NC):
                j = it[0]; it[0] += 1
                LD([("ccout",)], [eall.c(j)], lambda e, c2=c2, j=j, d=d: e.dma_start(
                    out=eall.t(j)[:].rearrange("p a b -> p (a b)"),
                    in_=Eall_v(c2)[d * 128:(d + 1) * 128, :]))
                V([eall.c(j), CO.c()], [eall.c(j)], lambda e, c2=c2, j=j, CO=CO: e.tensor_tensor(
                    out=eall.t(j)[:], in0=eall.t(j)[:],
                    in1=CO.t()[:, c2, :].unsqueeze(2).to_broadcast([128, 8, 128]), op=ALU.mult))
                V([eall.c(j), S.c()], [S.c()], lambda e, j=j: e.tensor_tensor(out=S3, in0=S3, in1=eall.t(j)[:], op=ALU.add))
            scan(stl, d, True, init_from=True)
        st.close()
        if stop_after is not None and stop_after == (lt + "s"):
            tr.barrier()
            return nc, tr

        s0 = cfg.NP * cfg.TP * 128
        tr.dma("sp", [], [("ccin",)], lambda e: e.dma_start(out=CCin_bf[0:cfg.SSEG, :], in_=AKV[s0:s0 + cfg.SSEG, :]))
        if "kvx" not in DBG:
            allgather()
        tr.barrier()

        st = Stage(nc, tr, lt + "r")
        gnw = load_bcast(st, "gnw", ret_gn_w[l:l + 1, :], D)
        wsT = st.sb("wsT", [128, 4, 128], BF16)
        wsl = st.sb("wsl", [128, 4, 128], BF16)
        tr.dma("pool", [], [wsl.c()], lambda e: e.dma_start(out=wsl.t()[:], in_=sg_ws[l].rearrange("g c m -> c g m")))
        pq = next_pt()
        transposes(lambda j: wsl.t()[:, j, :], 4, pq, [wsl.c()])
        V([PT.c(pq)], [wsT.c()], lambda e: e.tensor_copy(out=wsT.t()[:].rearrange("p a b -> p (a b)"), in_=PT.t(pq)[:, 0:512]))
        sgbt = st.sb("sgbt", [128, 4], F32)
        sgb4 = st.sb("sgb4", [4, 128], F32)
        tr.dma("sp", [], [sgb4.c()], lambda e: e.dma_start(out=sgb4.t()[:], in_=sg_b[l]))
        P([sgb4.c(), ("ctb",)], [PB.c(0)], lambda e: e.matmul(
            PB.t(0)[:, 0:4], lhsT=sgb4.t()[0:4, :], rhs=ctb[0:4, C_I4:C_I4 + 4], start=True, stop=True))
        V([PB.c(0)], [sgbt.c()], lambda e: e.tensor_copy(out=sgbt.t()[:], in_=PB.t(0)[:, 0:4]))
        q_t = st.sb("q", [128, D], BF16, 2)
        k_t = st.sb("k", [128, D], BF16, 2)
        v_t = st.sb("v", [128, 8, 128], BF16, 2)
        g_t = st.sb("g", [128, D], BF16, 2)
        su_t = st.sb("su", [128, D], BF16, 2)
        sv_t = st.sb("sv", [128, D], BF16, 2)
        sf_t = st.sb("sf", [128, 8, 128], BF16, 2)
        sb_t = st.sb("sbb", [128, 8, 128], BF16, 2)
        qT = st.sb("qT", [128, 8, 128], BF16, 2)
        kT = st.sb("kT", [128, 8, 128], BF16, 2)
        qTf = st.sb("qTf", [128, 8, 128], BF16, 2)
        qTb = st.sb("qTb", [128, 8, 128], BF16, 2)
        Pm = st.sb("Pm", [128, 8, 128], BF16, 2)
        ro = st.sb("ro", [128, 8, 128], F32, 2)
        rsq = st.sb("rsq", [128, 8, 128], F32, 2)
        gst = st.sb("gst", [128, 32], F32, 2)
        yr = st.sb("yr", [128, D], BF16, 2)
        ys = st.sb("ys", [128, D], BF16, 2)
        for i in range(NT):
            for (buf, src) in ((q_t, RQ), (k_t, RK), (g_t, RG), (su_t, SU), (sv_t, SV)):
                LD([], [buf.c(i)], lambda e, buf=buf, src=src, i=i: e.dma_start(out=buf.t(i)[:], in_=rows(src, i)))
            LD([], [v_t.c(i)], lambda e, i=i: e.dma_start(out=v_t.t(i)[:].rearrange("p a b -> p (a b)"), in_=rows(RV, i)))
            LD([], [sf_t.c(i)], lambda e, i=i: e.dma_start(out=sf_t.t(i)[:].rearrange("p a b -> p (a b)"), in_=rows(SF, i)))
            LD([], [sb_t.c(i)], lambda e, i=i: e.dma_start(out=sb_t.t(i)[:].rearrange("p a b -> p (a b)"), in_=rows(SB, i)))
            pq = next_pt()
            transposes(lambda j: q_t.t(i)[:, j * 128:(j + 1) * 128], 8, pq, [q_t.c(i)])
            ptq = PT.t(pq)[:].rearrange("p (a b) -> p a b", a=8)
            A([PT.c(pq)], [qT.c(i)], lambda e, ptq=ptq, i=i: e.copy(out=qT.t(i)[:], in_=ptq))
            V([PT.c(pq), XIF.c()], [qTf.c(i)], lambda e, ptq=ptq, i=i: e.tensor_tensor(
                out=qTf.t(i)[:], in0=ptq, in1=XIF.t()[:], op=ALU.mult))
            V([PT.c(pq), XIB.c()], [qTb.c(i)], lambda e, ptq=ptq, i=i: e.tensor_tensor(
                out=qTb.t(i)[:], in0=ptq, in1=XIB.t()[:], op=ALU.mult))
            to_T(k_t, i, 8, kT, i, copy_eng="act")
            p = next_pa()

            def fa(e, p=p, i=i):
                ins = None
                for h in range(8):
                    ins = e.matmul(PA.t(p)[:, h * 128:(h + 1) * 128], lhsT=kT.t(i)[:, h, :], rhs=qT.t(i)[:, h, :],
                                   start=True, stop=True)
                return ins
            P([kT.c(i), qT.c(i)], [PA.c(p, 0), PA.c(p, 1)], fa)
            V([PA.c(p, 0), PA.c(p, 1), DTb.c()], [Pm.c(i)], lambda e, p=p, i=i: e.tensor_tensor(
                out=Pm.t(i)[:], in0=PA.t(p)[:].rearrange("p (a b) -> p a b", a=8), in1=DTb.t()[:], op=ALU.mult))
            p2 = next_pa()

            def fo(e, p2=p2, i=i):
                ins = None
                for h in range(8):
                    o = PA.t(p2)[:, h * 128:(h + 1) * 128]
                    e.matmul(o, lhsT=Pm.t(i)[:, h, :], rhs=v_t.t(i)[:, h, :], start=True, stop=False)
                    e.matmul(o, lhsT=qTf.t(i)[:, h, :], rhs=sf_t.t(i)[:, h, :], start=False, stop=False)
                    ins = e.matmul(o, lhsT=qTb.t(i)[:, h, :], rhs=sb_t.t(i)[:, h, :], start=False, stop=True)
                return ins
            P([Pm.c(i), v_t.c(i), qTf.c(i), qTb.c(i), sf_t.c(i), sb_t.c(i)], [PA.c(p2, 0), PA.c(p2, 1)], fo)
            A([PA.c(p2, 0), PA.c(p2, 1)], [ro.c(i)], lambda e, p2=p2, i=i: e.copy(
                out=ro.t(i)[:].rearrange("p a b -> p (a b)"), in_=PA.t(p2)[:]))
            gs_ = gst.t(i)
            V([ro.c(i)], [gst.c(i, 0)], lambda e, i=i, gs_=gs_: e.tensor_reduce(
                out=gs_[:, 0:8], in_=ro.t(i)[:], axis=AX.X, op=ALU.add))
            G([ro.c(i)], [rsq.c(i)], lambda e, i=i: e.tensor_tensor(out=rsq.t(i)[:], in0=ro.t(i)[:], in1=ro.t(i)[:], op=ALU.mult))
            V([rsq.c(i)], [gst.c(i, 1)], lambda e, i=i, gs_=gs_: e.tensor_reduce(
                out=gs_[:, 8:16], in_=rsq.t(i)[:], axis=AX.X, op=ALU.add))
            V([gst.c(i, 0)], [gst.c(i, 0)], lambda e, gs_=gs_: e.tensor_scalar(
                out=gs_[:, 0:8], in0=gs_[:, 0:8], scalar1=1.0 / 128.0, scalar2=None, op0=ALU.mult))
            V([gst.c(i, 0)], [gst.c(i, 2)], lambda e, gs_=gs_: e.tensor_tensor(
                out=gs_[:, 16:24], in0=gs_[:, 0:8], in1=gs_[:, 0:8], op=ALU.mult))
            V([gst.c(i, 1), gst.c(i, 2)], [gst.c(i, 1)], lambda e, gs_=gs_: e.scalar_tensor_tensor(
                out=gs_[:, 8:16], in0=gs_[:, 8:16], scalar=1.0 / 128.0, in1=gs_[:, 16:24], op0=ALU.mult, op1=ALU.subtract))
            rsqrt(gs_[:, 8:16], [gst.c(i, 1)], 1.0, LN_EPS)
            V([ro.c(i), gst.c(i, 0)], [ro.c(i)], lambda e, i=i, gs_=gs_: e.tensor_tensor(
                out=ro.t(i)[:], in0=ro.t(i)[:], in1=gs_[:, 0:8].unsqueeze(2).to_broadcast([128, 8, 128]), op=ALU.subtract))
            V([ro.c(i), gst.c(i, 1)], [ro.c(i)], lambda e, i=i, gs_=gs_: e.tensor_tensor(
                out=ro.t(i)[:], in0=ro.t(i)[:], in1=gs_[:, 8:16].unsqueeze(2).to_broadcast([128, 8, 128]), op=ALU.mult))
            G([ro.c(i), gnw.c()], [ro.c(i)], lambda e, i=i: e.tensor_tensor(
                out=ro.t(i)[:].rearrange("p a b -> p (a b)"), in0=ro.t(i)[:].rearrange("p a b -> p (a b)"),
                in1=gnw.t()[:], op=ALU.mult))
            G([ro.c(i), g_t.c(i)], [yr.c(i)], lambda e, i=i: e.tensor_tensor(
                out=yr.t(i)[:], in0=ro.t(i)[:].rearrange("p a b -> p (a b)"), in1=g_t.t(i)[:], op=ALU.mult))
            STO([yr.c(i)], [], lambda e, i=i: e.dma_start(out=rows(YR, i), in_=yr.t(i)[:]))
            p3 = next_pa()

            def fs(e, p3=p3, i=i):
                ins = None
                for g in range(4):
                    ins = e.matmul(PA.t(p3)[:, g * 256:(g + 1) * 256], lhsT=wsT.t()[:, g, :],
                                   rhs=sv_t.t(i)[:, g * 256:(g + 1) * 256], start=True, stop=True)
                return ins
            P([wsT.c(), sv_t.c(i)], [PA.c(p3, 0), PA.c(p3, 1)], fs)

            def fy(e, p3=p3, i=i):
                ins = None
                for g in range(4):
                    ins = e.scalar_tensor_tensor(out=ys.t(i)[:, g * 256:(g + 1) * 256], in0=PA.t(p3)[:, g * 256:(g + 1) * 256],
                                                 scalar=sgbt.t()[:, g:g + 1], in1=su_t.t(i)[:, g * 256:(g + 1) * 256],
                                                 op0=ALU.add, op1=ALU.mult)
                return ins
            V([PA.c(p3, 0), PA.c(p3, 1), sgbt.c(), su_t.c(i)], [ys.c(i)], fy)
            STO([ys.c(i)], [], lambda e, i=i: e.dma_start(out=rows(YS, i), in_=ys.t(i)[:]))
        st.close()
        if stop_after is not None and stop_after == (lt + "r"):
            tr.barrier()
            return nc, tr

        for s in range(NSEG):
            tiles = cfg.seg_tiles(s)
            is_samp = (s == cfg.NP)
            nkb = (cfg.DSEQ // 128) if is_samp else cfg.TP
            st = Stage(nc, tr, lt + "t%d" % s)
            KT = st.sb("KT", [128, 2, nkb * 128], BF16)
            Vt = st.sb("Vt", [128, nkb, 256], BF16)
            kld = st.sb("kld", [128, 256], BF16, 3)
            if is_samp:
                ksrc = CCout_bf
                koff = 0
            else:
                ksrc = AKV
                koff = tiles[0] * 128
            for kb in range(nkb):
                r0 = koff + kb * 128
                if is_samp:
                    r0 = ((kb * 128) // cfg.SSEG) * CR + (kb * 128) % cfg.SSEG
                LD([], [Vt.c(0, kb)], lambda e, kb=kb, r0=r0: e.dma_start(out=Vt.t()[:, kb, :], in_=ksrc[r0:r0 + 128, 256:512]))
                LD([], [kld.c(kb)], lambda e, kb=kb, r0=r0: e.dma_start(out=kld.t(kb)[:], in_=ksrc[r0:r0 + 128, 0:256]))
                pq = next_pt()
                transposes(lambda j: kld.t(kb)[:, j * 128:(j + 1) * 128], 2, pq, [kld.c(kb)])
                V([PT.c(pq)], [KT.c(0, kb)], lambda e, kb=kb, pq=pq: e.tensor_copy(
                    out=KT.t()[:, :, kb * 128:(kb + 1) * 128], in_=PT.t(pq)[:, 0:256].rearrange("p (a b) -> p a b", a=2)))
            kcells = [KT.c(0, kb) for kb in range(nkb)]
            vcells = [Vt.c(0, kb) for kb in range(nkb)]
            aq_t = st.sb("aq", [128, D], BF16, 3)
            qTg = st.sb("qTg", [128, 8, 512], BF16, 2)
            PTs = st.sb("PTs", [128, 512], BF16, 3)
            rden = st.sb("rden", [128, 512], F32, 2)
            yat = st.sb("yat", [128, 4, 8, 128], BF16, 2)
            ngrp = (len(tiles) + 3) // 4
            ec = [0]
            for gi in range(ngrp):
                gt = tiles[gi * 4:(gi + 1) * 4]
                nq = len(gt) * 128
                for ti, i in enumerate(gt):
                    LD([], [aq_t.c(i)], lambda e, i=i: e.dma_start(out=aq_t.t(i)[:], in_=rows(AQ, i)))
                    pq = next_pt()
                    transposes(lambda j: aq_t.t(i)[:, j * 128:(j + 1) * 128], 8, pq, [aq_t.c(i)])
                    V([PT.c(pq)], [qTg.c(gi, ti)], lambda e, pq=pq, ti=ti, gi=gi: e.tensor_copy(
                        out=qTg.t(gi)[:, :, ti * 128:(ti + 1) * 128], in_=PT.t(pq)[:].rearrange("p (a b) -> p a b", a=8)))
                qcells = [qTg.c(gi, ti) for ti in range(len(gt))]
                for h in range(8):
                    g = h // 4
                    for kb in range(nkb):
                        j = ec[0]; ec[0] += 1
                        pj = (j % 4) // 2
                        half = j % 2
                        sps = PA.t(pj)[:, half * 512:half * 512 + nq]
                        P([KT.c(0, kb)] + qcells, [PA.c(pj, half)], lambda e, kb=kb, g=g, h=h, sps=sps, gi=gi, nq=nq: e.matmul(
                            sps, lhsT=KT.t()[:, g, kb * 128:(kb + 1) * 128], rhs=qTg.t(gi)[:, h, 0:nq], start=True, stop=True))
                        A([PA.c(pj, half), negM_c], [PTs.c(j)], lambda e, sps=sps, j=j, nq=nq: e.activation(
                            out=PTs.t(j)[:, 0:nq], in_=sps, func=AF.Exp, bias=negM, scale=128.0 ** -0.5))
                        P([PTs.c(j), Vt.c(0, kb)], [PB.c(0)], lambda e, kb=kb, g=g, j=j, nq=nq: e.matmul(
                            PB.t(0)[:, 0:nq], lhsT=Vt.t()[:, kb, g * 128:(g + 1) * 128], rhs=PTs.t(j)[:, 0:nq],
                            start=(kb == 0), stop=(kb == nkb - 1)))
                        P([PTs.c(j), ("ones",)], [PB.c(1)], lambda e, kb=kb, j=j, nq=nq: e.matmul(
                            PB.t(1)[:, 0:nq], lhsT=ones[:], rhs=PTs.t(j)[:, 0:nq], start=(kb == 0), stop=(kb == nkb - 1)))
                    hh = gi * 8 + h
                    V([PB.c(1)], [rden.c(hh)], lambda e, hh=hh, nq=nq: e.reciprocal(out=rden.t(hh)[:, 0:nq], in_=PB.t(1)[:, 0:nq]))
                    V([PB.c(0), rden.c(hh)], [yat.c(gi, h)], lambda e, hh=hh, h=h, gi=gi, gt=gt, nq=nq: e.tensor_tensor(
                        out=yat.t(gi)[:, 0:len(gt), h, :], in0=PB.t(0)[:, 0:nq].rearrange("p (t q) -> p t q", q=128),
                        in1=rden.t(hh)[:, 0:nq].rearrange("p (t q) -> p t q", q=128), op=ALU.mult))
                for ti, i in enumerate(gt):
                    STO([yat.c(gi, h) for h in range(8)], [], lambda e, ti=ti, i=i, gi=gi: e.dma_start(
                        out=rows(YAT, i), in_=yat.t(gi)[:, ti, :, :].rearrange("p a b -> p (a b)")))
            st.close()
            if stop_after is not None and stop_after == (lt + "t%d" % s):
                tr.barrier()
                return nc, tr

        st = Stage(nc, tr, lt + "c")
        wro = load_w(st, "wro", ret_wo[l], 8, D)
        wso = load_w(st, "wso", sg_wo[l], 8, D)
        wao = load_w(st, "wao", att_wo[l], 8, D)
        wout = load_w(st, "wout", w_out[l], 8, D)
        lw0 = load_bcast(st, "lw0", ln_w[l, 0:1, :], D)
        lb0 = load_bcast(st, "lb0", ln_b[l, 0:1, :], D)
        yr_t = st.sb("yr", [128, D], BF16, 2)
        ys_t = st.sb("ys", [128, D], BF16, 2)
        yaT = st.sb("yaT", [128, 8, 128], BF16, 2)
        gt_t = st.sb("gt", [128, 3 * D], BF16, 2)
        xr = st.sb("xr", [128, D], F32, 2)
        yrT = st.sb("yrT", [128, 8, 128], BF16, 2)
        ysT = st.sb("ysT", [128, 8, 128], BF16, 2)
        mg = st.sb("mg", [128, D], F32, 2)
        tmpm = st.sb("tmpm", [128, D], F32, 2)
        mgb = st.sb("mgb", [128, D], BF16, 2)
        mT = st.sb("mT", [128, 8, 128], BF16, 2)
        x1 = st.sb("x1", [128, D], F32, 2)
        lbf = ln_bufs(st, "ln")
        for i in range(NT):
            LD([], [yr_t.c(i)], lambda e, i=i: e.dma_start(out=yr_t.t(i)[:], in_=rows(YR, i)))
            LD([], [ys_t.c(i)], lambda e, i=i: e.dma_start(out=ys_t.t(i)[:], in_=rows(YS, i)))
            LD([], [yaT.c(i)], lambda e, i=i: e.dma_start(out=yaT.t(i)[:].rearrange("p a b -> p (a b)"), in_=rows(YAT, i)))
            LD([], [gt_t.c(i)], lambda e, i=i: e.dma_start(out=gt_t.t(i)[:], in_=rows(GT, i)))
            LD([], [xr.c(i)], lambda e, i=i: e.dma_start(out=xr.t(i)[:], in_=rows(X, i)))
            to_T(yr_t, i, 8, yrT, i, copy_eng="act")
            to_T(ys_t, i, 8, ysT, i, copy_eng="dve")
            for bi, (srcT, wbuf) in enumerate(((yrT, wro), (ysT, wso), (yaT, wao))):
                p = next_pa()
                for cbk in range(2):
                    gemm(lambda k: srcT.t(i)[:, k, :], 8, wbuf, cbk * 512, 512,
                         PA.t(p)[:, cbk * 512:(cbk + 1) * 512], [srcT.c(i)], [PA.c(p, cbk)])
                gsl = gt_t.t(i)[:, bi * 1024:(bi + 1) * 1024]
                if bi == 0:
                    V([PA.c(p, 0), PA.c(p, 1), gt_t.c(i)], [mg.c(i)], lambda e, p=p, gsl=gsl, i=i: e.tensor_tensor(
                        out=mg.t(i)[:], in0=PA.t(p)[:], in1=gsl, op=ALU.mult))
                else:
                    V([PA.c(p, 0), PA.c(p, 1), gt_t.c(i)], [tmpm.c(i)], lambda e, p=p, gsl=gsl, i=i: e.tensor_tensor(
                        out=tmpm.t(i)[:], in0=PA.t(p)[:], in1=gsl, op=ALU.mult))
                    if bi == 1:
                        G([mg.c(i), tmpm.c(i)], [mg.c(i)], lambda e, i=i: e.tensor_tensor(
                            out=mg.t(i)[:], in0=mg.t(i)[:], in1=tmpm.t(i)[:], op=ALU.add))
                    else:
                        G([mg.c(i), tmpm.c(i)], [mgb.c(i)], lambda e, i=i: e.tensor_tensor(
                            out=mgb.t(i)[:], in0=mg.t(i)[:], in1=tmpm.t(i)[:], op=ALU.add))
            to_T(mgb, i, 8, mT, i, copy_eng="act")
            p = next_pa()
            for cbk in range(2):
                gemm(lambda k: mT.t(i)[:, k, :], 8, wout, cbk * 512, 512,
                     PA.t(p)[:, cbk * 512:(cbk + 1) * 512], [mT.c(i)], [PA.c(p, cbk)])
            V([PA.c(p, 0), PA.c(p, 1), xr.c(i)], [x1.c(i)], lambda e, p=p, i=i: e.scalar_tensor_tensor(
                out=x1.t(i)[:], in0=xr.t(i)[:], scalar=ALPHA, in1=PA.t(p)[:], op0=ALU.mult, op1=ALU.add))
            layer_norm(x1.t(i)[:], [x1.c(i)], lbf, i, lw0, lb0, x1.t(i)[:], [x1.c(i)])
            STO([x1.c(i)], [], lambda e, i=i: e.dma_start(out=rows(X1, i), in_=x1.t(i)[:]))
        st.close()
        if stop_after is not None and stop_after == (lt + "c"):
            tr.barrier()
            return nc, tr

        st = Stage(nc, tr, lt + "x")
        wxq = load_w(st, "wxq", xa_wq[l], 8, D)
        wxo = load_w(st, "wxo", xa_wo[l], 8, D)
        lw1 = load_bcast(st, "lw1", ln_w[l, 1:2, :], D)
        lb1 = load_bcast(st, "lb1", ln_b[l, 1:2, :], D)
        x1 = st.sb("x1", [128, D], F32, 2)
        x1b = st.sb("x1b", [128, D], BF16, 2)
        x1T = st.sb("x1T", [128, 8, 128], BF16, 2)
        qxT = st.sb("qxT", [128, 8, 128], BF16, 2)
        mk_t = st.sb("mk", [128, 2, 8, 128], BF16, 2)
        mv_t = st.sb("mvv", [128, 2, D], BF16, 2)
        pex = st.sb("pex", [128, 8, 128], BF16, 2)
        rdx = st.sb("rdx", [128, 4, 128], F32, 2)
        oT = st.sb("oT", [128, 8, 128], BF16, 2)
        x2 = st.sb("x2", [128, D], F32, 2)
        lbf = ln_bufs(st, "ln")
        cur_seg = [-1]
        segc = [0]
        for i in range(NT):
            s = cfg.tile_seg(i)
            if s != cur_seg[0]:
                cur_seg[0] = s
                segc[0] += 1
                sc = segc[0]
                for kb in range(2):
                    LD([], [mk_t.c(sc, kb)], lambda e, kb=kb, sc=sc, s=s: e.dma_start(
                        out=mk_t.t(sc)[:, kb, :, :].rearrange("p a b -> p (a b)"), in_=rows(MK, s * 2 + kb)))
                    LD([], [mv_t.c(sc, kb)], lambda e, kb=kb, sc=sc, s=s: e.dma_start(out=mv_t.t(sc)[:, kb, :], in_=rows(MV, s * 2 + kb)))
            sc = segc[0]
            LD([], [x1.c(i)], lambda e, i=i: e.dma_start(out=x1.t(i)[:], in_=rows(X1, i)))
            A([x1.c(i)], [x1b.c(i)], lambda e, i=i: e.copy(out=x1b.t(i)[:], in_=x1.t(i)[:]))
            to_T(x1b, i, 8, x1T, i, copy_eng="dve")
            p = next_pa()

            def fq(e, p=p, i=i):
                ins = None
                for j in range(8):
                    for k in range(8):
                        ins = e.matmul(PA.t(p)[:, j * 128:(j + 1) * 128], lhsT=wxq.t()[:, k, j * 128:(j + 1) * 128],
                                       rhs=x1T.t(i)[:, k, :], start=(k == 0), stop=(k == 7))
                return ins
            P([wxq.c(), x1T.c(i)], [PA.c(p, 0), PA.c(p, 1)], fq)
            A([PA.c(p, 0), PA.c(p, 1)], [qxT.c(i)], lambda e, p=p, i=i: e.copy(
                out=qxT.t(i)[:].rearrange("p a b -> p (a b)"), in_=PA.t(p)[:]))
            p = next_pa()

            def fsx(e, p=p, i=i, sc=sc):
                ins = None
                for h in range(4):
                    for kb in range(2):
                        o = PA.t(p)[:, (h * 2 + kb) * 128:(h * 2 + kb + 1) * 128]
                        for c2 in range(2):
                            ins = e.matmul(o, lhsT=mk_t.t(sc)[:, kb, h * 2 + c2, :], rhs=qxT.t(i)[:, h * 2 + c2, :],
                                           start=(c2 == 0), stop=(c2 == 1))
                return ins
            P([mk_t.c(sc, 0), mk_t.c(sc, 1), qxT.c(i)], [PA.c(p, 0), PA.c(p, 1)], fsx)
            A([PA.c(p, 0), PA.c(p, 1)], [pex.c(i)], lambda e, p=p, i=i: e.activation(
                out=pex.t(i)[:].rearrange("p a b -> p (a b)"), in_=PA.t(p)[:], func=AF.Exp, scale=256.0 ** -0.5))
            pbi = i % 2

            def fden(e, i=i, pbi=pbi):
                ins = None
                for h in range(4):
                    for kb in range(2):
                        ins = e.matmul(PB.t(pbi)[:, h * 128:(h + 1) * 128], lhsT=ones[:], rhs=pex.t(i)[:, h * 2 + kb, :],
                                       start=(kb == 0), stop=(kb == 1))
                return ins
            P([pex.c(i), ("ones",)], [PB.c(pbi)], fden)
            p = next_pa()

            def fo2(e, p=p, i=i, sc=sc):
                ins = None
                for h in range(4):
                    for c2 in range(2):
                        o = PA.t(p)[:, (h * 2 + c2) * 128:(h * 2 + c2 + 1) * 128]
                        for kb in range(2):
                            ins = e.matmul(o, lhsT=mv_t.t(sc)[:, kb, (h * 2 + c2) * 128:(h * 2 + c2 + 1) * 128],
                                           rhs=pex.t(i)[:, h * 2 + kb, :], start=(kb == 0), stop=(kb == 1))
                return ins
            P([pex.c(i), mv_t.c(sc, 0), mv_t.c(sc, 1)], [PA.c(p, 0), PA.c(p, 1)], fo2)
            V([PB.c(pbi)], [rdx.c(i)], lambda e, i=i, pbi=pbi: e.reciprocal(
                out=rdx.t(i)[:].rearrange("p a b -> p (a b)"), in_=PB.t(pbi)[:]))
            V([PA.c(p, 0), PA.c(p, 1), rdx.c(i)], [oT.c(i)], lambda e, p=p, i=i: e.tensor_tensor(
                out=oT.t(i)[:].rearrange("p (h c) q -> p h c q", c=2),
                in0=PA.t(p)[:].rearrange("p (h c q) -> p h c q", h=4, c=2),
                in1=rdx.t(i)[:].unsqueeze(2).to_broadcast([128, 4, 2, 128]), op=ALU.mult))
            p = next_pa()
            for cbk in range(2):
                gemm(lambda k: oT.t(i)[:, k, :], 8, wxo, cbk * 512, 512,
                     PA.t(p)[:, cbk * 512:(cbk + 1) * 512], [oT.c(i)], [PA.c(p, cbk)])
            V([PA.c(p, 0), PA.c(p, 1), x1.c(i)], [x2.c(i)], lambda e, p=p, i=i: e.scalar_tensor_tensor(
                out=x2.t(i)[:], in0=x1.t(i)[:], scalar=ALPHA, in1=PA.t(p)[:], op0=ALU.mult, op1=ALU.add))
            layer_norm(x2.t(i)[:], [x2.c(i)], lbf, i, lw1, lb1, x2.t(i)[:], [x2.c(i)])
            STO([x2.c(i)], [], lambda e, i=i: e.dma_start(out=rows(X2, i), in_=x2.t(i)[:]))
        st.close()
        if stop_after is not None and stop_after == (lt + "x"):
            tr.barrier()
            return nc, tr

        st = Stage(nc, tr, lt + "f")
        wfi = load_w(st, "wfi", ffn_w_in[l], 8, 2 * D_FF)
        wfo = load_w(st, "wfo", ffn_w_out[l], 22, D)
        lw2 = load_bcast(st, "lw2", ln_w[l, 2:3, :], D)
        lb2 = load_bcast(st, "lb2", ln_b[l, 2:3, :], D)
        xin = st.sb("xin", [128, D], F32, 2)
        xb = st.sb("xb", [128, D], BF16, 2)
        xT = st.sb("xT", [128, 8, 128], BF16, 2)
        sa = st.sb("sa", [128, 512], F32, 3)
        act = st.sb("act", [128, D_FF], BF16, 1)
        actT = st.sb("actT", [128, 22, 128], BF16, 1)
        x3 = st.sb("x3", [128, D], F32, 2)
        lbf = ln_bufs(st, "ln")
        Xdst = X if l < L - 1 else y_out
        cc_ = [0]
        for i in range(NT):
            load_xT(st, X2, i, xin, xb, xT)
            blocks = [(c0, 512) for c0 in range(0, 2560, 512)] + [(2560, 256)]
            for (c0, w_) in blocks:
                p = next_pa()
                gemm(lambda k: xT.t(i)[:, k, :], 8, wfi, c0, w_, PA.t(p)[:, 0:w_], [xT.c(i)], [PA.c(p, 0)])
                gemm(lambda k: xT.t(i)[:, k, :], 8, wfi, D_FF + c0, w_, PA.t(p)[:, 512:512 + w_], [xT.c(i)], [PA.c(p, 1)])
                j = cc_[0]; cc_[0] += 1
                A([PA.c(p, 0)], [sa.c(j)], lambda e, p=p, j=j, w_=w_: e.activation(
                    out=sa.t(j)[:, 0:w_], in_=PA.t(p)[:, 0:w_], func=AF.Silu))
                V([sa.c(j), PA.c(p, 1)], [act.c(i, c0)], lambda e, p=p, j=j, w_=w_, c0=c0, i=i: e.tensor_tensor(
                    out=act.t(i)[:, c0:c0 + w_], in0=sa.t(j)[:, 0:w_], in1=PA.t(p)[:, 512:512 + w_], op=ALU.mult))
            acells = [act.c(i, c0) for (c0, _) in blocks]
            for r in range(3):
                nb = min(8, 22 - r * 8)
                pq = next_pt()
                transposes(lambda j, r=r: act.t(i)[:, (r * 8 + j) * 128:(r * 8 + j + 1) * 128], nb, pq, acells)
                eng = "act" if r % 2 == 0 else "dve"
                if eng == "act":
                    A([PT.c(pq)], [actT.c(i, r)], lambda e, pq=pq, r=r, nb=nb, i=i: e.copy(
                        out=actT.t(i)[:, r * 8:r * 8 + nb, :].rearrange("p a b -> p (a b)"), in_=PT.t(pq)[:, 0:nb * 128]))
                else:
                    V([PT.c(pq)], [actT.c(i, r)], lambda e, pq=pq, r=r, nb=nb, i=i: e.tensor_copy(
                        out=actT.t(i)[:, r * 8:r * 8 + nb, :].rearrange("p a b -> p (a b)"), in_=PT.t(pq)[:, 0:nb * 128]))
            p = next_pa()
            for cbk in range(2):
                gemm(lambda k: actT.t(i)[:, k, :], 22, wfo, cbk * 512, 512,
                     PA.t(p)[:, cbk * 512:(cbk + 1) * 512], [actT.c(i, r) for r in range(3)], [PA.c(p, cbk)])
            V([PA.c(p, 0), PA.c(p, 1), xin.c(i)], [x3.c(i)], lambda e, p=p, i=i: e.scalar_tensor_tensor(
                out=x3.t(i)[:], in0=xin.t(i)[:], scalar=ALPHA, in1=PA.t(p)[:], op0=ALU.mult, op1=ALU.add))
            layer_norm(x3.t(i)[:], [x3.c(i)], lbf, i, lw2, lb2, x3.t(i)[:], [x3.c(i)])
            STO([x3.c(i)], [], lambda e, i=i: e.dma_start(out=rows(Xdst, i), in_=x3.t(i)[:]))
        st.close()
        if stop_after is not None and stop_after == (lt + "f"):
            tr.barrier()
            return nc, tr
        ls.close()

    tr.barrier()
    gs.close()
    return nc, tr


def _rope_tables(cfg, core):
    def cs(pos, dim):
        inv = (ROPE_BASE ** (-np.arange(0, dim, 2, dtype=np.float32) / np.float32(dim))).astype(np.float32)
        ang = pos.astype(np.float32)[:, None] * inv[None, :]
        return np.cos(ang).astype(np.float32), np.sin(ang).astype(np.float32)
    pos = np.concatenate([np.arange(cfg.SEQ), core * cfg.SSEG + np.arange(cfg.SSEG)]).astype(np.int64)
    c, s = cs(pos, 128)
    C = np.concatenate([c, c], 1)
    S = np.concatenate([-s, s], 1)
    sc = np.float32(128.0 ** -0.5)
    tab_r = np.concatenate([C, S, C * sc, S * sc], 1).astype(np.float32)
    cr, sr = cs(pos // GRID_W, 64)
    cc, s2 = cs(pos % GRID_W, 64)
    Ca = np.concatenate([cr, cr, cc, cc], 1)
    Sa = np.concatenate([-sr, sr, -s2, s2], 1)
    tab_a = np.concatenate([Ca, Sa], 1).astype(np.float32)
    return tab_r, tab_a


def _ctab():
    m = np.arange(128, dtype=np.float32)[:, None]
    c = np.arange(128, dtype=np.float32)[None, :]
    t = np.zeros((128, 1024), np.float32)
    t[:, 0:128] = np.maximum(c - m, 0)
    t[:, 128:256] = (c >= m)
    t[:, 256:384] = np.maximum(m - c, 0)
    t[:, 384:512] = (m > c)
    t[:, 512:640] = c + 1
    t[:, 640:768] = 128 - c
    t[:, 768] = 127 - m[:, 0]
    t[:, 769] = m[:, 0]
    for j in range(4):
        t[j, 772 + j] = 1.0
    return t


def _cctab(cfg, core):
    t = np.zeros((128, 32), np.float32)
    for c2 in range(cfg.NC):
        if c2 < core:
            t[:, c2] = cfg.SSEG * (core - 1 - c2)
            t[:, 16 + c2] = 1.0
        if c2 > core:
            t[:, 8 + c2] = cfg.SSEG * (c2 - core - 1)
            t[:, 24 + c2] = 1.0
    return t


def make_in_maps(cfg, inp):
    f = lambda a: np.ascontiguousarray(np.asarray(a, dtype=np.float32))
    shared = {
        "in_ln_w": f(inp["in_ln_w"]).reshape(1, D), "in_ln_b": f(inp["in_ln_b"]).reshape(1, D),
        "w_in": f(inp["w_in"]), "b_gate": f(inp["b_gate"]),
        "ret_decay": f(np.concatenate([inp["ret_decay_f"], inp["ret_decay_b"]], axis=1)),
        "ret_gn_w": f(inp["ret_gn_w"]), "ret_wo": f(inp["ret_wo"]),
        "sg_ln_w": f(inp["sg_ln_w"]), "sg_ln_b": f(inp["sg_ln_b"]), "sg_ws": f(inp["sg_ws"]), "sg_b": f(inp["sg_b"]),
        "sg_wo": f(inp["sg_wo"]), "att_qn_w": f(inp["att_qn_w"]), "att_kn_w": f(inp["att_kn_w"]),
        "att_wo": f(inp["att_wo"]), "w_out": f(inp["w_out"]), "ln_w": f(inp["ln_w"]), "ln_b": f(inp["ln_b"]),
        "xa_wq": f(inp["xa_wq"]), "xa_wkv": f(inp["xa_wkv"]), "xa_wo": f(inp["xa_wo"]),
        "ffn_w_in": f(inp["ffn_w_in"]), "ffn_w_out": f(inp["ffn_w_out"]),
        "ctab": _ctab(), "ident": np.eye(128, dtype=np.float32),
    }
    xp, xs = f(inp["x_prompt"]), f(inp["x_sample"])
    mp, ms = f(inp["mem_prompt"]), f(inp["mem_sample"])
    maps = []
    for c in range(cfg.NC):
        m = dict(shared)
        xpc = xp[c * cfg.NP:(c + 1) * cfg.NP].reshape(cfg.NP * cfg.SEQ, D)
        xsc = xs[0, c * cfg.SSEG:(c + 1) * cfg.SSEG]
        m["x"] = np.ascontiguousarray(np.concatenate([xpc, xsc], 0))
        m["mem"] = np.ascontiguousarray(np.concatenate([mp[c * cfg.NP:(c + 1) * cfg.NP].reshape(cfg.NP * N_MEM, D), ms[0]], 0))
        tr_, ta_ = _rope_tables(cfg, c)
        m["tab_r"], m["tab_a"] = tr_, ta_
        m["cctab"] = _cctab(cfg, c)
        maps.append(m)
    return maps


def run(cfg, inp, debug=()):
    nc, tr = build(cfg, debug)
    maps = make_in_maps(cfg, inp)
    res = run_bass_kernel_spmd(nc, maps, core_ids=list(range(cfg.NC)))
    return res


def assemble(cfg, res):
    yp = np.zeros((cfg.NP * cfg.NC, cfg.SEQ, D), np.float32)
    ys = np.zeros((1, cfg.DSEQ, D), np.float32)
    for c in range(cfg.NC):
        y = res.results[c]["y"]
        yp[c * cfg.NP:(c + 1) * cfg.NP] = y[:cfg.NP * cfg.SEQ].reshape(cfg.NP, cfg.SEQ, D)
        ys[0, c * cfg.SSEG:(c + 1) * cfg.SSEG] = y[cfg.NP * cfg.SEQ:]
    return yp, ys


def kernel(**inputs):
    cfg = Cfg()
    res = run(cfg, inputs)
    return assemble(cfg, res)
```

```python
import math
from contextlib import ExitStack

import numpy as np
import concourse.bass as bass
import concourse.mybir as mybir
from concourse.bass_utils import run_bass_kernel_spmd

F32 = mybir.dt.float32
BF16 = mybir.dt.bfloat16
AF = mybir.ActivationFunctionType
ALU = mybir.AluOpType
AX = mybir.AxisListType

D = 1024
N_MEM = 256
GRID_W = 64
D_FF = 2816
D_IN = 10752
LN_EPS = 1e-5
RMS_EPS = 1e-6
ROPE_BASE = 10000.0


class Cfg:
    def __init__(self, nseg_p=4, seq=2048, sseg=2048, depth=4, ncores=8):
        self.NP = nseg_p
        self.SEQ = seq
        self.SSEG = sseg
        self.DEPTH = depth
        self.NC = ncores
        self.TP = seq // 128
        self.TS = sseg // 128
        self.NT = nseg_p * self.TP + self.TS
        self.T = self.NT * 128
        self.DSEQ = sseg * ncores
        self.ALPHA = (2 * depth) ** 0.25
        self.NSEG = nseg_p + 1

    def seg_tiles(self, s):
        if s < self.NP:
            return list(range(s * self.TP, (s + 1) * self.TP))
        return list(range(self.NP * self.TP, self.NT))

    def tile_seg(self, i):
        return min(i // self.TP, self.NP) if self.TP > 0 else self.NP

    def tab_row(self, i):
        if i < self.NP * self.TP:
            return i % self.TP
        return self.TP + (i - self.NP * self.TP)


class Tracker:
    NQ = 12

    def __init__(self, nc):
        self.nc = nc
        self.engs = {"pe": nc.tensor, "act": nc.scalar, "dve": nc.vector, "pool": nc.gpsimd, "sp": nc.sync}
        self.semh = {}
        self.ccnt = {}
        self.opidx = {}
        for e in ("pe", "act", "dve", "pool"):
            self.semh[("c", e)] = nc.alloc_semaphore("c_" + e)
            self.ccnt[e] = 0
            self.opidx[e] = 0
        self.dcnt = {}
        self.drr = {}
        for q in ("sp", "pool"):
            self.dcnt[q] = [0] * self.NQ
            self.drr[q] = 0
            for k in range(self.NQ):
                self.semh[("d", q, k)] = nc.alloc_semaphore("d_%s%d" % (q, k))
        self.semh[("cc",)] = nc.alloc_semaphore("ccsem")
        self.cccnt = 0
        self.waited = {e: {} for e in self.engs}
        self.lastw = {}
        self.readers = {}
        self.ninstr = 0

    def _wait(self, eng, t):
        semkey, val, teng, tidx = t
        if val <= 0:
            return
        if teng is not None and teng == eng:
            if eng == "pe":
                return
            if self.opidx[eng] - tidx > 2:
                return
        if self.waited[eng].get(semkey, 0) >= val:
            return
        self.engs[eng].wait_ge(self.semh[semkey], val)
        self.ninstr += 1
        self.waited[eng][semkey] = val

    def _deps(self, eng, reads, writes):
        for c in reads:
            t = self.lastw.get(c)
            if t is not None:
                self._wait(eng, t)
        for c in writes:
            t = self.lastw.get(c)
            if t is not None:
                self._wait(eng, t)
            rd = self.readers.get(c)
            if rd:
                for t in rd.values():
                    self._wait(eng, t)

    def _commit(self, t, reads, writes):
        for c in reads:
            self.readers.setdefault(c, {})[t[0]] = t
        for c in writes:
            self.lastw[c] = t
            self.readers[c] = {}

    def op(self, eng, reads, writes, fn):
        ps = [c for c in reads if c[0] in ("PA", "PT", "PB")]
        if ps:
            reads = [c for c in reads if c[0] not in ("PA", "PT", "PB")]
            writes = list(writes) + ps
        self._deps(eng, reads, writes)
        ins = fn(self.engs[eng])
        self.ccnt[eng] += 1
        ins.then_inc(self.semh[("c", eng)], 1)
        t = (("c", eng), self.ccnt[eng], eng, self.opidx[eng])
        self.opidx[eng] += 1
        self.ninstr += 1
        self._commit(t, reads, writes)

    def dma(self, q, reads, writes, fn):
        self._deps(q, reads, writes)
        k = self.drr[q]
        self.drr[q] = (k + 1) % self.NQ
        semkey = ("d", q, k)
        self._wait(q, (semkey, self.dcnt[q][k], None, 0))
        inss = fn(self.engs[q])
        if not isinstance(inss, (list, tuple)):
            inss = [inss]
        for ins in inss:
            ins.then_inc(self.semh[semkey], 16)
        self.dcnt[q][k] += 16 * len(inss)
        self.ninstr += len(inss)
        t = (semkey, self.dcnt[q][k], None, 0)
        self._commit(t, reads, writes)

    def collective(self, reads, writes, fn):
        self._deps("pool", reads, writes)
        ins = fn(self.engs["pool"])
        ins.then_inc(self.semh[("cc",)], 1)
        self.cccnt += 1
        t = (("cc",), self.cccnt, None, 0)
        self._commit(t, reads, writes)

    def barrier(self):
        for e in self.engs:
            for ce in ("pe", "act", "dve", "pool"):
                self._wait(e, (("c", ce), self.ccnt[ce], None, 0))
            for q in ("sp", "pool"):
                for k in range(self.NQ):
                    self._wait(e, (("d", q, k), self.dcnt[q][k], None, 0))
            self._wait(e, (("cc",), self.cccnt, None, 0))
        self.lastw = {}
        self.readers = {}


class Buf:
    def __init__(self, tensors, name):
        self.ts = tensors
        self.name = name
        self.n = len(tensors)

    def t(self, i=0):
        return self.ts[i % self.n]

    def c(self, i=0, sub=0):
        return (self.name, i % self.n, sub)


class Stage:
    def __init__(self, nc, tr, tag):
        self.nc = nc
        self.tr = tr
        self.tag = tag
        self.es = ExitStack()

    def sb(self, name, shape, dtype, nslots=1):
        ts = [self.es.enter_context(self.nc.sbuf_tensor("%s_%s_%d" % (self.tag, name, k), list(shape), dtype))
              for k in range(nslots)]
        return Buf(ts, self.tag + name)

    def close(self):
        self.tr.barrier()
        self.es.close()


def build(cfg, debug=(), stop_after=None):
    nc = bass.Bass("TRN2", target_bir_lowering=False)
    tr = Tracker(nc)
    T, NT, L = cfg.T, cfg.NT, cfg.DEPTH
    NSEG = cfg.NSEG
    ALPHA = cfg.ALPHA

    def din(name, shape, dt=F32):
        return nc.dram_tensor(name, list(shape), dt, kind="ExternalInput").ap()

    def dscr(name, shape, dt):
        kind = "ExternalOutput" if name in debug else "Internal"
        return nc.dram_tensor(name, list(shape), dt, kind=kind).ap()

    x_in = din("x", [T, D])
    mem_in = din("mem", [NSEG * N_MEM, D])
    in_ln_w = din("in_ln_w", [1, D])
    in_ln_b = din("in_ln_b", [1, D])
    w_in = din("w_in", [L, D, D_IN])
    b_gate = din("b_gate", [L, 3 * D])
    ret_decay = din("ret_decay", [L, 16])
    ret_gn_w = din("ret_gn_w", [L, D])
    ret_wo = din("ret_wo", [L, D, D])
    sg_ln_w = din("sg_ln_w", [L, D])
    sg_ln_b = din("sg_ln_b", [L, D])
    sg_ws = din("sg_ws", [L, 4, 128, 128])
    sg_b = din("sg_b", [L, 4, 128])
    sg_wo = din("sg_wo", [L, D, D])
    att_qn_w = din("att_qn_w", [L, 128])
    att_kn_w = din("att_kn_w", [L, 128])
    att_wo = din("att_wo", [L, D, D])
    w_out = din("w_out", [L, D, D])
    ln_w = din("ln_w", [L, 3, D])
    ln_b = din("ln_b", [L, 3, D])
    xa_wq = din("xa_wq", [L, D, D])
    xa_wkv = din("xa_wkv", [L, D, 2 * D])
    xa_wo = din("xa_wo", [L, D, D])
    ffn_w_in = din("ffn_w_in", [L, D, 2 * D_FF])
    ffn_w_out = din("ffn_w_out", [L, D_FF, D])
    NTAB = cfg.TP + cfg.TS
    tab_r = din("tab_r", [NTAB * 128, 512])
    tab_a = din("tab_a", [NTAB * 128, 256])
    ctab = din("ctab", [128, 1024])
    cctab = din("cctab", [128, 32])
    ident_in = din("ident", [128, 128])

    y_out = nc.dram_tensor("y", [T, D], F32, kind="ExternalOutput").ap()

    X = dscr("X", [T, D], F32)
    X1 = dscr("X1", [T, D], F32)
    X2 = dscr("X2", [T, D], F32)
    RQ = dscr("RQ", [T, D], BF16)
    RK = dscr("RK", [T, D], BF16)
    RV = dscr("RV", [T, D], BF16)
    RG = dscr("RG", [T, D], BF16)
    SU = dscr("SU", [T, D], BF16)
    SV = dscr("SV", [T, D], BF16)
    AQ = dscr("AQ", [T, D], BF16)
    AKV = dscr("AKV", [T, 512], BF16)
    GT = dscr("GT", [T, 3 * D], BF16)
    SF = dscr("STF", [NT * 128, D], BF16)
    SB = dscr("STB", [NT * 128, D], BF16)
    YR = dscr("YR", [T, D], BF16)
    YS = dscr("YS", [T, D], BF16)
    YAT = dscr("YAT", [NT * 128, D], BF16)
    MK = dscr("MK", [NSEG * 2 * 128, D], BF16)
    MV = dscr("MV", [NSEG * 2 * 128, D], BF16)
    CR = max(cfg.SSEG, 1024)
    CCin = nc.dram_tensor("CCin", [CR, 256], F32)
    CCout = nc.dram_tensor("CCout", [CR * cfg.NC, 256], F32)
    CCin_bf = CCin.ap().bitcast(BF16)
    CCout_bf = CCout.ap().bitcast(BF16)
    Eloc_v = CCin.ap()[0:1024, :].rearrange("(j q) c -> j (q c)", q=4)

    def Eall_v(c2):
        return CCout.ap()[c2 * CR:c2 * CR + 1024, :].rearrange("(j q) c -> j (q c)", q=4)

    def allgather():
        tr.collective([("ccin",)], [("ccout",)], lambda e: e.collective_compute(
            "AllGather", ALU.bypass, replica_groups=[list(range(cfg.NC))],
            ins=[CCin.ap().opt()], outs=[CCout.ap().opt()]))

    gs = ExitStack()

    def gsb(name, shape, dt):
        return gs.enter_context(nc.sbuf_tensor(name, list(shape), dt))

    ident = gsb("identb", [128, 128], BF16)
    ones = gsb("onesb", [128, 128], BF16)
    ones32 = gsb("ones32", [128, 128], F32)
    ctb = gsb("ctb", [128, 1024], F32)
    cct = gsb("cct", [128, 32], F32)
    pa = [gs.enter_context(nc.psum_tensor("pa%d" % k, [128, 1024], F32)) for k in range(2)]
    pt = [gs.enter_context(nc.psum_tensor("pt%d" % k, [128, 1024], BF16)) for k in range(2)]
    pb = [gs.enter_context(nc.psum_tensor("pb%d" % k, [128, 512], F32)) for k in range(2)]
    PA = Buf(pa, "PA")
    PT = Buf(pt, "PT")
    PB = Buf(pb, "PB")
    C_PF, C_MF, C_PB, C_MB, C_RP1, C_RM = 0, 128, 256, 384, 512, 640
    C_Z = 768
    C_I4 = 772

    tr.dma("pool", [], [("ident",)], lambda e: e.dma_start(out=ident[:], in_=ident_in))
    tr.dma("sp", [], [("ctb",)], lambda e: e.dma_start(out=ctb[:], in_=ctab))
    tr.dma("sp", [], [("cct",)], lambda e: e.dma_start(out=cct[:], in_=cctab))
    tr.op("dve", [], [("ones",)], lambda e: e.memset(ones[:], 1.0))
    tr.op("dve", [], [("ones32",)], lambda e: e.memset(ones32[:], 1.0))
    tr.barrier()

    def V(reads, writes, fn):
        tr.op("dve", reads, writes, fn)

    def A(reads, writes, fn):
        tr.op("act", reads, writes, fn)

    def G(reads, writes, fn):
        tr.op("pool", reads, writes, fn)

    def P(reads, writes, fn):
        tr.op("pe", reads, writes, fn)

    def LD(reads, writes, fn):
        tr.dma("sp", reads, writes, fn)

    def STO(reads, writes, fn):
        tr.dma("sp", reads, writes, fn)

    def rows(ap2d, i):
        return ap2d[i * 128:(i + 1) * 128, :]

    def load_w(st, name, src2d, nk, ncols, col0=0):
        buf = st.sb(name, [128, nk, ncols], BF16)
        w = buf.t()
        step = max(1, 4096 // ncols)
        k = 0
        while k < nk:
            k2 = min(nk, k + step)
            tr.dma("pool", [], [buf.c()],
                   lambda e, k=k, k2=k2: e.dma_start(
                       out=w[:, k:k2, :],
                       in_=src2d[k * 128:k2 * 128, col0:col0 + ncols].rearrange("(k p) c -> p k c", p=128)))
            k = k2
        return buf

    def load_bcast(st, name, src_row, n, dt=F32):
        buf = st.sb(name, [128, n], dt)
        q = "sp" if dt == F32 else "pool"
        tr.dma(q, [], [buf.c()], lambda e: e.dma_start(out=buf.t()[:], in_=src_row.to_broadcast([128, n])))
        return buf

    def transposes(src_fn, n, ptslot, reads):
        def f(e):
            ins = None
            for j in range(n):
                ins = e.transpose(PT.t(ptslot)[:, j * 128:(j + 1) * 128], src_fn(j), ident[:])
            return ins
        P(reads + [("ident",)], [PT.c(ptslot)], f)

    def gemm(lhs_fn, nk, wbuf, col0, ncols, out_ap, reads, writes, extra=None):
        def f(e):
            ins = None
            for k in range(nk):
                ins = e.matmul(out_ap, lhsT=lhs_fn(k), rhs=wbuf.t()[:, k, col0:col0 + ncols],
                               start=(k == 0), stop=(k == nk - 1 and extra is None))
            if extra is not None:
                ins = e.matmul(out_ap, lhsT=extra[0], rhs=extra[1], start=False, stop=True)
            return ins
        P(reads + [wbuf.c()], writes, f)

    pt_rr = [0]
    pa_rr = [0]

    def next_pt():
        pt_rr[0] += 1
        return pt_rr[0]

    def next_pa():
        pa_rr[0] += 1
        return pa_rr[0]

    def to_T(src_buf, slot, nblk, dst_buf, dslot, copy_eng="dve"):
        p = next_pt()
        src = src_buf.t(slot)
        transposes(lambda j: src[:, j * 128:(j + 1) * 128], nblk, p, [src_buf.c(slot)])
        dst = dst_buf.t(dslot)
        fn = lambda e: e.tensor_copy(out=dst[:].rearrange("p a b -> p (a b)")[:, 0:nblk * 128],
                                     in_=PT.t(p)[:, 0:nblk * 128])
        if copy_eng == "act":
            fn = lambda e: e.copy(out=dst[:].rearrange("p a b -> p (a b)")[:, 0:nblk * 128],
                                  in_=PT.t(p)[:, 0:nblk * 128])
        tr.op(copy_eng, [PT.c(p)], [dst_buf.c(dslot)], fn)

    def rsqrt(ap, cells, premul, eps):
        V(cells, cells, lambda e: e.tensor_scalar(out=ap, in0=ap, scalar1=premul, scalar2=eps, op0=ALU.mult, op1=ALU.add))
        A(cells, cells, lambda e: e.activation(out=ap, in_=ap, func=AF.Sqrt))
        V(cells, cells, lambda e: e.reciprocal(out=ap, in_=ap))

    def layer_norm(src_ap, src_cells, st_bufs, slot, wrep, brep, dst_ap, dst_cells, tmp_ap=None, tmp_cells=None):
        stats, mv, rs = st_bufs
        s_t, mv_t, rs_t = stats.t(slot), mv.t(slot), rs.t(slot)
        V(src_cells, [stats.c(slot)], lambda e: (
            e.bn_stats(out=s_t[:, 0:6], in_=src_ap[:, 0:512]),
            e.bn_stats(out=s_t[:, 6:12], in_=src_ap[:, 512:1024]))[-1])
        V([stats.c(slot)], [mv.c(slot)], lambda e: e.bn_aggr(out=mv_t[:, 0:2], in_=s_t[:, 0:12]))
        V([mv.c(slot)], [rs.c(slot)], lambda e: e.tensor_copy(out=rs_t[:, 0:1], in_=mv_t[:, 1:2]))
        rsqrt(rs_t[:, 0:1], [rs.c(slot)], 1.0, LN_EPS)
        t_ap = tmp_ap if tmp_ap is not None else src_ap
        t_cells = tmp_cells if tmp_cells is not None else src_cells
        V(src_cells + [mv.c(slot), rs.c(slot)], t_cells, lambda e: e.tensor_scalar(
            out=t_ap, in0=src_ap, scalar1=mv_t[:, 0:1], scalar2=rs_t[:, 0:1], op0=ALU.subtract, op1=ALU.mult))
        G(t_cells + [wrep.c()], t_cells, lambda e: e.tensor_tensor(out=t_ap, in0=t_ap, in1=wrep.t()[:], op=ALU.mult))
        G(t_cells + [brep.c()], dst_cells, lambda e: e.tensor_tensor(out=dst_ap, in0=t_ap, in1=brep.t()[:], op=ALU.add))

    def ln_bufs(st, tag, nslots=2):
        return (st.sb(tag + "st", [128, 12], F32, nslots), st.sb(tag + "mv", [128, 2], F32, nslots),
                st.sb(tag + "rs", [128, 1], F32, nslots))

    st = Stage(nc, tr, "l0")
    wrep = load_bcast(st, "w", in_ln_w, D)
    brep = load_bcast(st, "b", in_ln_b, D)
    xin = st.sb("xin", [128, D], F32, 3)
    xo = st.sb("xo", [128, D], F32, 2)
    lb = ln_bufs(st, "ln")
    for i in range(NT):
        LD([], [xin.c(i)], lambda e, i=i: e.dma_start(out=xin.t(i)[:], in_=rows(x_in, i)))
        layer_norm(xin.t(i)[:], [xin.c(i)], lb, i, wrep, brep, xo.t(i)[:], [xo.c(i)])
        STO([xo.c(i)], [], lambda e, i=i: e.dma_start(out=rows(X, i), in_=xo.t(i)[:]))
    st.close()
    if stop_after is not None and stop_after == ("l0"):
        tr.barrier()
        return nc, tr

    def load_xT(st, Xsrc, i, xin, xb, xT):
        LD([], [xin.c(i)], lambda e: e.dma_start(out=xin.t(i)[:], in_=rows(Xsrc, i)))
        A([xin.c(i)], [xb.c(i)], lambda e: e.copy(out=xb.t(i)[:], in_=xin.t(i)[:]))
        to_T(xb, i, 8, xT, i)

    for l in range(L):
        lt = "y%d" % l
        ls = Stage(nc, tr, lt + "p")
        lgt = ls.sb("lg", [128, 16], F32)
        tmp16 = ls.sb("t16", [128, 16], F32)
        tr.dma("sp", [], [tmp16.c()], lambda e: e.dma_start(
            out=tmp16.t()[:], in_=ret_decay[l:l + 1, :].to_broadcast([128, 16])))
        A([tmp16.c()], [lgt.c()], lambda e: e.activation(out=lgt.t()[:], in_=tmp16.t()[:], func=AF.Exp, scale=-1.0))
        V([lgt.c()], [tmp16.c()], lambda e: e.tensor_scalar(
            out=tmp16.t()[:], in0=lgt.t()[:], scalar1=1.0, scalar2=None, op0=ALU.add))
        A([tmp16.c()], [lgt.c()], lambda e: e.activation(out=lgt.t()[:], in_=tmp16.t()[:], func=AF.Ln))
        V([lgt.c()], [lgt.c()], lambda e: e.tensor_scalar(
            out=lgt.t()[:], in0=lgt.t()[:], scalar1=-1.0, scalar2=None, op0=ALU.mult))
        lg = lgt.t()
        DTb = ls.sb("DT", [128, 8, 128], F32)
        dtmp = ls.sb("dtmp", [128, 8, 128], F32)
        for h in range(8):
            A([lgt.c(), ("ctb",)], [DTb.c()], lambda e, h=h: e.activation(
                out=DTb.t()[:, h, :], in_=ctb[:, C_PF:C_PF + 128], func=AF.Exp, scale=lg[:, h:h + 1]))
            A([lgt.c(), ("ctb",)], [dtmp.c()], lambda e, h=h: e.activation(
                out=dtmp.t()[:, h, :], in_=ctb[:, C_PB:C_PB + 128], func=AF.Exp, scale=lg[:, 8 + h:9 + h]))
        V([DTb.c(), ("ctb",)], [DTb.c()], lambda e: e.tensor_tensor(
            out=DTb.t()[:], in0=DTb.t()[:], in1=ctb[:, C_MF:C_MF + 128].unsqueeze(1).to_broadcast([128, 8, 128]),
            op=ALU.mult))
        V([dtmp.c(), ("ctb",)], [dtmp.c()], lambda e: e.tensor_tensor(
            out=dtmp.t()[:], in0=dtmp.t()[:], in1=ctb[:, C_MB:C_MB + 128].unsqueeze(1).to_broadcast([128, 8, 128]),
            op=ALU.mult))
        V([DTb.c(), dtmp.c()], [DTb.c()], lambda e: e.tensor_tensor(
            out=DTb.t()[:], in0=DTb.t()[:], in1=dtmp.t()[:], op=ALU.add))
        XIF = ls.sb("XIF", [128, 8, 128], F32)
        XIB = ls.sb("XIB", [128, 8, 128], F32)
        for h in range(8):
            A([lgt.c(), ("ctb",)], [XIF.c()], lambda e, h=h: e.activation(
                out=XIF.t()[:, h, :], in_=ctb[:, C_RP1:C_RP1 + 128], func=AF.Exp, scale=lg[:, h:h + 1]))
            A([lgt.c(), ("ctb",)], [XIB.c()], lambda e, h=h: e.activation(
                out=XIB.t()[:, h, :], in_=ctb[:, C_RM:C_RM + 128], func=AF.Exp, scale=lg[:, 8 + h:9 + h]))
        ZT = ls.sb("ZT", [128, 16], F32)
        CDt = ls.sb("CD", [128, 16], F32)
        A([lgt.c(), ("ctb",)], [ZT.c()], lambda e: e.activation(
            out=ZT.t()[:, 0:8], in_=lg[:, 0:8], func=AF.Exp, scale=ctb[:, C_Z:C_Z + 1]))
        A([lgt.c(), ("ctb",)], [ZT.c()], lambda e: e.activation(
            out=ZT.t()[:, 8:16], in_=lg[:, 8:16], func=AF.Exp, scale=ctb[:, C_Z + 1:C_Z + 2]))
        A([lgt.c()], [CDt.c()], lambda e: e.activation(out=CDt.t()[:], in_=lg[:], func=AF.Exp, scale=128.0))
        CF = ls.sb("CF", [128, 8, 8], F32)
        CB = ls.sb("CB", [128, 8, 8], F32)
        for c2 in range(cfg.NC):
            A([lgt.c(), ("cct",)], [CF.c()], lambda e, c2=c2: e.activation(
                out=CF.t()[:, c2, :], in_=lg[:, 0:8], func=AF.Exp, scale=cct[:, c2:c2 + 1]))
            A([lgt.c(), ("cct",)], [CB.c()], lambda e, c2=c2: e.activation(
                out=CB.t()[:, c2, :], in_=lg[:, 8:16], func=AF.Exp, scale=cct[:, 8 + c2:9 + c2]))
        V([CF.c(), ("cct",)], [CF.c()], lambda e: e.tensor_tensor(
            out=CF.t()[:], in0=CF.t()[:], in1=cct[:, 16:24].unsqueeze(2).to_broadcast([128, 8, 8]), op=ALU.mult))
        V([CB.c(), ("cct",)], [CB.c()], lambda e: e.tensor_tensor(
            out=CB.t()[:], in0=CB.t()[:], in1=cct[:, 24:32].unsqueeze(2).to_broadcast([128, 8, 8]), op=ALU.mult))
        wq_rep = ls.sb("wqr", [128, 128], F32)
        wk_rep = ls.sb("wkr", [128, 128], F32)
        tr.dma("sp", [], [wq_rep.c()], lambda e: e.dma_start(
            out=wq_rep.t()[:], in_=att_qn_w[l:l + 1, :].to_broadcast([128, 128])))
        tr.dma("sp", [], [wk_rep.c()], lambda e: e.dma_start(
            out=wk_rep.t()[:], in_=att_kn_w[l:l + 1, :].to_broadcast([128, 128])))
        mx = ls.sb("mx", [128, 4], F32)
        V([wq_rep.c()], [mx.c(0, 1)], lambda e: e.tensor_reduce(
            out=mx.t()[:, 0:1], in_=wq_rep.t()[:], axis=AX.X, op=ALU.max, apply_absolute_value=True))
        V([wk_rep.c()], [mx.c(0, 2)], lambda e: e.tensor_reduce(
            out=mx.t()[:, 1:2], in_=wk_rep.t()[:], axis=AX.X, op=ALU.max, apply_absolute_value=True))
        V([mx.c(0, 1), mx.c(0, 2)], [mx.c(0, 3)], lambda e: e.tensor_scalar(
            out=mx.t()[:, 2:3], in0=mx.t()[:, 0:1], scalar1=mx.t()[:, 1:2], scalar2=-math.sqrt(128.0),
            op0=ALU.mult, op1=ALU.mult))
        negM = mx.t()[:, 2:3]
        negM_c = mx.c(0, 3)
        wq_sw = ls.sb("wqs", [128, 128], F32)
        wk_sw = ls.sb("wks", [128, 128], F32)
        for (src, dst) in ((wq_rep, wq_sw), (wk_rep, wk_sw)):
            s4 = src.t()[:].rearrange("p (a h b) -> p a h b", a=2, h=2)
            d4 = dst.t()[:].rearrange("p (a h b) -> p a h b", a=2, h=2)
            V([src.c()], [dst.c()], lambda e, s4=s4, d4=d4: (
                e.tensor_copy(out=d4[:, :, 0, :], in_=s4[:, :, 1, :]),
                e.tensor_copy(out=d4[:, :, 1, :], in_=s4[:, :, 0, :]))[-1])
        tr.barrier()

        st = Stage(nc, tr, lt + "m")
        wkv = load_w(st, "wkv", xa_wkv[l], 8, 2 * D)
        xin = st.sb("xin", [128, D], F32, 2)
        xb = st.sb("xb", [128, D], BF16, 2)
        xT = st.sb("xT", [128, 8, 128], BF16, 2)
        kb_ = st.sb("kb", [128, D], BF16, 2)
        kT = st.sb("kT", [128, 8, 128], BF16, 2)
        vb_ = st.sb("vb", [128, D], BF16, 2)
        for i in range(NSEG * 2):
            load_xT(st, mem_in, i, xin, xb, xT)
            for half, (dst, eng) in enumerate(((kb_, "act"), (vb_, "dve"))):
                p = next_pa()
                for cbk in range(2):
                    gemm(lambda k: xT.t(i)[:, k, :], 8, wkv, half * D + cbk * 512, 512,
                         PA.t(p)[:, cbk * 512:(cbk + 1) * 512], [xT.c(i)], [PA.c(p, cbk)])
                if eng == "act":
                    A([PA.c(p, 0), PA.c(p, 1)], [dst.c(i)], lambda e, p=p, dst=dst: e.copy(out=dst.t(i)[:], in_=PA.t(p)[:]))
                else:
                    V([PA.c(p, 0), PA.c(p, 1)], [dst.c(i)], lambda e, p=p, dst=dst: e.tensor_copy(out=dst.t(i)[:], in_=PA.t(p)[:]))
            to_T(kb_, i, 8, kT, i, copy_eng="act")
            STO([kT.c(i)], [], lambda e, i=i: e.dma_start(out=rows(MK, i), in_=kT.t(i)[:].rearrange("p a b -> p (a b)")))
            STO([vb_.c(i)], [], lambda e, i=i: e.dma_start(out=rows(MV, i), in_=vb_.t(i)[:]))
        st.close()
        if stop_after is not None and stop_after == (lt + "m"):
            tr.barrier()
            return nc, tr

        st = Stage(nc, tr, lt + "a")
        wa = load_w(st, "w", w_in[l], 8, 4096, 0)
        xin = st.sb("xin", [128, D], F32, 2)
        xb = st.sb("xb", [128, D], BF16, 2)
        xT = st.sb("xT", [128, 8, 128], BF16, 2)
        tabr = st.sb("tabr", [128, 512], F32, 2)
        xs = st.sb("xs", [128, 8, 128], F32, 2)
        t1 = st.sb("t1", [128, 8, 128], F32, 2)
        u = st.sb("u", [128, 8, 128], F32, 2)
        ob = st.sb("ob", [128, D], BF16, 4)
        oc = [0]

        def rope_ret(i, p, tab_off, dstD):
            k = oc[0]
            oc[0] += 1
            tb = tabr.t(i)
            Cb = tb[:, tab_off:tab_off + 128].unsqueeze(1).to_broadcast([128, 8, 128])
            Slo = tb[:, tab_off + 128:tab_off + 192].unsqueeze(1).to_broadcast([128, 8, 64])
            Shi = tb[:, tab_off + 192:tab_off + 256].unsqueeze(1).to_broadcast([128, 8, 64])
            A([PA.c(p, 0), PA.c(p, 1)], [xs.c(k)], lambda e: e.copy(
                out=xs.t(k)[:].rearrange("p a b -> p (a b)"), in_=PA.t(p)[:]))
            V([xs.c(k), tabr.c(i)], [t1.c(k)], lambda e: e.tensor_tensor(out=t1.t(k)[:], in0=xs.t(k)[:], in1=Cb, op=ALU.mult))
            G([xs.c(k), tabr.c(i)], [u.c(k)], lambda e: (
                e.tensor_tensor(out=u.t(k)[:, :, 0:64], in0=xs.t(k)[:, :, 64:128], in1=Slo, op=ALU.mult),
                e.tensor_tensor(out=u.t(k)[:, :, 64:128], in0=xs.t(k)[:, :, 0:64], in1=Shi, op=ALU.mult))[-1])
            V([t1.c(k), u.c(k)], [ob.c(k)], lambda e: e.tensor_tensor(
                out=ob.t(k)[:].rearrange("p (a b) -> p a b", a=8), in0=t1.t(k)[:], in1=u.t(k)[:], op=ALU.add))
            STO([ob.c(k)], [], lambda e: e.dma_start(out=rows(dstD, i), in_=ob.t(k)[:]))

        for i in range(NT):
            load_xT(st, X, i, xin, xb, xT)
            LD([], [tabr.c(i)], lambda e, i=i: e.dma_start(out=tabr.t(i)[:], in_=rows(tab_r, cfg.tab_row(i))))
            for grp in range(4):
                p = next_pa()
                for cbk in range(2):
                    gemm(lambda k: xT.t(i)[:, k, :], 8, wa, grp * 1024 + cbk * 512, 512,
                         PA.t(p)[:, cbk * 512:(cbk + 1) * 512], [xT.c(i)], [PA.c(p, cbk)])
                if grp == 0:
                    rope_ret(i, p, 0, RQ)
                elif grp == 1:
                    rope_ret(i, p, 256, RK)
                else:
                    k = oc[0]
                    oc[0] += 1
                    fn = AF.Copy if grp == 2 else AF.Silu
                    A([PA.c(p, 0), PA.c(p, 1)], [ob.c(k)], lambda e, p=p, k=k, fn=fn: e.activation(
                        out=ob.t(k)[:], in_=PA.t(p)[:], func=fn))
                    dstD = RV if grp == 2 else RG
                    STO([ob.c(k)], [], lambda e, k=k, dstD=dstD, i=i: e.dma_start(out=rows(dstD, i), in_=ob.t(k)[:]))
        st.close()
        if stop_after is not None and stop_after == (lt + "a"):
            tr.barrier()
            return nc, tr

        st = Stage(nc, tr, lt + "b")
        wb = load_w(st, "w", w_in[l], 8, 3584, 4096)
        sgw = load_bcast(st, "sgw", sg_ln_w[l:l + 1, :], D)
        sgb = load_bcast(st, "sgb", sg_ln_b[l:l + 1, :], D)
        xin = st.sb("xin", [128, D], F32, 2)
        xb = st.sb("xb", [128, D], BF16, 2)
        xT = st.sb("xT", [128, 8, 128], BF16, 2)
        taba = st.sb("taba", [128, 256], F32, 2)
        cw = st.sb("cw", [128, 4, 128], F32, 2)
        xs = st.sb("xs", [128, 8, 128], F32, 2)
        sq = st.sb("sq", [128, 8, 128], F32, 2)
        t1 = st.sb("t1", [128, 8, 128], F32, 2)
        u = st.sb("u", [128, 8, 128], F32, 2)
        ss = st.sb("ss", [128, 16], F32, 2)
        svf = st.sb("svf", [128, D], F32, 2)
        ob = st.sb("ob", [128, D], BF16, 4)
        okv = st.sb("okv", [128, 512], BF16, 2)
        lb = ln_bufs(st, "ln")
        oc = [0]

        def axial(i, k, nh, Ci, Si, src3, dst3, rst, rs_off, cells_in, cells_out):
            cwt = cw.t(i)
            G(cells_in, [sq.c(k)], lambda e: e.tensor_tensor(out=sq.t(k)[:, 0:nh, :], in0=src3, in1=src3, op=ALU.mult))
            V([sq.c(k)], [ss.c(k, rs_off)], lambda e: e.tensor_reduce(
                out=rst[:, rs_off:rs_off + nh], in_=sq.t(k)[:, 0:nh, :], axis=AX.X, op=ALU.add))
            rsqrt(rst[:, rs_off:rs_off + nh], [ss.c(k, rs_off)], 1.0 / 128.0, RMS_EPS)
            Cb = cwt[:, Ci, :].unsqueeze(1).to_broadcast([128, nh, 128])
            V(cells_in + [cw.c(i)], [t1.c(k)], lambda e: e.tensor_tensor(out=t1.t(k)[:, 0:nh, :], in0=src3, in1=Cb, op=ALU.mult))
            s5 = src3.rearrange("p h (a c b) -> p h a c b", a=2, c=2)
            u5 = u.t(k)[:, 0:nh, :].rearrange("p h (a c b) -> p h a c b", a=2, c=2)
            S4 = cwt[:, Si, :].rearrange("p (a c b) -> p a c b", a=2, c=2)

            def fu2(e):
                ins = None
                for a in range(2):
                    for c in range(2):
                        ins = e.tensor_tensor(out=u5[:, :, a, c, :], in0=s5[:, :, a, 1 - c, :],
                                              in1=S4[:, a, c, :].unsqueeze(1).to_broadcast([128, nh, 32]), op=ALU.mult)
                return ins
            G(cells_in + [cw.c(i)], [u.c(k)], fu2)
            V([t1.c(k), u.c(k)], [t1.c(k)], lambda e: e.tensor_tensor(
                out=t1.t(k)[:, 0:nh, :], in0=t1.t(k)[:, 0:nh, :], in1=u.t(k)[:, 0:nh, :], op=ALU.add))
            V([t1.c(k), ss.c(k, rs_off)], cells_out, lambda e: e.tensor_tensor(
                out=dst3, in0=t1.t(k)[:, 0:nh, :],
                in1=rst[:, rs_off:rs_off + nh].unsqueeze(2).to_broadcast([128, nh, 128]), op=ALU.mult))

        for i in range(NT):
            load_xT(st, X, i, xin, xb, xT)
            LD([], [taba.c(i)], lambda e, i=i: e.dma_start(out=taba.t(i)[:], in_=rows(tab_a, cfg.tab_row(i))))
            ta = taba.t(i)
            G([taba.c(i), wq_rep.c(), wq_sw.c(), wk_rep.c(), wk_sw.c()], [cw.c(i)], lambda e, i=i, ta=ta: (
                e.tensor_tensor(out=cw.t(i)[:, 0, :], in0=ta[:, 0:128], in1=wq_rep.t()[:], op=ALU.mult),
                e.tensor_tensor(out=cw.t(i)[:, 1, :], in0=ta[:, 128:256], in1=wq_sw.t()[:], op=ALU.mult),
                e.tensor_tensor(out=cw.t(i)[:, 2, :], in0=ta[:, 0:128], in1=wk_rep.t()[:], op=ALU.mult),
                e.tensor_tensor(out=cw.t(i)[:, 3, :], in0=ta[:, 128:256], in1=wk_sw.t()[:], op=ALU.mult))[-1])
            p = next_pa()
            for cbk in range(2):
                gemm(lambda k: xT.t(i)[:, k, :], 8, wb, cbk * 512, 512,
                     PA.t(p)[:, cbk * 512:(cbk + 1) * 512], [xT.c(i)], [PA.c(p, cbk)])
            k = oc[0]; oc[0] += 1
            A([PA.c(p, 0), PA.c(p, 1)], [ob.c(k)], lambda e, p=p, k=k: e.activation(
                out=ob.t(k)[:], in_=PA.t(p)[:], func=AF.Gelu_apprx_tanh))
            STO([ob.c(k)], [], lambda e, k=k, i=i: e.dma_start(out=rows(SU, i), in_=ob.t(k)[:]))
            p = next_pa()
            for cbk in range(2):
                gemm(lambda k: xT.t(i)[:, k, :], 8, wb, 1024 + cbk * 512, 512,
                     PA.t(p)[:, cbk * 512:(cbk + 1) * 512], [xT.c(i)], [PA.c(p, cbk)])
            A([PA.c(p, 0), PA.c(p, 1)], [svf.c(i)], lambda e, p=p, i=i: e.activation(
                out=svf.t(i)[:], in_=PA.t(p)[:], func=AF.Gelu_apprx_tanh))
            k = oc[0]; oc[0] += 1
            layer_norm(svf.t(i)[:], [svf.c(i)], lb, i, sgw, sgb, ob.t(k)[:], [ob.c(k)])
            STO([ob.c(k)], [], lambda e, k=k, i=i: e.dma_start(out=rows(SV, i), in_=ob.t(k)[:]))
            p = next_pa()
            for cbk in range(2):
                gemm(lambda k: xT.t(i)[:, k, :], 8, wb, 2048 + cbk * 512, 512,
                     PA.t(p)[:, cbk * 512:(cbk + 1) * 512], [xT.c(i)], [PA.c(p, cbk)])
            k = oc[0]; oc[0] += 1
            A([PA.c(p, 0), PA.c(p, 1)], [xs.c(k)], lambda e, p=p, k=k: e.copy(
                out=xs.t(k)[:].rearrange("p a b -> p (a b)"), in_=PA.t(p)[:]))
            axial(i, k, 8, 0, 1, xs.t(k)[:], ob.t(k)[:].rearrange("p (a b) -> p a b", a=8), ss.t(k), 0,
                  [xs.c(k)], [ob.c(k)])
            STO([ob.c(k)], [], lambda e, k=k, i=i: e.dma_start(out=rows(AQ, i), in_=ob.t(k)[:]))
            p = next_pa()
            gemm(lambda k: xT.t(i)[:, k, :], 8, wb, 3072, 512, PA.t(p)[:, 0:512], [xT.c(i)], [PA.c(p, 0)])
            k2 = oc[0]; oc[0] += 1
            A([PA.c(p, 0)], [xs.c(k2)], lambda e, p=p, k2=k2: e.copy(
                out=xs.t(k2)[:].rearrange("p a b -> p (a b)")[:, 0:512], in_=PA.t(p)[:, 0:512]))
            axial(i, k2, 2, 2, 3, xs.t(k2)[:, 0:2, :], okv.t(i)[:, 0:256].rearrange("p (a b) -> p a b", a=2), ss.t(k2), 8,
                  [xs.c(k2)], [okv.c(i, 0)])
            V([xs.c(k2)], [okv.c(i, 1)], lambda e, k2=k2, i=i: e.tensor_copy(
                out=okv.t(i)[:, 256:512], in_=xs.t(k2)[:, 2:4, :].rearrange("p a b -> p (a b)")))
            STO([okv.c(i, 0), okv.c(i, 1)], [], lambda e, i=i: e.dma_start(out=rows(AKV, i), in_=okv.t(i)[:]))
        st.close()
        if stop_after is not None and stop_after == (lt + "b"):
            tr.barrier()
            return nc, tr

        st = Stage(nc, tr, lt + "g")
        wg = load_w(st, "w", w_in[l], 8, 3 * D, 4096 + 3584)
        bgr = st.sb("bg", [1, 3 * D], BF16)
        tr.dma("pool", [], [bgr.c()], lambda e: e.dma_start(out=bgr.t()[:], in_=b_gate[l:l + 1, :]))
        xin = st.sb("xin", [128, D], F32, 2)
        xb = st.sb("xb", [128, D], BF16, 2)
        xT = st.sb("xT", [128, 8, 128], BF16, 2)
        og = st.sb("og", [128, 3 * D], BF16, 2)
        for i in range(NT):
            load_xT(st, X, i, xin, xb, xT)
            for g3 in range(3):
                p = next_pa()
                for cbk in range(2):
                    c0 = g3 * 1024 + cbk * 512
                    gemm(lambda k: xT.t(i)[:, k, :], 8, wg, c0, 512,
                         PA.t(p)[:, cbk * 512:(cbk + 1) * 512], [xT.c(i), bgr.c(), ("ones",)], [PA.c(p, cbk)],
                         extra=(ones[0:1, :], bgr.t()[0:1, c0:c0 + 512]))
                A([PA.c(p, 0), PA.c(p, 1)], [og.c(i, g3)], lambda e, p=p, i=i, g3=g3: e.activation(
                    out=og.t(i)[:, g3 * 1024:(g3 + 1) * 1024], in_=PA.t(p)[:], func=AF.Sigmoid))
            STO([og.c(i, 0), og.c(i, 1), og.c(i, 2)], [], lambda e, i=i: e.dma_start(out=rows(GT, i), in_=og.t(i)[:]))
        st.close()
        if stop_after is not None and stop_after == (lt + "g"):
            tr.barrier()
            return nc, tr

        st = Stage(nc, tr, lt + "s")
        rk_t = st.sb("rk", [128, 8, 128], BF16, 2)
        rv_t = st.sb("rv", [128, 8, 128], BF16, 2)
        kz = st.sb("kz", [128, 8, 128], BF16, 2)
        S = st.sb("S", [128, 8, 128], F32, 1)
        Sb16 = st.sb("Sb", [128, 8, 128], BF16, 2)
        eall = st.sb("eall", [128, 8, 128], F32, 2)
        it = [0]

        def scan(tiles, direction, store, init_from=None):
            order = tiles if direction == 0 else tiles[::-1]
            zoff = 8 * direction
            Sd = SF if direction == 0 else SB
            S3 = S.t()[:]
            if init_from is None:
                V([], [S.c()], lambda e: e.memset(S3.rearrange("p a b -> p (a b)"), 0.0))
            for i in order:
                j = it[0]; it[0] += 1
                LD([], [rk_t.c(j)], lambda e: e.dma_start(out=rk_t.t(j)[:].rearrange("p a b -> p (a b)"), in_=rows(RK, i)))
                LD([], [rv_t.c(j)], lambda e: e.dma_start(out=rv_t.t(j)[:].rearrange("p a b -> p (a b)"), in_=rows(RV, i)))
                if store:
                    A([S.c()], [Sb16.c(j)], lambda e: e.copy(out=Sb16.t(j)[:], in_=S3))
                    STO([Sb16.c(j)], [], lambda e: e.dma_start(out=rows(Sd, i), in_=Sb16.t(j)[:].rearrange("p a b -> p (a b)")))
                V([rk_t.c(j), ZT.c()], [kz.c(j)], lambda e: e.tensor_tensor(
                    out=kz.t(j)[:], in0=rk_t.t(j)[:],
                    in1=ZT.t()[:, zoff:zoff + 8].unsqueeze(2).to_broadcast([128, 8, 128]), op=ALU.mult))
                p = next_pa()

                def f(e):
                    ins = None
                    for h in range(8):
                        ins = e.matmul(PA.t(p)[:, h * 128:(h + 1) * 128], lhsT=kz.t(j)[:, h, :], rhs=rv_t.t(j)[:, h, :],
                                       start=True, stop=True)
                    return ins
                P([kz.c(j), rv_t.c(j)], [PA.c(p, 0), PA.c(p, 1)], f)
                V([S.c(), CDt.c()], [S.c()], lambda e: e.tensor_tensor(
                    out=S3, in0=S3, in1=CDt.t()[:, zoff:zoff + 8].unsqueeze(2).to_broadcast([128, 8, 128]), op=ALU.mult))
                V([S.c(), PA.c(p, 0), PA.c(p, 1)], [S.c()], lambda e: e.tensor_tensor(
                    out=S3, in0=S3, in1=PA.t(p)[:].rearrange("p (a b) -> p a b", a=8), op=ALU.add))

        for s in range(cfg.NP):
            scan(cfg.seg_tiles(s), 0, True)
            scan(cfg.seg_tiles(s), 1, True)
        stl = cfg.seg_tiles(cfg.NP)
        import os as _os
        DBG = _os.environ.get("DBGSKIP", "")
        for d in range(2):
            if "pre" in DBG:
                break
            scan(stl, d, False)
            tr.dma("sp", [S.c()], [("ccin",)], lambda e, d=d: e.dma_start(
                out=Eloc_v[d * 128:(d + 1) * 128, :], in_=S.t()[:].rearrange("p a b -> p (a b)")))
        if "cc" not in DBG:
            allgather()
        for d in range(2):
            if "post" in DBG:
                break
            S3 = S.t()[:]
            V([], [S.c()], lambda e: e.memset(S3.rearrange("p a b -> p (a b)"), 0.0))
            CO = CF if d == 0 else CB
            for c2 in range(cfg.NC):
                j = it[0]; it[0] += 1
                LD([("ccout",)], [eall.c(j)], lambda e, c2=c2, j=j, d=d: e.dma_start(
                    out=eall.t(j)[:].rearrange("p a b -> p (a b)"),
                    in_=Eall_v(c2)[d * 128:(d + 1) * 128, :]))
                V([eall.c(j), CO.c()], [eall.c(j)], lambda e, c2=c2, j=j, CO=CO: e.tensor_tensor(
                    out=eall.t(j)[:], in0=eall.t(j)[:],
                    in1=CO.t()[:, c2, :].unsqueeze(2).to_broadcast([128, 8, 128]), op=ALU.mult))
                V([eall.c(j), S.c()], [S.c()], lambda e, j=j: e.tensor_tensor(out=S3, in0=S3, in1=eall.t(j)[:], op=ALU.add))
            scan(stl, d, True, init_from=True)
        st.close()
        if stop_after is not None and stop_after == (lt + "s"):
            tr.barrier()
            return nc, tr

        s0 = cfg.NP * cfg.TP * 128
        tr.dma("sp", [], [("ccin",)], lambda e: e.dma_start(out=CCin_bf[0:cfg.SSEG, :], in_=AKV[s0:s0 + cfg.SSEG, :]))
        if "kvx" not in DBG:
            allgather()
        tr.barrier()

        st = Stage(nc, tr, lt + "r")
        gnw = load_bcast(st, "gnw", ret_gn_w[l:l + 1, :], D)
        wsT = st.sb("wsT", [128, 4, 128], BF16)
        wsl = st.sb("wsl", [128, 4, 128], BF16)
        tr.dma("pool", [], [wsl.c()], lambda e: e.dma_start(out=wsl.t()[:], in_=sg_ws[l].rearrange("g c m -> c g m")))
        pq = next_pt()
        transposes(lambda j: wsl.t()[:, j, :], 4, pq, [wsl.c()])
        V([PT.c(pq)], [wsT.c()], lambda e: e.tensor_copy(out=wsT.t()[:].rearrange("p a b -> p (a b)"), in_=PT.t(pq)[:, 0:512]))
        sgbt = st.sb("sgbt", [128, 4], F32)
        sgb4 = st.sb("sgb4", [4, 128], F32)
        tr.dma("sp", [], [sgb4.c()], lambda e: e.dma_start(out=sgb4.t()[:], in_=sg_b[l]))
        P([sgb4.c(), ("ctb",)], [PB.c(0)], lambda e: e.matmul(
            PB.t(0)[:, 0:4], lhsT=sgb4.t()[0:4, :], rhs=ctb[0:4, C_I4:C_I4 + 4], start=True, stop=True))
        V([PB.c(0)], [sgbt.c()], lambda e: e.tensor_copy(out=sgbt.t()[:], in_=PB.t(0)[:, 0:4]))
        q_t = st.sb("q", [128, D], BF16, 2)
        k_t = st.sb("k", [128, D], BF16, 2)
        v_t = st.sb("v", [128, 8, 128], BF16, 2)
        g_t = st.sb("g", [128, D], BF16, 2)
        su_t = st.sb("su", [128, D], BF16, 2)
        sv_t = st.sb("sv", [128, D], BF16, 2)
        sf_t = st.sb("sf", [128, 8, 128], BF16, 2)
        sb_t = st.sb("sbb", [128, 8, 128], BF16, 2)
        qT = st.sb("qT", [128, 8, 128], BF16, 2)
        kT = st.sb("kT", [128, 8, 128], BF16, 2)
        qTf = st.sb("qTf", [128, 8, 128], BF16, 2)
        qTb = st.sb("qTb", [128, 8, 128], BF16, 2)
        Pm = st.sb("Pm", [128, 8, 128], BF16, 2)
        ro = st.sb("ro", [128, 8, 128], F32, 2)
        rsq = st.sb("rsq", [128, 8, 128], F32, 2)
        gst = st.sb("gst", [128, 32], F32, 2)
        yr = st.sb("yr", [128, D], BF16, 2)
        ys = st.sb("ys", [128, D], BF16, 2)
        for i in range(NT):
            for (buf, src) in ((q_t, RQ), (k_t, RK), (g_t, RG), (su_t, SU), (sv_t, SV)):
                LD([], [buf.c(i)], lambda e, buf=buf, src=src, i=i: e.dma_start(out=buf.t(i)[:], in_=rows(src, i)))
            LD([], [v_t.c(i)], lambda e, i=i: e.dma_start(out=v_t.t(i)[:].rearrange("p a b -> p (a b)"), in_=rows(RV, i)))
            LD([], [sf_t.c(i)], lambda e, i=i: e.dma_start(out=sf_t.t(i)[:].rearrange("p a b -> p (a b)"), in_=rows(SF, i)))
            LD([], [sb_t.c(i)], lambda e, i=i: e.dma_start(out=sb_t.t(i)[:].rearrange("p a b -> p (a b)"), in_=rows(SB, i)))
            pq = next_pt()
            transposes(lambda j: q_t.t(i)[:, j * 128:(j + 1) * 128], 8, pq, [q_t.c(i)])
            ptq = PT.t(pq)[:].rearrange("p (a b) -> p a b", a=8)
            A([PT.c(pq)], [qT.c(i)], lambda e, ptq=ptq, i=i: e.copy(out=qT.t(i)[:], in_=ptq))
            V([PT.c(pq), XIF.c()], [qTf.c(i)], lambda e, ptq=ptq, i=i: e.tensor_tensor(
                out=qTf.t(i)[:], in0=ptq, in1=XIF.t()[:], op=ALU.mult))
            V([PT.c(pq), XIB.c()], [qTb.c(i)], lambda e, ptq=ptq, i=i: e.tensor_tensor(
                out=qTb.t(i)[:], in0=ptq, in1=XIB.t()[:], op=ALU.mult))
            to_T(k_t, i, 8, kT, i, copy_eng="act")
            p = next_pa()

            def fa(e, p=p, i=i):
                ins = None
                for h in range(8):
                    ins = e.matmul(PA.t(p)[:, h * 128:(h + 1) * 128], lhsT=kT.t(i)[:, h, :], rhs=qT.t(i)[:, h, :],
                                   start=True, stop=True)
                return ins
            P([kT.c(i), qT.c(i)], [PA.c(p, 0), PA.c(p, 1)], fa)
            V([PA.c(p, 0), PA.c(p, 1), DTb.c()], [Pm.c(i)], lambda e, p=p, i=i: e.tensor_tensor(
                out=Pm.t(i)[:], in0=PA.t(p)[:].rearrange("p (a b) -> p a b", a=8), in1=DTb.t()[:], op=ALU.mult))
            p2 = next_pa()

            def fo(e, p2=p2, i=i):
                ins = None
                for h in range(8):
                    o = PA.t(p2)[:, h * 128:(h + 1) * 128]
                    e.matmul(o, lhsT=Pm.t(i)[:, h, :], rhs=v_t.t(i)[:, h, :], start=True, stop=False)
                    e.matmul(o, lhsT=qTf.t(i)[:, h, :], rhs=sf_t.t(i)[:, h, :], start=False, stop=False)
                    ins = e.matmul(o, lhsT=qTb.t(i)[:, h, :], rhs=sb_t.t(i)[:, h, :], start=False, stop=True)
                return ins
            P([Pm.c(i), v_t.c(i), qTf.c(i), qTb.c(i), sf_t.c(i), sb_t.c(i)], [PA.c(p2, 0), PA.c(p2, 1)], fo)
            A([PA.c(p2, 0), PA.c(p2, 1)], [ro.c(i)], lambda e, p2=p2, i=i: e.copy(
                out=ro.t(i)[:].rearrange("p a b -> p (a b)"), in_=PA.t(p2)[:]))
            gs_ = gst.t(i)
            V([ro.c(i)], [gst.c(i, 0)], lambda e, i=i, gs_=gs_: e.tensor_reduce(
                out=gs_[:, 0:8], in_=ro.t(i)[:], axis=AX.X, op=ALU.add))
            G([ro.c(i)], [rsq.c(i)], lambda e, i=i: e.tensor_tensor(out=rsq.t(i)[:], in0=ro.t(i)[:], in1=ro.t(i)[:], op=ALU.mult))
            V([rsq.c(i)], [gst.c(i, 1)], lambda e, i=i, gs_=gs_: e.tensor_reduce(
                out=gs_[:, 8:16], in_=rsq.t(i)[:], axis=AX.X, op=ALU.add))
            V([gst.c(i, 0)], [gst.c(i, 0)], lambda e, gs_=gs_: e.tensor_scalar(
                out=gs_[:, 0:8], in0=gs_[:, 0:8], scalar1=1.0 / 128.0, scalar2=None, op0=ALU.mult))
            V([gst.c(i, 0)], [gst.c(i, 2)], lambda e, gs_=gs_: e.tensor_tensor(
                out=gs_[:, 16:24], in0=gs_[:, 0:8], in1=gs_[:, 0:8], op=ALU.mult))
            V([gst.c(i, 1), gst.c(i, 2)], [gst.c(i, 1)], lambda e, gs_=gs_: e.scalar_tensor_tensor(
                out=gs_[:, 8:16], in0=gs_[:, 8:16], scalar=1.0 / 128.0, in1=gs_[:, 16:24], op0=ALU.mult, op1=ALU.subtract))
            rsqrt(gs_[:, 8:16], [gst.c(i, 1)], 1.0, LN_EPS)
            V([ro.c(i), gst.c(i, 0)], [ro.c(i)], lambda e, i=i, gs_=gs_: e.tensor_tensor(
                out=ro.t(i)[:], in0=ro.t(i)[:], in1=gs_[:, 0:8].unsqueeze(2).to_broadcast([128, 8, 128]), op=ALU.subtract))
            V([ro.c(i), gst.c(i, 1)], [ro.c(i)], lambda e, i=i, gs_=gs_: e.tensor_tensor(
                out=ro.t(i)[:], in0=ro.t(i)[:], in1=gs_[:, 8:16].unsqueeze(2).to_broadcast([128, 8, 128]), op=ALU.mult))
            G([ro.c(i), gnw.c()], [ro.c(i)], lambda e, i=i: e.tensor_tensor(
                out=ro.t(i)[:].rearrange("p a b -> p (a b)"), in0=ro.t(i)[:].rearrange("p a b -> p (a b)"),
                in1=gnw.t()[:], op=ALU.mult))
            G([ro.c(i), g_t.c(i)], [yr.c(i)], lambda e, i=i: e.tensor_tensor(
                out=yr.t(i)[:], in0=ro.t(i)[:].rearrange("p a b -> p (a b)"), in1=g_t.t(i)[:], op=ALU.mult))
            STO([yr.c(i)], [], lambda e, i=i: e.dma_start(out=rows(YR, i), in_=yr.t(i)[:]))
            p3 = next_pa()

            def fs(e, p3=p3, i=i):
                ins = None
                for g in range(4):
                    ins = e.matmul(PA.t(p3)[:, g * 256:(g + 1) * 256], lhsT=wsT.t()[:, g, :],
                                   rhs=sv_t.t(i)[:, g * 256:(g + 1) * 256], start=True, stop=True)
                return ins
            P([wsT.c(), sv_t.c(i)], [PA.c(p3, 0), PA.c(p3, 1)], fs)

            def fy(e, p3=p3, i=i):
                ins = None
                for g in range(4):
                    ins = e.scalar_tensor_tensor(out=ys.t(i)[:, g * 256:(g + 1) * 256], in0=PA.t(p3)[:, g * 256:(g + 1) * 256],
                                                 scalar=sgbt.t()[:, g:g + 1], in1=su_t.t(i)[:, g * 256:(g + 1) * 256],
                                                 op0=ALU.add, op1=ALU.mult)
                return ins
            V([PA.c(p3, 0), PA.c(p3, 1), sgbt.c(), su_t.c(i)], [ys.c(i)], fy)
            STO([ys.c(i)], [], lambda e, i=i: e.dma_start(out=rows(YS, i), in_=ys.t(i)[:]))
        st.close()
        if stop_after is not None and stop_after == (lt + "r"):
            tr.barrier()
            return nc, tr

        for s in range(NSEG):
            tiles = cfg.seg_tiles(s)
            is_samp = (s == cfg.NP)
            nkb = (cfg.DSEQ // 128) if is_samp else cfg.TP
            st = Stage(nc, tr, lt + "t%d" % s)
            KT = st.sb("KT", [128, 2, nkb * 128], BF16)
            Vt = st.sb("Vt", [128, nkb, 256], BF16)
            kld = st.sb("kld", [128, 256], BF16, 3)
            if is_samp:
                ksrc = CCout_bf
                koff = 0
            else:
                ksrc = AKV
                koff = tiles[0] * 128
            for kb in range(nkb):
                r0 = koff + kb * 128
                if is_samp:
                    r0 = ((kb * 128) // cfg.SSEG) * CR + (kb * 128) % cfg.SSEG
                LD([], [Vt.c(0, kb)], lambda e, kb=kb, r0=r0: e.dma_start(out=Vt.t()[:, kb, :], in_=ksrc[r0:r0 + 128, 256:512]))
                LD([], [kld.c(kb)], lambda e, kb=kb, r0=r0: e.dma_start(out=kld.t(kb)[:], in_=ksrc[r0:r0 + 128, 0:256]))
                pq = next_pt()
                transposes(lambda j: kld.t(kb)[:, j * 128:(j + 1) * 128], 2, pq, [kld.c(kb)])
                V([PT.c(pq)], [KT.c(0, kb)], lambda e, kb=kb, pq=pq: e.tensor_copy(
                    out=KT.t()[:, :, kb * 128:(kb + 1) * 128], in_=PT.t(pq)[:, 0:256].rearrange("p (a b) -> p a b", a=2)))
            aq_t = st.sb("aq", [128, D], BF16, 3)
            qTg = st.sb("qTg", [128, 8, 512], BF16, 2)
            PTs = st.sb("PTs", [128, 512], BF16, 4)
            accv = st.sb("accv", [128, 512], F32, 2)
            accg = st.sb("accg", [128, 512], F32, 2)
            rden = st.sb("rden", [128, 512], F32, 2)
            yat = st.sb("yat", [128, 4, 8, 128], BF16, 2)
            ngrp = (len(tiles) + 3) // 4
            LA = 2
            pending = []
            jc = [0]
            hc = [0]
            SSL = [(0, 0), (0, 1), (1, 0)]

            def q_prep(gi, gt):
                for ti, i in enumerate(gt):
                    LD([], [aq_t.c(i)], lambda e, i=i: e.dma_start(out=aq_t.t(i)[:], in_=rows(AQ, i)))
                    pq = next_pt()
                    transposes(lambda j: aq_t.t(i)[:, j * 128:(j + 1) * 128], 8, pq, [aq_t.c(i)])
                    V([PT.c(pq)], [qTg.c(gi, ti)], lambda e, pq=pq, ti=ti, gi=gi: e.tensor_copy(
                        out=qTg.t(gi)[:, :, ti * 128:(ti + 1) * 128], in_=PT.t(pq)[:].rearrange("p (a b) -> p a b", a=8)))

            def rest(gi, gt, h, kb, j, hh):
                nq = len(gt) * 128
                g = h // 4
                pi, half = SSL[j % 3]
                sps = PA.t(pi)[:, half * 512:half * 512 + nq]
                A([PA.c(pi, half), negM_c], [PTs.c(j)], lambda e: e.activation(
                    out=PTs.t(j)[:, 0:nq], in_=sps, func=AF.Exp, bias=negM, scale=128.0 ** -0.5))
                P([PTs.c(j), Vt.c(0, kb)], [PB.c(hh)], lambda e: e.matmul(
                    PB.t(hh)[:, 0:nq], lhsT=Vt.t()[:, kb, g * 128:(g + 1) * 128], rhs=PTs.t(j)[:, 0:nq],
                    start=(kb == 0), stop=(kb == nkb - 1)))
                on_pool = (kb % 3 == 2)
                acc = accg if on_pool else accv
                first = (kb == 2) if on_pool else (kb == 0)
                opf = G if on_pool else V
                if first:
                    opf([PTs.c(j)], [acc.c(hh)], lambda e: e.tensor_copy(out=acc.t(hh)[:, 0:nq], in_=PTs.t(j)[:, 0:nq]))
                else:
                    opf([PTs.c(j), acc.c(hh)], [acc.c(hh)], lambda e: e.tensor_tensor(
                        out=acc.t(hh)[:, 0:nq], in0=acc.t(hh)[:, 0:nq], in1=PTs.t(j)[:, 0:nq], op=ALU.add))
                if kb == nkb - 1:
                    two = nkb > 2
                    dps = PA.t(1)[:, 512:512 + nq]

                    def fden(e):
                        ins = e.matmul(dps, lhsT=ones32[:], rhs=accv.t(hh)[:, 0:nq], start=True, stop=not two)
                        if two:
                            ins = e.matmul(dps, lhsT=ones32[:], rhs=accg.t(hh)[:, 0:nq], start=False, stop=True)
                        return ins
                    P([accv.c(hh), accg.c(hh), ("ones32",)], [PA.c(1, 1)], fden)
                    V([PA.c(1, 1)], [rden.c(hh)], lambda e: e.reciprocal(out=rden.t(hh)[:, 0:nq], in_=dps))
                    V([PB.c(hh), rden.c(hh)], [yat.c(gi, h)], lambda e: e.tensor_tensor(
                        out=yat.t(gi)[:, 0:len(gt), h, :], in0=PB.t(hh)[:, 0:nq].rearrange("p (t q) -> p t q", q=128),
                        in1=rden.t(hh)[:, 0:nq].rearrange("p (t q) -> p t q", q=128), op=ALU.mult))
                    if h == 7:
                        for ti, i in enumerate(gt):
                            STO([yat.c(gi, hx) for hx in range(8)], [], lambda e, ti=ti, i=i: e.dma_start(
                                out=rows(YAT, i), in_=yat.t(gi)[:, ti, :, :].rearrange("p a b -> p (a b)")))

            for gi in range(ngrp):
                gt = tiles[gi * 4:(gi + 1) * 4]
                nq = len(gt) * 128
                q_prep(gi, gt)
                qcells = [qTg.c(gi, ti) for ti in range(len(gt))]
                for h in range(8):
                    g = h // 4
                    hh = hc[0]; hc[0] += 1
                    for kb in range(nkb):
                        j = jc[0]; jc[0] += 1
                        pi, half = SSL[j % 3]
                        sps = PA.t(pi)[:, half * 512:half * 512 + nq]
                        P([KT.c(0, kb)] + qcells, [PA.c(pi, half)], lambda e, kb=kb, g=g, h=h, sps=sps, gi=gi, nq=nq: e.matmul(
                            sps, lhsT=KT.t()[:, g, kb * 128:(kb + 1) * 128], rhs=qTg.t(gi)[:, h, 0:nq], start=True, stop=True))
                        pending.append((gi, gt, h, kb, j, hh))
                        if len(pending) > LA:
                            rest(*pending.pop(0))
            while pending:
                rest(*pending.pop(0))
            st.close()
            if stop_after is not None and stop_after == (lt + "t%d" % s):
                tr.barrier()
                return nc, tr

        st = Stage(nc, tr, lt + "c")
        wro = load_w(st, "wro", ret_wo[l], 8, D)
        wso = load_w(st, "wso", sg_wo[l], 8, D)
        wao = load_w(st, "wao", att_wo[l], 8, D)
        wout = load_w(st, "wout", w_out[l], 8, D)
        lw0 = load_bcast(st, "lw0", ln_w[l, 0:1, :], D)
        lb0 = load_bcast(st, "lb0", ln_b[l, 0:1, :], D)
        yr_t = st.sb("yr", [128, D], BF16, 2)
        ys_t = st.sb("ys", [128, D], BF16, 2)
        yaT = st.sb("yaT", [128, 8, 128], BF16, 2)
        gt_t = st.sb("gt", [128, 3 * D], BF16, 2)
        xr = st.sb("xr", [128, D], F32, 2)
        yrT = st.sb("yrT", [128, 8, 128], BF16, 2)
        ysT = st.sb("ysT", [128, 8, 128], BF16, 2)
        mg = st.sb("mg", [128, D], F32, 2)
        tmpm = st.sb("tmpm", [128, D], F32, 2)
        mgb = st.sb("mgb", [128, D], BF16, 2)
        mT = st.sb("mT", [128, 8, 128], BF16, 2)
        x1 = st.sb("x1", [128, D], F32, 2)
        lbf = ln_bufs(st, "ln")
        for i in range(NT):
            LD([], [yr_t.c(i)], lambda e, i=i: e.dma_start(out=yr_t.t(i)[:], in_=rows(YR, i)))
            LD([], [ys_t.c(i)], lambda e, i=i: e.dma_start(out=ys_t.t(i)[:], in_=rows(YS, i)))
            LD([], [yaT.c(i)], lambda e, i=i: e.dma_start(out=yaT.t(i)[:].rearrange("p a b -> p (a b)"), in_=rows(YAT, i)))
            LD([], [gt_t.c(i)], lambda e, i=i: e.dma_start(out=gt_t.t(i)[:], in_=rows(GT, i)))
            LD([], [xr.c(i)], lambda e, i=i: e.dma_start(out=xr.t(i)[:], in_=rows(X, i)))
            to_T(yr_t, i, 8, yrT, i, copy_eng="act")
            to_T(ys_t, i, 8, ysT, i, copy_eng="dve")
            for bi, (srcT, wbuf) in enumerate(((yrT, wro), (ysT, wso), (yaT, wao))):
                p = next_pa()
                for cbk in range(2):
                    gemm(lambda k: srcT.t(i)[:, k, :], 8, wbuf, cbk * 512, 512,
                         PA.t(p)[:, cbk * 512:(cbk + 1) * 512], [srcT.c(i)], [PA.c(p, cbk)])
                gsl = gt_t.t(i)[:, bi * 1024:(bi + 1) * 1024]
                if bi == 0:
                    V([PA.c(p, 0), PA.c(p, 1), gt_t.c(i)], [mg.c(i)], lambda e, p=p, gsl=gsl, i=i: e.tensor_tensor(
                        out=mg.t(i)[:], in0=PA.t(p)[:], in1=gsl, op=ALU.mult))
                else:
                    V([PA.c(p, 0), PA.c(p, 1), gt_t.c(i)], [tmpm.c(i)], lambda e, p=p, gsl=gsl, i=i: e.tensor_tensor(
                        out=tmpm.t(i)[:], in0=PA.t(p)[:], in1=gsl, op=ALU.mult))
                    if bi == 1:
                        G([mg.c(i), tmpm.c(i)], [mg.c(i)], lambda e, i=i: e.tensor_tensor(
                            out=mg.t(i)[:], in0=mg.t(i)[:], in1=tmpm.t(i)[:], op=ALU.add))
                    else:
                        G([mg.c(i), tmpm.c(i)], [mgb.c(i)], lambda e, i=i: e.tensor_tensor(
                            out=mgb.t(i)[:], in0=mg.t(i)[:], in1=tmpm.t(i)[:], op=ALU.add))
            to_T(mgb, i, 8, mT, i, copy_eng="act")
            p = next_pa()
            for cbk in range(2):
                gemm(lambda k: mT.t(i)[:, k, :], 8, wout, cbk * 512, 512,
                     PA.t(p)[:, cbk * 512:(cbk + 1) * 512], [mT.c(i)], [PA.c(p, cbk)])
            V([PA.c(p, 0), PA.c(p, 1), xr.c(i)], [x1.c(i)], lambda e, p=p, i=i: e.scalar_tensor_tensor(
                out=x1.t(i)[:], in0=xr.t(i)[:], scalar=ALPHA, in1=PA.t(p)[:], op0=ALU.mult, op1=ALU.add))
            layer_norm(x1.t(i)[:], [x1.c(i)], lbf, i, lw0, lb0, x1.t(i)[:], [x1.c(i)])
            STO([x1.c(i)], [], lambda e, i=i: e.dma_start(out=rows(X1, i), in_=x1.t(i)[:]))
        st.close()
        if stop_after is not None and stop_after == (lt + "c"):
            tr.barrier()
            return nc, tr

        st = Stage(nc, tr, lt + "x")
        wxq = load_w(st, "wxq", xa_wq[l], 8, D)
        wxo = load_w(st, "wxo", xa_wo[l], 8, D)
        lw1 = load_bcast(st, "lw1", ln_w[l, 1:2, :], D)
        lb1 = load_bcast(st, "lb1", ln_b[l, 1:2, :], D)
        x1 = st.sb("x1", [128, D], F32, 2)
        x1b = st.sb("x1b", [128, D], BF16, 2)
        x1T = st.sb("x1T", [128, 8, 128], BF16, 2)
        qxT = st.sb("qxT", [128, 8, 128], BF16, 2)
        mk_t = st.sb("mk", [128, 2, 8, 128], BF16, 2)
        mv_t = st.sb("mvv", [128, 2, D], BF16, 2)
        pex = st.sb("pex", [128, 8, 128], BF16, 2)
        rdx = st.sb("rdx", [128, 4, 128], F32, 2)
        oT = st.sb("oT", [128, 8, 128], BF16, 2)
        x2 = st.sb("x2", [128, D], F32, 2)
        lbf = ln_bufs(st, "ln")
        cur_seg = [-1]
        segc = [0]
        for i in range(NT):
            s = cfg.tile_seg(i)
            if s != cur_seg[0]:
                cur_seg[0] = s
                segc[0] += 1
                sc = segc[0]
                for kb in range(2):
                    LD([], [mk_t.c(sc, kb)], lambda e, kb=kb, sc=sc, s=s: e.dma_start(
                        out=mk_t.t(sc)[:, kb, :, :].rearrange("p a b -> p (a b)"), in_=rows(MK, s * 2 + kb)))
                    LD([], [mv_t.c(sc, kb)], lambda e, kb=kb, sc=sc, s=s: e.dma_start(out=mv_t.t(sc)[:, kb, :], in_=rows(MV, s * 2 + kb)))
            sc = segc[0]
            LD([], [x1.c(i)], lambda e, i=i: e.dma_start(out=x1.t(i)[:], in_=rows(X1, i)))
            A([x1.c(i)], [x1b.c(i)], lambda e, i=i: e.copy(out=x1b.t(i)[:], in_=x1.t(i)[:]))
            to_T(x1b, i, 8, x1T, i, copy_eng="dve")
            p = next_pa()

            def fq(e, p=p, i=i):
                ins = None
                for j in range(8):
                    for k in range(8):
                        ins = e.matmul(PA.t(p)[:, j * 128:(j + 1) * 128], lhsT=wxq.t()[:, k, j * 128:(j + 1) * 128],
                                       rhs=x1T.t(i)[:, k, :], start=(k == 0), stop=(k == 7))
                return ins
            P([wxq.c(), x1T.c(i)], [PA.c(p, 0), PA.c(p, 1)], fq)
            A([PA.c(p, 0), PA.c(p, 1)], [qxT.c(i)], lambda e, p=p, i=i: e.copy(
                out=qxT.t(i)[:].rearrange("p a b -> p (a b)"), in_=PA.t(p)[:]))
            p = next_pa()

            def fsx(e, p=p, i=i, sc=sc):
                ins = None
                for h in range(4):
                    for kb in range(2):
                        o = PA.t(p)[:, (h * 2 + kb) * 128:(h * 2 + kb + 1) * 128]
                        for c2 in range(2):
                            ins = e.matmul(o, lhsT=mk_t.t(sc)[:, kb, h * 2 + c2, :], rhs=qxT.t(i)[:, h * 2 + c2, :],
                                           start=(c2 == 0), stop=(c2 == 1))
                return ins
            P([mk_t.c(sc, 0), mk_t.c(sc, 1), qxT.c(i)], [PA.c(p, 0), PA.c(p, 1)], fsx)
            A([PA.c(p, 0), PA.c(p, 1)], [pex.c(i)], lambda e, p=p, i=i: e.activation(
                out=pex.t(i)[:].rearrange("p a b -> p (a b)"), in_=PA.t(p)[:], func=AF.Exp, scale=256.0 ** -0.5))
            pbi = i % 2

            def fden(e, i=i, pbi=pbi):
                ins = None
                for h in range(4):
                    for kb in range(2):
                        ins = e.matmul(PB.t(pbi)[:, h * 128:(h + 1) * 128], lhsT=ones[:], rhs=pex.t(i)[:, h * 2 + kb, :],
                                       start=(kb == 0), stop=(kb == 1))
                return ins
            P([pex.c(i), ("ones",)], [PB.c(pbi)], fden)
            p = next_pa()

            def fo2(e, p=p, i=i, sc=sc):
                ins = None
                for h in range(4):
                    for c2 in range(2):
                        o = PA.t(p)[:, (h * 2 + c2) * 128:(h * 2 + c2 + 1) * 128]
                        for kb in range(2):
                            ins = e.matmul(o, lhsT=mv_t.t(sc)[:, kb, (h * 2 + c2) * 128:(h * 2 + c2 + 1) * 128],
                                           rhs=pex.t(i)[:, h * 2 + kb, :], start=(kb == 0), stop=(kb == 1))
                return ins
            P([pex.c(i), mv_t.c(sc, 0), mv_t.c(sc, 1)], [PA.c(p, 0), PA.c(p, 1)], fo2)
            V([PB.c(pbi)], [rdx.c(i)], lambda e, i=i, pbi=pbi: e.reciprocal(
                out=rdx.t(i)[:].rearrange("p a b -> p (a b)"), in_=PB.t(pbi)[:]))
            V([PA.c(p, 0), PA.c(p, 1), rdx.c(i)], [oT.c(i)], lambda e, p=p, i=i: e.tensor_tensor(
                out=oT.t(i)[:].rearrange("p (h c) q -> p h c q", c=2),
                in0=PA.t(p)[:].rearrange("p (h c q) -> p h c q", h=4, c=2),
                in1=rdx.t(i)[:].unsqueeze(2).to_broadcast([128, 4, 2, 128]), op=ALU.mult))
            p = next_pa()
            for cbk in range(2):
                gemm(lambda k: oT.t(i)[:, k, :], 8, wxo, cbk * 512, 512,
                     PA.t(p)[:, cbk * 512:(cbk + 1) * 512], [oT.c(i)], [PA.c(p, cbk)])
            V([PA.c(p, 0), PA.c(p, 1), x1.c(i)], [x2.c(i)], lambda e, p=p, i=i: e.scalar_tensor_tensor(
                out=x2.t(i)[:], in0=x1.t(i)[:], scalar=ALPHA, in1=PA.t(p)[:], op0=ALU.mult, op1=ALU.add))
            layer_norm(x2.t(i)[:], [x2.c(i)], lbf, i, lw1, lb1, x2.t(i)[:], [x2.c(i)])
            STO([x2.c(i)], [], lambda e, i=i: e.dma_start(out=rows(X2, i), in_=x2.t(i)[:]))
        st.close()
        if stop_after is not None and stop_after == (lt + "x"):
            tr.barrier()
            return nc, tr

        st = Stage(nc, tr, lt + "f")
        wfi = load_w(st, "wfi", ffn_w_in[l], 8, 2 * D_FF)
        wfo = load_w(st, "wfo", ffn_w_out[l], 22, D)
        lw2 = load_bcast(st, "lw2", ln_w[l, 2:3, :], D)
        lb2 = load_bcast(st, "lb2", ln_b[l, 2:3, :], D)
        xin = st.sb("xin", [128, D], F32, 2)
        xb = st.sb("xb", [128, D], BF16, 2)
        xT = st.sb("xT", [128, 8, 128], BF16, 2)
        sa = st.sb("sa", [128, 512], F32, 3)
        act = st.sb("act", [128, D_FF], BF16, 1)
        actT = st.sb("actT", [128, 22, 128], BF16, 1)
        x3 = st.sb("x3", [128, D], F32, 2)
        lbf = ln_bufs(st, "ln")
        Xdst = X if l < L - 1 else y_out
        cc_ = [0]
        for i in range(NT):
            load_xT(st, X2, i, xin, xb, xT)
            blocks = [(c0, 512) for c0 in range(0, 2560, 512)] + [(2560, 256)]
            for (c0, w_) in blocks:
                p = next_pa()
                gemm(lambda k: xT.t(i)[:, k, :], 8, wfi, c0, w_, PA.t(p)[:, 0:w_], [xT.c(i)], [PA.c(p, 0)])
                gemm(lambda k: xT.t(i)[:, k, :], 8, wfi, D_FF + c0, w_, PA.t(p)[:, 512:512 + w_], [xT.c(i)], [PA.c(p, 1)])
                j = cc_[0]; cc_[0] += 1
                A([PA.c(p, 0)], [sa.c(j)], lambda e, p=p, j=j, w_=w_: e.activation(
                    out=sa.t(j)[:, 0:w_], in_=PA.t(p)[:, 0:w_], func=AF.Silu))
                V([sa.c(j), PA.c(p, 1)], [act.c(i, c0)], lambda e, p=p, j=j, w_=w_, c0=c0, i=i: e.tensor_tensor(
                    out=act.t(i)[:, c0:c0 + w_], in0=sa.t(j)[:, 0:w_], in1=PA.t(p)[:, 512:512 + w_], op=ALU.mult))
            acells = [act.c(i, c0) for (c0, _) in blocks]
            for r in range(3):
                nb = min(8, 22 - r * 8)
                pq = next_pt()
                transposes(lambda j, r=r: act.t(i)[:, (r * 8 + j) * 128:(r * 8 + j + 1) * 128], nb, pq, acells)
                eng = "act" if r % 2 == 0 else "dve"
                if eng == "act":
                    A([PT.c(pq)], [actT.c(i, r)], lambda e, pq=pq, r=r, nb=nb, i=i: e.copy(
                        out=actT.t(i)[:, r * 8:r * 8 + nb, :].rearrange("p a b -> p (a b)"), in_=PT.t(pq)[:, 0:nb * 128]))
                else:
                    V([PT.c(pq)], [actT.c(i, r)], lambda e, pq=pq, r=r, nb=nb, i=i: e.tensor_copy(
                        out=actT.t(i)[:, r * 8:r * 8 + nb, :].rearrange("p a b -> p (a b)"), in_=PT.t(pq)[:, 0:nb * 128]))
            p = next_pa()
            for cbk in range(2):
                gemm(lambda k: actT.t(i)[:, k, :], 22, wfo, cbk * 512, 512,
                     PA.t(p)[:, cbk * 512:(cbk + 1) * 512], [actT.c(i, r) for r in range(3)], [PA.c(p, cbk)])
            V([PA.c(p, 0), PA.c(p, 1), xin.c(i)], [x3.c(i)], lambda e, p=p, i=i: e.scalar_tensor_tensor(
                out=x3.t(i)[:], in0=xin.t(i)[:], scalar=ALPHA, in1=PA.t(p)[:], op0=ALU.mult, op1=ALU.add))
            layer_norm(x3.t(i)[:], [x3.c(i)], lbf, i, lw2, lb2, x3.t(i)[:], [x3.c(i)])
            STO([x3.c(i)], [], lambda e, i=i: e.dma_start(out=rows(Xdst, i), in_=x3.t(i)[:]))
        st.close()
        if stop_after is not None and stop_after == (lt + "f"):
            tr.barrier()
            return nc, tr
        ls.close()

    tr.barrier()
    gs.close()
    return nc, tr


def _rope_tables(cfg, core):
    def cs(pos, dim):
        inv = (ROPE_BASE ** (-np.arange(0, dim, 2, dtype=np.float32) / np.float32(dim))).astype(np.float32)
        ang = pos.astype(np.float32)[:, None] * inv[None, :]
        return np.cos(ang).astype(np.float32), np.sin(ang).astype(np.float32)
    pos = np.concatenate([np.arange(cfg.SEQ), core * cfg.SSEG + np.arange(cfg.SSEG)]).astype(np.int64)
    c, s = cs(pos, 128)
    C = np.concatenate([c, c], 1)
    S = np.concatenate([-s, s], 1)
    sc = np.float32(128.0 ** -0.5)
    tab_r = np.concatenate([C, S, C * sc, S * sc], 1).astype(np.float32)
    cr, sr = cs(pos // GRID_W, 64)
    cc, s2 = cs(pos % GRID_W, 64)
    Ca = np.concatenate([cr, cr, cc, cc], 1)
    Sa = np.concatenate([-sr, sr, -s2, s2], 1)
    tab_a = np.concatenate([Ca, Sa], 1).astype(np.float32)
    return tab_r, tab_a


def _ctab():
    m = np.arange(128, dtype=np.float32)[:, None]
    c = np.arange(128, dtype=np.float32)[None, :]
    t = np.zeros((128, 1024), np.float32)
    t[:, 0:128] = np.maximum(c - m, 0)
    t[:, 128:256] = (c >= m)
    t[:, 256:384] = np.maximum(m - c, 0)
    t[:, 384:512] = (m > c)
    t[:, 512:640] = c + 1
    t[:, 640:768] = 128 - c
    t[:, 768] = 127 - m[:, 0]
    t[:, 769] = m[:, 0]
    for j in range(4):
        t[j, 772 + j] = 1.0
    return t


def _cctab(cfg, core):
    t = np.zeros((128, 32), np.float32)
    for c2 in range(cfg.NC):
        if c2 < core:
            t[:, c2] = cfg.SSEG * (core - 1 - c2)
            t[:, 16 + c2] = 1.0
        if c2 > core:
            t[:, 8 + c2] = cfg.SSEG * (c2 - core - 1)
            t[:, 24 + c2] = 1.0
    return t


def make_in_maps(cfg, inp):
    f = lambda a: np.ascontiguousarray(np.asarray(a, dtype=np.float32))
    shared = {
        "in_ln_w": f(inp["in_ln_w"]).reshape(1, D), "in_ln_b": f(inp["in_ln_b"]).reshape(1, D),
        "w_in": f(inp["w_in"]), "b_gate": f(inp["b_gate"]),
        "ret_decay": f(np.concatenate([inp["ret_decay_f"], inp["ret_decay_b"]], axis=1)),
        "ret_gn_w": f(inp["ret_gn_w"]), "ret_wo": f(inp["ret_wo"]),
        "sg_ln_w": f(inp["sg_ln_w"]), "sg_ln_b": f(inp["sg_ln_b"]), "sg_ws": f(inp["sg_ws"]), "sg_b": f(inp["sg_b"]),
        "sg_wo": f(inp["sg_wo"]), "att_qn_w": f(inp["att_qn_w"]), "att_kn_w": f(inp["att_kn_w"]),
        "att_wo": f(inp["att_wo"]), "w_out": f(inp["w_out"]), "ln_w": f(inp["ln_w"]), "ln_b": f(inp["ln_b"]),
        "xa_wq": f(inp["xa_wq"]), "xa_wkv": f(inp["xa_wkv"]), "xa_wo": f(inp["xa_wo"]),
        "ffn_w_in": f(inp["ffn_w_in"]), "ffn_w_out": f(inp["ffn_w_out"]),
        "ctab": _ctab(), "ident": np.eye(128, dtype=np.float32),
    }
    xp, xs = f(inp["x_prompt"]), f(inp["x_sample"])
    mp, ms = f(inp["mem_prompt"]), f(inp["mem_sample"])
    maps = []
    for c in range(cfg.NC):
        m = dict(shared)
        xpc = xp[c * cfg.NP:(c + 1) * cfg.NP].reshape(cfg.NP * cfg.SEQ, D)
        xsc = xs[0, c * cfg.SSEG:(c + 1) * cfg.SSEG]
        m["x"] = np.ascontiguousarray(np.concatenate([xpc, xsc], 0))
        m["mem"] = np.ascontiguousarray(np.concatenate([mp[c * cfg.NP:(c + 1) * cfg.NP].reshape(cfg.NP * N_MEM, D), ms[0]], 0))
        tr_, ta_ = _rope_tables(cfg, c)
        m["tab_r"], m["tab_a"] = tr_, ta_
        m["cctab"] = _cctab(cfg, c)
        maps.append(m)
    return maps


def run(cfg, inp, debug=()):
    nc, tr = build(cfg, debug)
    maps = make_in_maps(cfg, inp)
    res = run_bass_kernel_spmd(nc, maps, core_ids=list(range(cfg.NC)))
    return res


def assemble(cfg, res):
    yp = np.zeros((cfg.NP * cfg.NC, cfg.SEQ, D), np.float32)
    ys = np.zeros((1, cfg.DSEQ, D), np.float32)
    for c in range(cfg.NC):
        y = res.results[c]["y"]
        yp[c * cfg.NP:(c + 1) * cfg.NP] = y[:cfg.NP * cfg.SEQ].reshape(cfg.NP, cfg.SEQ, D)
        ys[0, c * cfg.SSEG:(c + 1) * cfg.SSEG] = y[cfg.NP * cfg.SEQ:]
    return yp, ys


def kernel(**inputs):
    cfg = Cfg()
    res = run(cfg, inputs)
    return assemble(cfg, res)
```

```python
import math
from contextlib import ExitStack

import numpy as np
import concourse.bass as bass
import concourse.mybir as mybir
from concourse.bass_utils import run_bass_kernel_spmd

F32 = mybir.dt.float32
BF16 = mybir.dt.bfloat16
AF = mybir.ActivationFunctionType
ALU = mybir.AluOpType
AX = mybir.AxisListType

D = 1024
N_MEM = 256
GRID_W = 64
D_FF = 2816
D_IN = 10752
LN_EPS = 1e-5
RMS_EPS = 1e-6
ROPE_BASE = 10000.0


class Cfg:
    def __init__(self, nseg_p=4, seq=2048, sseg=2048, depth=4, ncores=8):
        self.NP = nseg_p
        self.SEQ = seq
        self.SSEG = sseg
        self.DEPTH = depth
        self.NC = ncores
        self.TP = seq // 128
        self.TS = sseg // 128
        self.NT = nseg_p * self.TP + self.TS
        self.T = self.NT * 128
        self.DSEQ = sseg * ncores
        self.ALPHA = (2 * depth) ** 0.25
        self.NSEG = nseg_p + 1

    def seg_tiles(self, s):
        if s < self.NP:
            return list(range(s * self.TP, (s + 1) * self.TP))
        return list(range(self.NP * self.TP, self.NT))

    def tile_seg(self, i):
        return min(i // self.TP, self.NP) if self.TP > 0 else self.NP

    def tab_row(self, i):
        if i < self.NP * self.TP:
            return i % self.TP
        return self.TP + (i - self.NP * self.TP)


class Tracker:
    NQ = 12

    def __init__(self, nc):
        self.nc = nc
        self.engs = {"pe": nc.tensor, "act": nc.scalar, "dve": nc.vector, "pool": nc.gpsimd, "sp": nc.sync}
        self.semh = {}
        self.ccnt = {}
        self.opidx = {}
        for e in ("pe", "act", "dve", "pool"):
            self.semh[("c", e)] = nc.alloc_semaphore("c_" + e)
            self.ccnt[e] = 0
            self.opidx[e] = 0
        self.dcnt = {}
        self.drr = {}
        for q in ("sp", "pool"):
            self.dcnt[q] = [0] * self.NQ
            self.drr[q] = 0
            for k in range(self.NQ):
                self.semh[("d", q, k)] = nc.alloc_semaphore("d_%s%d" % (q, k))
        self.semh[("cc",)] = nc.alloc_semaphore("ccsem")
        self.cccnt = 0
        self.waited = {e: {} for e in self.engs}
        self.lastw = {}
        self.readers = {}
        self.ninstr = 0

    def _wait(self, eng, t):
        semkey, val, teng, tidx = t
        if val <= 0:
            return
        if teng is not None and teng == eng:
            if eng == "pe":
                return
        if self.waited[eng].get(semkey, 0) >= val:
            return
        self.engs[eng].wait_ge(self.semh[semkey], val)
        self.ninstr += 1
        self.waited[eng][semkey] = val

    def _deps(self, eng, reads, writes):
        for c in reads:
            t = self.lastw.get(c)
            if t is not None:
                self._wait(eng, t)
        for c in writes:
            t = self.lastw.get(c)
            if t is not None:
                self._wait(eng, t)
            rd = self.readers.get(c)
            if rd:
                for t in rd.values():
                    self._wait(eng, t)

    def _commit(self, t, reads, writes):
        for c in reads:
            self.readers.setdefault(c, {})[t[0]] = t
        for c in writes:
            self.lastw[c] = t
            self.readers[c] = {}

    def op(self, eng, reads, writes, fn):
        ps = [c for c in reads if c[0] in ("PA", "PT", "PB")]
        if ps:
            reads = [c for c in reads if c[0] not in ("PA", "PT", "PB")]
            writes = list(writes) + ps
        self._deps(eng, reads, writes)
        ins = fn(self.engs[eng])
        self.ccnt[eng] += 1
        ins.then_inc(self.semh[("c", eng)], 1)
        t = (("c", eng), self.ccnt[eng], eng, self.opidx[eng])
        self.opidx[eng] += 1
        self.ninstr += 1
        self._commit(t, reads, writes)

    def dma(self, q, reads, writes, fn):
        self._deps(q, reads, writes)
        k = self.drr[q]
        self.drr[q] = (k + 1) % self.NQ
        semkey = ("d", q, k)
        self._wait(q, (semkey, self.dcnt[q][k], None, 0))
        inss = fn(self.engs[q])
        if not isinstance(inss, (list, tuple)):
            inss = [inss]
        for ins in inss:
            ins.then_inc(self.semh[semkey], 16)
        self.dcnt[q][k] += 16 * len(inss)
        self.ninstr += len(inss)
        t = (semkey, self.dcnt[q][k], None, 0)
        self._commit(t, reads, writes)

    def collective(self, reads, writes, fn):
        self._deps("pool", reads, writes)
        ins = fn(self.engs["pool"])
        ins.then_inc(self.semh[("cc",)], 1)
        self.cccnt += 1
        t = (("cc",), self.cccnt, None, 0)
        self._commit(t, reads, writes)

    def barrier(self):
        for e in self.engs:
            for ce in ("pe", "act", "dve", "pool"):
                self._wait(e, (("c", ce), self.ccnt[ce], None, 0))
            for q in ("sp", "pool"):
                for k in range(self.NQ):
                    self._wait(e, (("d", q, k), self.dcnt[q][k], None, 0))
            self._wait(e, (("cc",), self.cccnt, None, 0))
        self.lastw = {}
        self.readers = {}


class Buf:
    def __init__(self, tensors, name):
        self.ts = tensors
        self.name = name
        self.n = len(tensors)

    def t(self, i=0):
        return self.ts[i % self.n]

    def c(self, i=0, sub=0):
        return (self.name, i % self.n, sub)


class Stage:
    def __init__(self, nc, tr, tag):
        self.nc = nc
        self.tr = tr
        self.tag = tag
        self.es = ExitStack()

    def sb(self, name, shape, dtype, nslots=1):
        ts = [self.es.enter_context(self.nc.sbuf_tensor("%s_%s_%d" % (self.tag, name, k), list(shape), dtype))
              for k in range(nslots)]
        return Buf(ts, self.tag + name)

    def close(self):
        self.tr.barrier()
        self.es.close()


def build(cfg, debug=(), stop_after=None):
    nc = bass.Bass("TRN2", target_bir_lowering=False)
    tr = Tracker(nc)
    T, NT, L = cfg.T, cfg.NT, cfg.DEPTH
    NSEG = cfg.NSEG
    ALPHA = cfg.ALPHA

    def din(name, shape, dt=F32):
        return nc.dram_tensor(name, list(shape), dt, kind="ExternalInput").ap()

    def dscr(name, shape, dt):
        kind = "ExternalOutput" if name in debug else "Internal"
        return nc.dram_tensor(name, list(shape), dt, kind=kind).ap()

    x_in = din("x", [T, D])
    mem_in = din("mem", [NSEG * N_MEM, D])
    in_ln_w = din("in_ln_w", [1, D])
    in_ln_b = din("in_ln_b", [1, D])
    w_in = din("w_in", [L, D, D_IN])
    b_gate = din("b_gate", [L, 3 * D])
    ret_decay = din("ret_decay", [L, 16])
    ret_gn_w = din("ret_gn_w", [L, D])
    ret_wo = din("ret_wo", [L, D, D])
    sg_ln_w = din("sg_ln_w", [L, D])
    sg_ln_b = din("sg_ln_b", [L, D])
    sg_ws = din("sg_ws", [L, 4, 128, 128])
    sg_b = din("sg_b", [L, 4, 128])
    sg_wo = din("sg_wo", [L, D, D])
    att_qn_w = din("att_qn_w", [L, 128])
    att_kn_w = din("att_kn_w", [L, 128])
    att_wo = din("att_wo", [L, D, D])
    w_out = din("w_out", [L, D, D])
    ln_w = din("ln_w", [L, 3, D])
    ln_b = din("ln_b", [L, 3, D])
    xa_wq = din("xa_wq", [L, D, D])
    xa_wkv = din("xa_wkv", [L, D, 2 * D])
    xa_wo = din("xa_wo", [L, D, D])
    ffn_w_in = din("ffn_w_in", [L, D, 2 * D_FF])
    ffn_w_out = din("ffn_w_out", [L, D_FF, D])
    NTAB = cfg.TP + cfg.TS
    tab_r = din("tab_r", [NTAB * 128, 512])
    tab_a = din("tab_a", [NTAB * 128, 256])
    ctab = din("ctab", [128, 1024])
    cctab = din("cctab", [128, 32])
    ident_in = din("ident", [128, 128])

    y_out = nc.dram_tensor("y", [T, D], F32, kind="ExternalOutput").ap()

    X = dscr("X", [T, D], F32)
    X1 = dscr("X1", [T, D], F32)
    X2 = dscr("X2", [T, D], F32)
    RQ = dscr("RQ", [T, D], BF16)
    RK = dscr("RK", [T, D], BF16)
    RV = dscr("RV", [T, D], BF16)
    RG = dscr("RG", [T, D], BF16)
    SU = dscr("SU", [T, D], BF16)
    SV = dscr("SV", [T, D], BF16)
    AQ = dscr("AQ", [T, D], BF16)
    AKV = dscr("AKV", [T, 512], BF16)
    GT = dscr("GT", [T, 3 * D], BF16)
    SF = dscr("STF", [NT * 128, D], BF16)
    SB = dscr("STB", [NT * 128, D], BF16)
    YR = dscr("YR", [T, D], BF16)
    YS = dscr("YS", [T, D], BF16)
    YAT = dscr("YAT", [NT * 128, D], BF16)
    MK = dscr("MK", [NSEG * 2 * 128, D], BF16)
    MV = dscr("MV", [NSEG * 2 * 128, D], BF16)
    CR = max(cfg.SSEG, 1024)
    CCin = nc.dram_tensor("CCin", [CR, 256], F32)
    CCout = nc.dram_tensor("CCout", [CR * cfg.NC, 256], F32)
    CCin_bf = CCin.ap().bitcast(BF16)
    CCout_bf = CCout.ap().bitcast(BF16)
    Eloc_v = CCin.ap()[0:1024, :].rearrange("(j q) c -> j (q c)", q=4)

    def Eall_v(c2):
        return CCout.ap()[c2 * CR:c2 * CR + 1024, :].rearrange("(j q) c -> j (q c)", q=4)

    def allgather():
        tr.collective([("ccin",)], [("ccout",)], lambda e: e.collective_compute(
            "AllGather", ALU.bypass, replica_groups=[list(range(cfg.NC))],
            ins=[CCin.ap().opt()], outs=[CCout.ap().opt()]))

    gs = ExitStack()

    def gsb(name, shape, dt):
        return gs.enter_context(nc.sbuf_tensor(name, list(shape), dt))

    ident = gsb("identb", [128, 128], BF16)
    ones = gsb("onesb", [128, 128], BF16)
    ones32 = gsb("ones32", [128, 128], F32)
    ctb = gsb("ctb", [128, 1024], F32)
    cct = gsb("cct", [128, 32], F32)
    pa = [gs.enter_context(nc.psum_tensor("pa%d" % k, [128, 1024], F32)) for k in range(2)]
    pt = [gs.enter_context(nc.psum_tensor("pt%d" % k, [128, 1024], BF16)) for k in range(2)]
    pb = [gs.enter_context(nc.psum_tensor("pb%d" % k, [128, 512], F32)) for k in range(2)]
    PA = Buf(pa, "PA")
    PT = Buf(pt, "PT")
    PB = Buf(pb, "PB")
    C_PF, C_MF, C_PB, C_MB, C_RP1, C_RM = 0, 128, 256, 384, 512, 640
    C_Z = 768
    C_I4 = 772

    tr.dma("pool", [], [("ident",)], lambda e: e.dma_start(out=ident[:], in_=ident_in))
    tr.dma("sp", [], [("ctb",)], lambda e: e.dma_start(out=ctb[:], in_=ctab))
    tr.dma("sp", [], [("cct",)], lambda e: e.dma_start(out=cct[:], in_=cctab))
    tr.op("dve", [], [("ones",)], lambda e: e.memset(ones[:], 1.0))
    tr.op("dve", [], [("ones32",)], lambda e: e.memset(ones32[:], 1.0))
    tr.barrier()

    def V(reads, writes, fn):
        tr.op("dve", reads, writes, fn)

    def A(reads, writes, fn):
        tr.op("act", reads, writes, fn)

    def G(reads, writes, fn):
        tr.op("pool", reads, writes, fn)

    def P(reads, writes, fn):
        tr.op("pe", reads, writes, fn)

    def LD(reads, writes, fn):
        tr.dma("sp", reads, writes, fn)

    def STO(reads, writes, fn):
        tr.dma("sp", reads, writes, fn)

    def rows(ap2d, i):
        return ap2d[i * 128:(i + 1) * 128, :]

    def load_w(st, name, src2d, nk, ncols, col0=0):
        buf = st.sb(name, [128, nk, ncols], BF16)
        w = buf.t()
        step = max(1, 4096 // ncols)
        k = 0
        while k < nk:
            k2 = min(nk, k + step)
            tr.dma("pool", [], [buf.c()],
                   lambda e, k=k, k2=k2: e.dma_start(
                       out=w[:, k:k2, :],
                       in_=src2d[k * 128:k2 * 128, col0:col0 + ncols].rearrange("(k p) c -> p k c", p=128)))
            k = k2
        return buf

    def load_bcast(st, name, src_row, n, dt=F32):
        buf = st.sb(name, [128, n], dt)
        q = "sp" if dt == F32 else "pool"
        tr.dma(q, [], [buf.c()], lambda e: e.dma_start(out=buf.t()[:], in_=src_row.to_broadcast([128, n])))
        return buf

    def transposes(src_fn, n, ptslot, reads):
        def f(e):
            ins = None
            for j in range(n):
                ins = e.transpose(PT.t(ptslot)[:, j * 128:(j + 1) * 128], src_fn(j), ident[:])
            return ins
        P(reads + [("ident",)], [PT.c(ptslot)], f)

    def gemm(lhs_fn, nk, wbuf, col0, ncols, out_ap, reads, writes, extra=None):
        def f(e):
            ins = None
            for k in range(nk):
                ins = e.matmul(out_ap, lhsT=lhs_fn(k), rhs=wbuf.t()[:, k, col0:col0 + ncols],
                               start=(k == 0), stop=(k == nk - 1 and extra is None))
            if extra is not None:
                ins = e.matmul(out_ap, lhsT=extra[0], rhs=extra[1], start=False, stop=True)
            return ins
        P(reads + [wbuf.c()], writes, f)

    pt_rr = [0]
    pa_rr = [0]

    def next_pt():
        pt_rr[0] += 1
        return pt_rr[0]

    def next_pa():
        pa_rr[0] += 1
        return pa_rr[0]

    def to_T(src_buf, slot, nblk, dst_buf, dslot, copy_eng="dve"):
        p = next_pt()
        src = src_buf.t(slot)
        transposes(lambda j: src[:, j * 128:(j + 1) * 128], nblk, p, [src_buf.c(slot)])
        dst = dst_buf.t(dslot)
        fn = lambda e: e.tensor_copy(out=dst[:].rearrange("p a b -> p (a b)")[:, 0:nblk * 128],
                                     in_=PT.t(p)[:, 0:nblk * 128])
        if copy_eng == "act":
            fn = lambda e: e.copy(out=dst[:].rearrange("p a b -> p (a b)")[:, 0:nblk * 128],
                                  in_=PT.t(p)[:, 0:nblk * 128])
        tr.op(copy_eng, [PT.c(p)], [dst_buf.c(dslot)], fn)

    def rsqrt(ap, cells, premul, eps):
        V(cells, cells, lambda e: e.tensor_scalar(out=ap, in0=ap, scalar1=premul, scalar2=eps, op0=ALU.mult, op1=ALU.add))
        yield
        A(cells, cells, lambda e: e.activation(out=ap, in_=ap, func=AF.Sqrt))
        yield
        V(cells, cells, lambda e: e.reciprocal(out=ap, in_=ap))

    def layer_norm(src_ap, src_cells, st_bufs, slot, wrep, brep, dst_ap, dst_cells, tmp_ap=None, tmp_cells=None):
        stats, mv, rs = st_bufs
        s_t, mv_t, rs_t = stats.t(slot), mv.t(slot), rs.t(slot)
        V(src_cells, [stats.c(slot)], lambda e: (
            e.bn_stats(out=s_t[:, 0:6], in_=src_ap[:, 0:512]),
            e.bn_stats(out=s_t[:, 6:12], in_=src_ap[:, 512:1024]))[-1])
        V([stats.c(slot)], [mv.c(slot)], lambda e: e.bn_aggr(out=mv_t[:, 0:2], in_=s_t[:, 0:12]))
        V([mv.c(slot)], [rs.c(slot)], lambda e: e.tensor_copy(out=rs_t[:, 0:1], in_=mv_t[:, 1:2]))
        yield from rsqrt(rs_t[:, 0:1], [rs.c(slot)], 1.0, LN_EPS)
        t_ap = tmp_ap if tmp_ap is not None else src_ap
        t_cells = tmp_cells if tmp_cells is not None else src_cells
        V(src_cells + [mv.c(slot), rs.c(slot)], t_cells, lambda e: e.tensor_scalar(
            out=t_ap, in0=src_ap, scalar1=mv_t[:, 0:1], scalar2=rs_t[:, 0:1], op0=ALU.subtract, op1=ALU.mult))
        yield
        G(t_cells + [wrep.c()], t_cells, lambda e: e.tensor_tensor(out=t_ap, in0=t_ap, in1=wrep.t()[:], op=ALU.mult))
        G(t_cells + [brep.c()], dst_cells, lambda e: e.tensor_tensor(out=dst_ap, in0=t_ap, in1=brep.t()[:], op=ALU.add))

    def ln_bufs(st, tag, nslots=2):
        return (st.sb(tag + "st", [128, 12], F32, nslots), st.sb(tag + "mv", [128, 2], F32, nslots),
                st.sb(tag + "rs", [128, 1], F32, nslots))

    def run_tiles(body, n, width=2):
        nxt = 0
        active = []
        while nxt < n or active:
            while len(active) < width and nxt < n:
                active.append(body(nxt))
                nxt += 1
            for g in list(active):
                try:
                    next(g)
                except StopIteration:
                    active.remove(g)

    st = Stage(nc, tr, "l0")
    wrep = load_bcast(st, "w", in_ln_w, D)
    brep = load_bcast(st, "b", in_ln_b, D)
    xin = st.sb("xin", [128, D], F32, 3)
    xo = st.sb("xo", [128, D], F32, 2)
    lb = ln_bufs(st, "ln")
    for i in range(NT):
        LD([], [xin.c(i)], lambda e, i=i: e.dma_start(out=xin.t(i)[:], in_=rows(x_in, i)))
        for _ in layer_norm(xin.t(i)[:], [xin.c(i)], lb, i, wrep, brep, xo.t(i)[:], [xo.c(i)]):
            pass
        STO([xo.c(i)], [], lambda e, i=i: e.dma_start(out=rows(X, i), in_=xo.t(i)[:]))
    st.close()
    if stop_after is not None and stop_after == ("l0"):
        tr.barrier()
        return nc, tr

    def load_xT(st, Xsrc, i, xin, xb, xT):
        LD([], [xin.c(i)], lambda e: e.dma_start(out=xin.t(i)[:], in_=rows(Xsrc, i)))
        A([xin.c(i)], [xb.c(i)], lambda e: e.copy(out=xb.t(i)[:], in_=xin.t(i)[:]))
        to_T(xb, i, 8, xT, i)

    for l in range(L):
        lt = "y%d" % l
        ls = Stage(nc, tr, lt + "p")
        lgt = ls.sb("lg", [128, 16], F32)
        tmp16 = ls.sb("t16", [128, 16], F32)
        DTb = ls.sb("DT", [128, 8, 128], F32)
        XIF = ls.sb("XIF", [128, 8, 128], F32)
        XIB = ls.sb("XIB", [128, 8, 128], F32)
        ZT = ls.sb("ZT", [128, 16], F32)
        CDt = ls.sb("CD", [128, 16], F32)
        CF = ls.sb("CF", [128, 8, 8], F32)
        CB = ls.sb("CB", [128, 8, 8], F32)
        wq_rep = ls.sb("wqr", [128, 128], F32)
        wk_rep = ls.sb("wkr", [128, 128], F32)
        mx = ls.sb("mx", [128, 4], F32)
        wq_sw = ls.sb("wqs", [128, 128], F32)
        wk_sw = ls.sb("wks", [128, 128], F32)
        lsd = Stage(nc, tr, lt + "q")
        dtmp = lsd.sb("dtmp", [128, 8, 128], F32)
        tr.dma("sp", [], [tmp16.c()], lambda e: e.dma_start(
            out=tmp16.t()[:], in_=ret_decay[l:l + 1, :].to_broadcast([128, 16])))
        A([tmp16.c()], [lgt.c()], lambda e: e.activation(out=lgt.t()[:], in_=tmp16.t()[:], func=AF.Exp, scale=-1.0))
        V([lgt.c()], [tmp16.c()], lambda e: e.tensor_scalar(
            out=tmp16.t()[:], in0=lgt.t()[:], scalar1=1.0, scalar2=None, op0=ALU.add))
        A([tmp16.c()], [lgt.c()], lambda e: e.activation(out=lgt.t()[:], in_=tmp16.t()[:], func=AF.Ln))
        V([lgt.c()], [lgt.c()], lambda e: e.tensor_scalar(
            out=lgt.t()[:], in0=lgt.t()[:], scalar1=-1.0, scalar2=None, op0=ALU.mult))
        lg = lgt.t()
        for h in range(8):
            A([lgt.c(), ("ctb",)], [DTb.c()], lambda e, h=h: e.activation(
                out=DTb.t()[:, h, :], in_=ctb[:, C_PF:C_PF + 128], func=AF.Exp, scale=lg[:, h:h + 1]))
            A([lgt.c(), ("ctb",)], [dtmp.c()], lambda e, h=h: e.activation(
                out=dtmp.t()[:, h, :], in_=ctb[:, C_PB:C_PB + 128], func=AF.Exp, scale=lg[:, 8 + h:9 + h]))
        V([DTb.c(), ("ctb",)], [DTb.c()], lambda e: e.tensor_tensor(
            out=DTb.t()[:], in0=DTb.t()[:], in1=ctb[:, C_MF:C_MF + 128].unsqueeze(1).to_broadcast([128, 8, 128]),
            op=ALU.mult))
        V([dtmp.c(), ("ctb",)], [dtmp.c()], lambda e: e.tensor_tensor(
            out=dtmp.t()[:], in0=dtmp.t()[:], in1=ctb[:, C_MB:C_MB + 128].unsqueeze(1).to_broadcast([128, 8, 128]),
            op=ALU.mult))
        V([DTb.c(), dtmp.c()], [DTb.c()], lambda e: e.tensor_tensor(
            out=DTb.t()[:], in0=DTb.t()[:], in1=dtmp.t()[:], op=ALU.add))
        for h in range(8):
            A([lgt.c(), ("ctb",)], [XIF.c()], lambda e, h=h: e.activation(
                out=XIF.t()[:, h, :], in_=ctb[:, C_RP1:C_RP1 + 128], func=AF.Exp, scale=lg[:, h:h + 1]))
            A([lgt.c(), ("ctb",)], [XIB.c()], lambda e, h=h: e.activation(
                out=XIB.t()[:, h, :], in_=ctb[:, C_RM:C_RM + 128], func=AF.Exp, scale=lg[:, 8 + h:9 + h]))
        A([lgt.c(), ("ctb",)], [ZT.c()], lambda e: e.activation(
            out=ZT.t()[:, 0:8], in_=lg[:, 0:8], func=AF.Exp, scale=ctb[:, C_Z:C_Z + 1]))
        A([lgt.c(), ("ctb",)], [ZT.c()], lambda e: e.activation(
            out=ZT.t()[:, 8:16], in_=lg[:, 8:16], func=AF.Exp, scale=ctb[:, C_Z + 1:C_Z + 2]))
        A([lgt.c()], [CDt.c()], lambda e: e.activation(out=CDt.t()[:], in_=lg[:], func=AF.Exp, scale=128.0))
        for c2 in range(cfg.NC):
            A([lgt.c(), ("cct",)], [CF.c()], lambda e, c2=c2: e.activation(
                out=CF.t()[:, c2, :], in_=lg[:, 0:8], func=AF.Exp, scale=cct[:, c2:c2 + 1]))
            A([lgt.c(), ("cct",)], [CB.c()], lambda e, c2=c2: e.activation(
                out=CB.t()[:, c2, :], in_=lg[:, 8:16], func=AF.Exp, scale=cct[:, 8 + c2:9 + c2]))
        V([CF.c(), ("cct",)], [CF.c()], lambda e: e.tensor_tensor(
            out=CF.t()[:], in0=CF.t()[:], in1=cct[:, 16:24].unsqueeze(2).to_broadcast([128, 8, 8]), op=ALU.mult))
        V([CB.c(), ("cct",)], [CB.c()], lambda e: e.tensor_tensor(
            out=CB.t()[:], in0=CB.t()[:], in1=cct[:, 24:32].unsqueeze(2).to_broadcast([128, 8, 8]), op=ALU.mult))
        tr.dma("sp", [], [wq_rep.c()], lambda e: e.dma_start(
            out=wq_rep.t()[:], in_=att_qn_w[l:l + 1, :].to_broadcast([128, 128])))
        tr.dma("sp", [], [wk_rep.c()], lambda e: e.dma_start(
            out=wk_rep.t()[:], in_=att_kn_w[l:l + 1, :].to_broadcast([128, 128])))
        V([wq_rep.c()], [mx.c(0, 1)], lambda e: e.tensor_reduce(
            out=mx.t()[:, 0:1], in_=wq_rep.t()[:], axis=AX.X, op=ALU.max, apply_absolute_value=True))
        V([wk_rep.c()], [mx.c(0, 2)], lambda e: e.tensor_reduce(
            out=mx.t()[:, 1:2], in_=wk_rep.t()[:], axis=AX.X, op=ALU.max, apply_absolute_value=True))
        V([mx.c(0, 1), mx.c(0, 2)], [mx.c(0, 3)], lambda e: e.tensor_scalar(
            out=mx.t()[:, 2:3], in0=mx.t()[:, 0:1], scalar1=mx.t()[:, 1:2], scalar2=-math.sqrt(128.0),
            op0=ALU.mult, op1=ALU.mult))
        negM = mx.t()[:, 2:3]
        negM_c = mx.c(0, 3)
        for (src, dst) in ((wq_rep, wq_sw), (wk_rep, wk_sw)):
            s4 = src.t()[:].rearrange("p (a h b) -> p a h b", a=2, h=2)
            d4 = dst.t()[:].rearrange("p (a h b) -> p a h b", a=2, h=2)
            V([src.c()], [dst.c()], lambda e, s4=s4, d4=d4: (
                e.tensor_copy(out=d4[:, :, 0, :], in_=s4[:, :, 1, :]),
                e.tensor_copy(out=d4[:, :, 1, :], in_=s4[:, :, 0, :]))[-1])
        lsd.close()

        st = Stage(nc, tr, lt + "m")
        wkv = load_w(st, "wkv", xa_wkv[l], 8, 2 * D)
        xin = st.sb("xin", [128, D], F32, 2)
        xb = st.sb("xb", [128, D], BF16, 2)
        xT = st.sb("xT", [128, 8, 128], BF16, 2)
        kb_ = st.sb("kb", [128, D], BF16, 2)
        kT = st.sb("kT", [128, 8, 128], BF16, 2)
        vb_ = st.sb("vb", [128, D], BF16, 2)
        for i in range(NSEG * 2):
            load_xT(st, mem_in, i, xin, xb, xT)
            for half, (dst, eng) in enumerate(((kb_, "act"), (vb_, "dve"))):
                p = next_pa()
                for cbk in range(2):
                    gemm(lambda k: xT.t(i)[:, k, :], 8, wkv, half * D + cbk * 512, 512,
                         PA.t(p)[:, cbk * 512:(cbk + 1) * 512], [xT.c(i)], [PA.c(p, cbk)])
                if eng == "act":
                    A([PA.c(p, 0), PA.c(p, 1)], [dst.c(i)], lambda e, p=p, dst=dst: e.copy(out=dst.t(i)[:], in_=PA.t(p)[:]))
                else:
                    V([PA.c(p, 0), PA.c(p, 1)], [dst.c(i)], lambda e, p=p, dst=dst: e.tensor_copy(out=dst.t(i)[:], in_=PA.t(p)[:]))
            to_T(kb_, i, 8, kT, i, copy_eng="act")
            STO([kT.c(i)], [], lambda e, i=i: e.dma_start(out=rows(MK, i), in_=kT.t(i)[:].rearrange("p a b -> p (a b)")))
            STO([vb_.c(i)], [], lambda e, i=i: e.dma_start(out=rows(MV, i), in_=vb_.t(i)[:]))
        st.close()
        if stop_after is not None and stop_after == (lt + "m"):
            tr.barrier()
            return nc, tr

        st = Stage(nc, tr, lt + "a")
        wa = load_w(st, "w", w_in[l], 8, 4096, 0)
        xin = st.sb("xin", [128, D], F32, 2)
        xb = st.sb("xb", [128, D], BF16, 2)
        xT = st.sb("xT", [128, 8, 128], BF16, 2)
        tabr = st.sb("tabr", [128, 512], F32, 2)
        xs = st.sb("xs", [128, 8, 128], F32, 2)
        t1 = st.sb("t1", [128, 8, 128], F32, 2)
        u = st.sb("u", [128, 8, 128], F32, 2)
        ob = st.sb("ob", [128, D], BF16, 4)
        oc = [0]

        def rope_ret(i, p, tab_off, dstD):
            k = oc[0]
            oc[0] += 1
            tb = tabr.t(i)
            Cb = tb[:, tab_off:tab_off + 128].unsqueeze(1).to_broadcast([128, 8, 128])
            Slo = tb[:, tab_off + 128:tab_off + 192].unsqueeze(1).to_broadcast([128, 8, 64])
            Shi = tb[:, tab_off + 192:tab_off + 256].unsqueeze(1).to_broadcast([128, 8, 64])
            A([PA.c(p, 0), PA.c(p, 1)], [xs.c(k)], lambda e: e.copy(
                out=xs.t(k)[:].rearrange("p a b -> p (a b)"), in_=PA.t(p)[:]))
            V([xs.c(k), tabr.c(i)], [t1.c(k)], lambda e: e.tensor_tensor(out=t1.t(k)[:], in0=xs.t(k)[:], in1=Cb, op=ALU.mult))
            G([xs.c(k), tabr.c(i)], [u.c(k)], lambda e: (
                e.tensor_tensor(out=u.t(k)[:, :, 0:64], in0=xs.t(k)[:, :, 64:128], in1=Slo, op=ALU.mult),
                e.tensor_tensor(out=u.t(k)[:, :, 64:128], in0=xs.t(k)[:, :, 0:64], in1=Shi, op=ALU.mult))[-1])
            V([t1.c(k), u.c(k)], [ob.c(k)], lambda e: e.tensor_tensor(
                out=ob.t(k)[:].rearrange("p (a b) -> p a b", a=8), in0=t1.t(k)[:], in1=u.t(k)[:], op=ALU.add))
            STO([ob.c(k)], [], lambda e: e.dma_start(out=rows(dstD, i), in_=ob.t(k)[:]))

        def body(i):
            load_xT(st, X, i, xin, xb, xT)
            LD([], [tabr.c(i)], lambda e, i=i: e.dma_start(out=tabr.t(i)[:], in_=rows(tab_r, cfg.tab_row(i))))
            yield
            for grp in range(4):
                p = next_pa()
                for cbk in range(2):
                    gemm(lambda k: xT.t(i)[:, k, :], 8, wa, grp * 1024 + cbk * 512, 512,
                         PA.t(p)[:, cbk * 512:(cbk + 1) * 512], [xT.c(i)], [PA.c(p, cbk)])
                if grp == 0:
                    rope_ret(i, p, 0, RQ)
                elif grp == 1:
                    rope_ret(i, p, 256, RK)
                else:
                    k = oc[0]
                    oc[0] += 1
                    fn = AF.Copy if grp == 2 else AF.Silu
                    A([PA.c(p, 0), PA.c(p, 1)], [ob.c(k)], lambda e, p=p, k=k, fn=fn: e.activation(
                        out=ob.t(k)[:], in_=PA.t(p)[:], func=fn))
                    dstD = RV if grp == 2 else RG
                    STO([ob.c(k)], [], lambda e, k=k, dstD=dstD, i=i: e.dma_start(out=rows(dstD, i), in_=ob.t(k)[:]))
                yield
        run_tiles(body, NT)
        st.close()
        if stop_after is not None and stop_after == (lt + "a"):
            tr.barrier()
            return nc, tr

        st = Stage(nc, tr, lt + "b")
        wb = load_w(st, "w", w_in[l], 8, 3584, 4096)
        sgw = load_bcast(st, "sgw", sg_ln_w[l:l + 1, :], D)
        sgb = load_bcast(st, "sgb", sg_ln_b[l:l + 1, :], D)
        xin = st.sb("xin", [128, D], F32, 2)
        xb = st.sb("xb", [128, D], BF16, 2)
        xT = st.sb("xT", [128, 8, 128], BF16, 2)
        taba = st.sb("taba", [128, 256], F32, 2)
        cw = st.sb("cw", [128, 4, 128], F32, 2)
        xs = st.sb("xs", [128, 8, 128], F32, 4)
        sq = st.sb("sq", [128, 8, 128], F32, 4)
        t1 = st.sb("t1", [128, 8, 128], F32, 4)
        u = st.sb("u", [128, 8, 128], F32, 4)
        ss = st.sb("ss", [128, 16], F32, 4)
        svf = st.sb("svf", [128, D], F32, 2)
        ob = st.sb("ob", [128, D], BF16, 4)
        okv = st.sb("okv", [128, 512], BF16, 2)
        lb = ln_bufs(st, "ln")
        oc = [0]

        def axial(i, k, nh, Ci, Si, src3, dst3, rst, rs_off, cells_in, cells_out):
            cwt = cw.t(i)
            G(cells_in, [sq.c(k)], lambda e: e.tensor_tensor(out=sq.t(k)[:, 0:nh, :], in0=src3, in1=src3, op=ALU.mult))
            V([sq.c(k)], [ss.c(k, rs_off)], lambda e: e.tensor_reduce(
                out=rst[:, rs_off:rs_off + nh], in_=sq.t(k)[:, 0:nh, :], axis=AX.X, op=ALU.add))
            yield from rsqrt(rst[:, rs_off:rs_off + nh], [ss.c(k, rs_off)], 1.0 / 128.0, RMS_EPS)
            Cb = cwt[:, Ci, :].unsqueeze(1).to_broadcast([128, nh, 128])
            V(cells_in + [cw.c(i)], [t1.c(k)], lambda e: e.tensor_tensor(out=t1.t(k)[:, 0:nh, :], in0=src3, in1=Cb, op=ALU.mult))
            s5 = src3.rearrange("p h (a c b) -> p h a c b", a=2, c=2)
            u5 = u.t(k)[:, 0:nh, :].rearrange("p h (a c b) -> p h a c b", a=2, c=2)
            S4 = cwt[:, Si, :].rearrange("p (a c b) -> p a c b", a=2, c=2)

            def fu2(e):
                ins = None
                for a in range(2):
                    for c in range(2):
                        ins = e.tensor_tensor(out=u5[:, :, a, c, :], in0=s5[:, :, a, 1 - c, :],
                                              in1=S4[:, a, c, :].unsqueeze(1).to_broadcast([128, nh, 32]), op=ALU.mult)
                return ins
            G(cells_in + [cw.c(i)], [u.c(k)], fu2)
            V([t1.c(k), u.c(k)], [t1.c(k)], lambda e: e.tensor_tensor(
                out=t1.t(k)[:, 0:nh, :], in0=t1.t(k)[:, 0:nh, :], in1=u.t(k)[:, 0:nh, :], op=ALU.add))
            V([t1.c(k), ss.c(k, rs_off)], cells_out, lambda e: e.tensor_tensor(
                out=dst3, in0=t1.t(k)[:, 0:nh, :],
                in1=rst[:, rs_off:rs_off + nh].unsqueeze(2).to_broadcast([128, nh, 128]), op=ALU.mult))

        def body(i):
            load_xT(st, X, i, xin, xb, xT)
            LD([], [taba.c(i)], lambda e, i=i: e.dma_start(out=taba.t(i)[:], in_=rows(tab_a, cfg.tab_row(i))))
            ta = taba.t(i)
            G([taba.c(i), wq_rep.c(), wq_sw.c(), wk_rep.c(), wk_sw.c()], [cw.c(i)], lambda e, i=i, ta=ta: (
                e.tensor_tensor(out=cw.t(i)[:, 0, :], in0=ta[:, 0:128], in1=wq_rep.t()[:], op=ALU.mult),
                e.tensor_tensor(out=cw.t(i)[:, 1, :], in0=ta[:, 128:256], in1=wq_sw.t()[:], op=ALU.mult),
                e.tensor_tensor(out=cw.t(i)[:, 2, :], in0=ta[:, 0:128], in1=wk_rep.t()[:], op=ALU.mult),
                e.tensor_tensor(out=cw.t(i)[:, 3, :], in0=ta[:, 128:256], in1=wk_sw.t()[:], op=ALU.mult))[-1])
            yield
            p = next_pa()
            for cbk in range(2):
                gemm(lambda k: xT.t(i)[:, k, :], 8, wb, cbk * 512, 512,
                     PA.t(p)[:, cbk * 512:(cbk + 1) * 512], [xT.c(i)], [PA.c(p, cbk)])
            k = oc[0]; oc[0] += 1
            A([PA.c(p, 0), PA.c(p, 1)], [ob.c(k)], lambda e, p=p, k=k: e.activation(
                out=ob.t(k)[:], in_=PA.t(p)[:], func=AF.Gelu_apprx_tanh))
            STO([ob.c(k)], [], lambda e, k=k, i=i: e.dma_start(out=rows(SU, i), in_=ob.t(k)[:]))
            yield
            p = next_pa()
            for cbk in range(2):
                gemm(lambda k: xT.t(i)[:, k, :], 8, wb, 1024 + cbk * 512, 512,
                     PA.t(p)[:, cbk * 512:(cbk + 1) * 512], [xT.c(i)], [PA.c(p, cbk)])
            A([PA.c(p, 0), PA.c(p, 1)], [svf.c(i)], lambda e, p=p, i=i: e.activation(
                out=svf.t(i)[:], in_=PA.t(p)[:], func=AF.Gelu_apprx_tanh))
            k = oc[0]; oc[0] += 1
            yield from layer_norm(svf.t(i)[:], [svf.c(i)], lb, i, sgw, sgb, ob.t(k)[:], [ob.c(k)])
            STO([ob.c(k)], [], lambda e, k=k, i=i: e.dma_start(out=rows(SV, i), in_=ob.t(k)[:]))
            yield
            p = next_pa()
            for cbk in range(2):
                gemm(lambda k: xT.t(i)[:, k, :], 8, wb, 2048 + cbk * 512, 512,
                     PA.t(p)[:, cbk * 512:(cbk + 1) * 512], [xT.c(i)], [PA.c(p, cbk)])
            k = oc[0]; oc[0] += 1
            kx = (i % 2) * 2
            A([PA.c(p, 0), PA.c(p, 1)], [xs.c(kx)], lambda e, p=p, kx=kx: e.copy(
                out=xs.t(kx)[:].rearrange("p a b -> p (a b)"), in_=PA.t(p)[:]))
            yield from axial(i, kx, 8, 0, 1, xs.t(kx)[:], ob.t(k)[:].rearrange("p (a b) -> p a b", a=8), ss.t(kx), 0,
                  [xs.c(kx)], [ob.c(k)])
            STO([ob.c(k)], [], lambda e, k=k, i=i: e.dma_start(out=rows(AQ, i), in_=ob.t(k)[:]))
            yield
            p = next_pa()
            gemm(lambda k: xT.t(i)[:, k, :], 8, wb, 3072, 512, PA.t(p)[:, 0:512], [xT.c(i)], [PA.c(p, 0)])
            k2 = (i % 2) * 2 + 1
            A([PA.c(p, 0)], [xs.c(k2)], lambda e, p=p, k2=k2: e.copy(
                out=xs.t(k2)[:].rearrange("p a b -> p (a b)")[:, 0:512], in_=PA.t(p)[:, 0:512]))
            yield from axial(i, k2, 2, 2, 3, xs.t(k2)[:, 0:2, :], okv.t(i)[:, 0:256].rearrange("p (a b) -> p a b", a=2), ss.t(k2), 8,
                  [xs.c(k2)], [okv.c(i, 0)])
            V([xs.c(k2)], [okv.c(i, 1)], lambda e, k2=k2, i=i: e.tensor_copy(
                out=okv.t(i)[:, 256:512], in_=xs.t(k2)[:, 2:4, :].rearrange("p a b -> p (a b)")))
            STO([okv.c(i, 0), okv.c(i, 1)], [], lambda e, i=i: e.dma_start(out=rows(AKV, i), in_=okv.t(i)[:]))
            yield
        run_tiles(body, NT)
        st.close()
        if stop_after is not None and stop_after == (lt + "b"):
            tr.barrier()
            return nc, tr

        st = Stage(nc, tr, lt + "g")
        wg = load_w(st, "w", w_in[l], 8, 3 * D, 4096 + 3584)
        bgr = st.sb("bg", [1, 3 * D], BF16)
        tr.dma("pool", [], [bgr.c()], lambda e: e.dma_start(out=bgr.t()[:], in_=b_gate[l:l + 1, :]))
        xin = st.sb("xin", [128, D], F32, 2)
        xb = st.sb("xb", [128, D], BF16, 2)
        xT = st.sb("xT", [128, 8, 128], BF16, 2)
        og = st.sb("og", [128, 3 * D], BF16, 2)
        def body(i):
            load_xT(st, X, i, xin, xb, xT)
            for g3 in range(3):
                p = next_pa()
                for cbk in range(2):
                    c0 = g3 * 1024 + cbk * 512
                    gemm(lambda k: xT.t(i)[:, k, :], 8, wg, c0, 512,
                         PA.t(p)[:, cbk * 512:(cbk + 1) * 512], [xT.c(i), bgr.c(), ("ones",)], [PA.c(p, cbk)],
                         extra=(ones[0:1, :], bgr.t()[0:1, c0:c0 + 512]))
                A([PA.c(p, 0), PA.c(p, 1)], [og.c(i, g3)], lambda e, p=p, i=i, g3=g3: e.activation(
                    out=og.t(i)[:, g3 * 1024:(g3 + 1) * 1024], in_=PA.t(p)[:], func=AF.Sigmoid))
                yield
            STO([og.c(i, 0), og.c(i, 1), og.c(i, 2)], [], lambda e, i=i: e.dma_start(out=rows(GT, i), in_=og.t(i)[:]))
        run_tiles(body, NT)
        st.close()
        if stop_after is not None and stop_after == (lt + "g"):
            tr.barrier()
            return nc, tr

        st = Stage(nc, tr, lt + "s")
        rk_t = st.sb("rk", [128, 8, 128], BF16, 4)
        rv_t = st.sb("rv", [128, 8, 128], BF16, 4)
        kz = st.sb("kz", [128, 8, 128], BF16, 4)
        S = st.sb("S", [128, 8, 128], F32, 4)
        Sb16 = st.sb("Sb", [128, 8, 128], BF16, 4)
        eall = st.sb("eall", [128, 8, 128], F32, 2)
        it = [0]

        def run_gens(gens, width):
            gens = list(gens)
            active = []
            while gens or active:
                while len(active) < width and gens:
                    active.append(gens.pop(0))
                for g in list(active):
                    try:
                        next(g)
                    except StopIteration:
                        active.remove(g)

        def scan(tiles, direction, store, sl, zero=True):
            order = tiles if direction == 0 else tiles[::-1]
            zoff = 8 * direction
            Sd = SF if direction == 0 else SB
            S3 = S.t(sl)[:]
            if zero:
                V([], [S.c(sl)], lambda e: e.memset(S3.rearrange("p a b -> p (a b)"), 0.0))
            for i in order:
                j = it[0]; it[0] += 1
                LD([], [rk_t.c(j)], lambda e: e.dma_start(out=rk_t.t(j)[:].rearrange("p a b -> p (a b)"), in_=rows(RK, i)))
                LD([], [rv_t.c(j)], lambda e: e.dma_start(out=rv_t.t(j)[:].rearrange("p a b -> p (a b)"), in_=rows(RV, i)))
                if store:
                    A([S.c(sl)], [Sb16.c(j)], lambda e: e.copy(out=Sb16.t(j)[:], in_=S3))
                    STO([Sb16.c(j)], [], lambda e: e.dma_start(out=rows(Sd, i), in_=Sb16.t(j)[:].rearrange("p a b -> p (a b)")))
                V([rk_t.c(j), ZT.c()], [kz.c(j)], lambda e: e.tensor_tensor(
                    out=kz.t(j)[:], in0=rk_t.t(j)[:],
                    in1=ZT.t()[:, zoff:zoff + 8].unsqueeze(2).to_broadcast([128, 8, 128]), op=ALU.mult))
                V([S.c(sl), CDt.c()], [S.c(sl)], lambda e: e.tensor_tensor(
                    out=S3, in0=S3, in1=CDt.t()[:, zoff:zoff + 8].unsqueeze(2).to_broadcast([128, 8, 128]), op=ALU.mult))
                yield
                p = next_pa()

                def f(e):
                    ins = None
                    for h in range(8):
                        ins = e.matmul(PA.t(p)[:, h * 128:(h + 1) * 128], lhsT=kz.t(j)[:, h, :], rhs=rv_t.t(j)[:, h, :],
                                       start=True, stop=True)
                    return ins
                P([kz.c(j), rv_t.c(j)], [PA.c(p, 0), PA.c(p, 1)], f)
                V([S.c(sl), PA.c(p, 0), PA.c(p, 1)], [S.c(sl)], lambda e: e.tensor_tensor(
                    out=S3, in0=S3, in1=PA.t(p)[:].rearrange("p (a b) -> p a b", a=8), op=ALU.add))
                yield

        run_gens([scan(cfg.seg_tiles(s_), d, True, (2 * s_ + d) % 4) for s_ in range(cfg.NP) for d in range(2)], 4)
        stl = cfg.seg_tiles(cfg.NP)
        run_gens([scan(stl, d, False, d) for d in range(2)], 2)
        for d in range(2):
            tr.dma("sp", [S.c(d)], [("ccin",)], lambda e, d=d: e.dma_start(
                out=Eloc_v[d * 128:(d + 1) * 128, :], in_=S.t(d)[:].rearrange("p a b -> p (a b)")))
        allgather()
        for d in range(2):
            S3 = S.t(d)[:]
            V([], [S.c(d)], lambda e: e.memset(S3.rearrange("p a b -> p (a b)"), 0.0))
            CO = CF if d == 0 else CB
            for c2 in range(cfg.NC):
                j = it[0]; it[0] += 1
                LD([("ccout",)], [eall.c(j)], lambda e, c2=c2, j=j, d=d: e.dma_start(
                    out=eall.t(j)[:].rearrange("p a b -> p (a b)"),
                    in_=Eall_v(c2)[d * 128:(d + 1) * 128, :]))
                V([eall.c(j), CO.c()], [eall.c(j)], lambda e, c2=c2, j=j, CO=CO: e.tensor_tensor(
                    out=eall.t(j)[:], in0=eall.t(j)[:],
                    in1=CO.t()[:, c2, :].unsqueeze(2).to_broadcast([128, 8, 128]), op=ALU.mult))
                V([eall.c(j), S.c(d)], [S.c(d)], lambda e, j=j, S3=S3: e.tensor_tensor(out=S3, in0=S3, in1=eall.t(j)[:], op=ALU.add))
        run_gens([scan(stl, d, True, d, zero=False) for d in range(2)], 2)
        st.close()
        if stop_after is not None and stop_after == (lt + "s"):
            tr.barrier()
            return nc, tr

        s0 = cfg.NP * cfg.TP * 128
        tr.dma("sp", [], [("ccin",)], lambda e: e.dma_start(out=CCin_bf[0:cfg.SSEG, :], in_=AKV[s0:s0 + cfg.SSEG, :]))
        allgather()
        tr.barrier()

        st = Stage(nc, tr, lt + "r")
        gnw = load_bcast(st, "gnw", ret_gn_w[l:l + 1, :], D)
        wsT = st.sb("wsT", [128, 4, 128], BF16)
        wsl = st.sb("wsl", [128, 4, 128], BF16)
        tr.dma("pool", [], [wsl.c()], lambda e: e.dma_start(out=wsl.t()[:], in_=sg_ws[l].rearrange("g c m -> c g m")))
        pq = next_pt()
        transposes(lambda j: wsl.t()[:, j, :], 4, pq, [wsl.c()])
        V([PT.c(pq)], [wsT.c()], lambda e: e.tensor_copy(out=wsT.t()[:].rearrange("p a b -> p (a b)"), in_=PT.t(pq)[:, 0:512]))
        sgbt = st.sb("sgbt", [128, 4], F32)
        sgb4 = st.sb("sgb4", [4, 128], F32)
        tr.dma("sp", [], [sgb4.c()], lambda e: e.dma_start(out=sgb4.t()[:], in_=sg_b[l]))
        P([sgb4.c(), ("ctb",)], [PB.c(0)], lambda e: e.matmul(
            PB.t(0)[:, 0:4], lhsT=sgb4.t()[0:4, :], rhs=ctb[0:4, C_I4:C_I4 + 4], start=True, stop=True))
        V([PB.c(0)], [sgbt.c()], lambda e: e.tensor_copy(out=sgbt.t()[:], in_=PB.t(0)[:, 0:4]))
        q_t = st.sb("q", [128, D], BF16, 2)
        k_t = st.sb("k", [128, D], BF16, 2)
        v_t = st.sb("v", [128, 8, 128], BF16, 2)
        g_t = st.sb("g", [128, D], BF16, 2)
        su_t = st.sb("su", [128, D], BF16, 2)
        sv_t = st.sb("sv", [128, D], BF16, 2)
        sf_t = st.sb("sf", [128, 8, 128], BF16, 2)
        sb_t = st.sb("sbb", [128, 8, 128], BF16, 2)
        qT = st.sb("qT", [128, 8, 128], BF16, 2)
        kT = st.sb("kT", [128, 8, 128], BF16, 2)
        qTf = st.sb("qTf", [128, 8, 128], BF16, 2)
        qTb = st.sb("qTb", [128, 8, 128], BF16, 2)
        Pm = st.sb("Pm", [128, 8, 128], BF16, 2)
        ro = st.sb("ro", [128, 8, 128], F32, 2)
        rsq = st.sb("rsq", [128, 8, 128], F32, 2)
        gst = st.sb("gst", [128, 32], F32, 2)
        yr = st.sb("yr", [128, D], BF16, 2)
        ys = st.sb("ys", [128, D], BF16, 2)
        def body(i):
            for (buf, src) in ((q_t, RQ), (k_t, RK), (g_t, RG), (su_t, SU), (sv_t, SV)):
                LD([], [buf.c(i)], lambda e, buf=buf, src=src, i=i: e.dma_start(out=buf.t(i)[:], in_=rows(src, i)))
            LD([], [v_t.c(i)], lambda e, i=i: e.dma_start(out=v_t.t(i)[:].rearrange("p a b -> p (a b)"), in_=rows(RV, i)))
            LD([], [sf_t.c(i)], lambda e, i=i: e.dma_start(out=sf_t.t(i)[:].rearrange("p a b -> p (a b)"), in_=rows(SF, i)))
            LD([], [sb_t.c(i)], lambda e, i=i: e.dma_start(out=sb_t.t(i)[:].rearrange("p a b -> p (a b)"), in_=rows(SB, i)))
            yield
            pq = next_pt()
            transposes(lambda j: q_t.t(i)[:, j * 128:(j + 1) * 128], 8, pq, [q_t.c(i)])
            ptq = PT.t(pq)[:].rearrange("p (a b) -> p a b", a=8)
            A([PT.c(pq)], [qT.c(i)], lambda e, ptq=ptq, i=i: e.copy(out=qT.t(i)[:], in_=ptq))
            V([PT.c(pq), XIF.c()], [qTf.c(i)], lambda e, ptq=ptq, i=i: e.tensor_tensor(
                out=qTf.t(i)[:], in0=ptq, in1=XIF.t()[:], op=ALU.mult))
            V([PT.c(pq), XIB.c()], [qTb.c(i)], lambda e, ptq=ptq, i=i: e.tensor_tensor(
                out=qTb.t(i)[:], in0=ptq, in1=XIB.t()[:], op=ALU.mult))
            to_T(k_t, i, 8, kT, i, copy_eng="act")
            yield
            p = next_pa()

            def fa(e, p=p, i=i):
                ins = None
                for h in range(8):
                    ins = e.matmul(PA.t(p)[:, h * 128:(h + 1) * 128], lhsT=kT.t(i)[:, h, :], rhs=qT.t(i)[:, h, :],
                                   start=True, stop=True)
                return ins
            P([kT.c(i), qT.c(i)], [PA.c(p, 0), PA.c(p, 1)], fa)
            V([PA.c(p, 0), PA.c(p, 1), DTb.c()], [Pm.c(i)], lambda e, p=p, i=i: e.tensor_tensor(
                out=Pm.t(i)[:], in0=PA.t(p)[:].rearrange("p (a b) -> p a b", a=8), in1=DTb.t()[:], op=ALU.mult))
            yield
            p2 = next_pa()

            def fo(e, p2=p2, i=i):
                ins = None
                for h in range(8):
                    o = PA.t(p2)[:, h * 128:(h + 1) * 128]
                    e.matmul(o, lhsT=Pm.t(i)[:, h, :], rhs=v_t.t(i)[:, h, :], start=True, stop=False)
                    e.matmul(o, lhsT=qTf.t(i)[:, h, :], rhs=sf_t.t(i)[:, h, :], start=False, stop=False)
                    ins = e.matmul(o, lhsT=qTb.t(i)[:, h, :], rhs=sb_t.t(i)[:, h, :], start=False, stop=True)
                return ins
            P([Pm.c(i), v_t.c(i), qTf.c(i), qTb.c(i), sf_t.c(i), sb_t.c(i)], [PA.c(p2, 0), PA.c(p2, 1)], fo)
            A([PA.c(p2, 0), PA.c(p2, 1)], [ro.c(i)], lambda e, p2=p2, i=i: e.copy(
                out=ro.t(i)[:].rearrange("p a b -> p (a b)"), in_=PA.t(p2)[:]))
            yield
            gs_ = gst.t(i)
            V([ro.c(i)], [gst.c(i, 0)], lambda e, i=i, gs_=gs_: e.tensor_reduce(
                out=gs_[:, 0:8], in_=ro.t(i)[:], axis=AX.X, op=ALU.add))
            G([ro.c(i)], [rsq.c(i)], lambda e, i=i: e.tensor_tensor(out=rsq.t(i)[:], in0=ro.t(i)[:], in1=ro.t(i)[:], op=ALU.mult))
            V([rsq.c(i)], [gst.c(i, 1)], lambda e, i=i, gs_=gs_: e.tensor_reduce(
                out=gs_[:, 8:16], in_=rsq.t(i)[:], axis=AX.X, op=ALU.add))
            V([gst.c(i, 0)], [gst.c(i, 0)], lambda e, gs_=gs_: e.tensor_scalar(
                out=gs_[:, 0:8], in0=gs_[:, 0:8], scalar1=1.0 / 128.0, scalar2=None, op0=ALU.mult))
            V([gst.c(i, 0)], [gst.c(i, 2)], lambda e, gs_=gs_: e.tensor_tensor(
                out=gs_[:, 16:24], in0=gs_[:, 0:8], in1=gs_[:, 0:8], op=ALU.mult))
            V([gst.c(i, 1), gst.c(i, 2)], [gst.c(i, 1)], lambda e, gs_=gs_: e.scalar_tensor_tensor(
                out=gs_[:, 8:16], in0=gs_[:, 8:16], scalar=1.0 / 128.0, in1=gs_[:, 16:24], op0=ALU.mult, op1=ALU.subtract))
            yield from rsqrt(gs_[:, 8:16], [gst.c(i, 1)], 1.0, LN_EPS)
            V([ro.c(i), gst.c(i, 0)], [ro.c(i)], lambda e, i=i, gs_=gs_: e.tensor_tensor(
                out=ro.t(i)[:], in0=ro.t(i)[:], in1=gs_[:, 0:8].unsqueeze(2).to_broadcast([128, 8, 128]), op=ALU.subtract))
            V([ro.c(i), gst.c(i, 1)], [ro.c(i)], lambda e, i=i, gs_=gs_: e.tensor_tensor(
                out=ro.t(i)[:], in0=ro.t(i)[:], in1=gs_[:, 8:16].unsqueeze(2).to_broadcast([128, 8, 128]), op=ALU.mult))
            G([ro.c(i), gnw.c()], [ro.c(i)], lambda e, i=i: e.tensor_tensor(
                out=ro.t(i)[:].rearrange("p a b -> p (a b)"), in0=ro.t(i)[:].rearrange("p a b -> p (a b)"),
                in1=gnw.t()[:], op=ALU.mult))
            G([ro.c(i), g_t.c(i)], [yr.c(i)], lambda e, i=i: e.tensor_tensor(
                out=yr.t(i)[:], in0=ro.t(i)[:].rearrange("p a b -> p (a b)"), in1=g_t.t(i)[:], op=ALU.mult))
            STO([yr.c(i)], [], lambda e, i=i: e.dma_start(out=rows(YR, i), in_=yr.t(i)[:]))
            yield
            p3 = next_pa()

            def fs(e, p3=p3, i=i):
                ins = None
                for g in range(4):
                    ins = e.matmul(PA.t(p3)[:, g * 256:(g + 1) * 256], lhsT=wsT.t()[:, g, :],
                                   rhs=sv_t.t(i)[:, g * 256:(g + 1) * 256], start=True, stop=True)
                return ins
            P([wsT.c(), sv_t.c(i)], [PA.c(p3, 0), PA.c(p3, 1)], fs)

            def fy(e, p3=p3, i=i):
                ins = None
                for g in range(4):
                    ins = e.scalar_tensor_tensor(out=ys.t(i)[:, g * 256:(g + 1) * 256], in0=PA.t(p3)[:, g * 256:(g + 1) * 256],
                                                 scalar=sgbt.t()[:, g:g + 1], in1=su_t.t(i)[:, g * 256:(g + 1) * 256],
                                                 op0=ALU.add, op1=ALU.mult)
                return ins
            V([PA.c(p3, 0), PA.c(p3, 1), sgbt.c(), su_t.c(i)], [ys.c(i)], fy)
            STO([ys.c(i)], [], lambda e, i=i: e.dma_start(out=rows(YS, i), in_=ys.t(i)[:]))
        run_tiles(body, NT)
        st.close()
        if stop_after is not None and stop_after == (lt + "r"):
            tr.barrier()
            return nc, tr

        for s in range(NSEG):
            tiles = cfg.seg_tiles(s)
            is_samp = (s == cfg.NP)
            nkb = (cfg.DSEQ // 128) if is_samp else cfg.TP
            st = Stage(nc, tr, lt + "t%d" % s)
            KT = st.sb("KT", [128, 2, nkb * 128], BF16)
            Vt = st.sb("Vt", [128, nkb, 256], BF16)
            kld = st.sb("kld", [128, 256], BF16, 3)
            if is_samp:
                ksrc = CCout_bf
                koff = 0
            else:
                ksrc = AKV
                koff = tiles[0] * 128
            for kb in range(nkb):
                r0 = koff + kb * 128
                if is_samp:
                    r0 = ((kb * 128) // cfg.SSEG) * CR + (kb * 128) % cfg.SSEG
                LD([], [Vt.c(0, kb)], lambda e, kb=kb, r0=r0: e.dma_start(out=Vt.t()[:, kb, :], in_=ksrc[r0:r0 + 128, 256:512]))
                LD([], [kld.c(kb)], lambda e, kb=kb, r0=r0: e.dma_start(out=kld.t(kb)[:], in_=ksrc[r0:r0 + 128, 0:256]))
                pq = next_pt()
                transposes(lambda j: kld.t(kb)[:, j * 128:(j + 1) * 128], 2, pq, [kld.c(kb)])
                V([PT.c(pq)], [KT.c(0, kb)], lambda e, kb=kb, pq=pq: e.tensor_copy(
                    out=KT.t()[:, :, kb * 128:(kb + 1) * 128], in_=PT.t(pq)[:, 0:256].rearrange("p (a b) -> p a b", a=2)))
            aq_t = st.sb("aq", [128, D], BF16, 3)
            qTg = st.sb("qTg", [128, 8, 512], BF16, 2)
            PTs = st.sb("PTs", [128, 512], BF16, 4)
            accv = st.sb("accv", [128, 512], F32, 2)
            rden = st.sb("rden", [128, 512], F32, 2)
            yat = st.sb("yat", [128, 4, 8, 128], BF16, 2)
            ngrp = (len(tiles) + 3) // 4
            LA = 2
            pending = []
            jc = [0]
            hc = [0]
            SSL = [(0, 0), (0, 1), (1, 0)]

            def q_prep(gi, gt):
                for ti, i in enumerate(gt):
                    LD([], [aq_t.c(i)], lambda e, i=i: e.dma_start(out=aq_t.t(i)[:], in_=rows(AQ, i)))
                    pq = next_pt()
                    transposes(lambda j: aq_t.t(i)[:, j * 128:(j + 1) * 128], 8, pq, [aq_t.c(i)])
                    V([PT.c(pq)], [qTg.c(gi, ti)], lambda e, pq=pq, ti=ti, gi=gi: e.tensor_copy(
                        out=qTg.t(gi)[:, :, ti * 128:(ti + 1) * 128], in_=PT.t(pq)[:].rearrange("p (a b) -> p a b", a=8)))

            def rest(gi, gt, h, kb, j, hh):
                nq = len(gt) * 128
                g = h // 4
                pi, half = SSL[j % 3]
                sps = PA.t(pi)[:, half * 512:half * 512 + nq]
                A([PA.c(pi, half), negM_c], [PTs.c(j)], lambda e: e.activation(
                    out=PTs.t(j)[:, 0:nq], in_=sps, func=AF.Exp, bias=negM, scale=128.0 ** -0.5))
                P([PTs.c(j), Vt.c(0, kb)], [PB.c(hh)], lambda e: e.matmul(
                    PB.t(hh)[:, 0:nq], lhsT=Vt.t()[:, kb, g * 128:(g + 1) * 128], rhs=PTs.t(j)[:, 0:nq],
                    start=(kb == 0), stop=(kb == nkb - 1)))
                dps = PA.t(1)[:, 512:512 + nq]
                if kb % 2 == 1:
                    P([PTs.c(j), ("ones",)], [PA.c(1, 1)], lambda e: e.matmul(
                        dps, lhsT=ones[:], rhs=PTs.t(j)[:, 0:nq], start=(kb == 1), stop=False))
                elif kb == 0:
                    V([PTs.c(j)], [accv.c(hh)], lambda e: e.tensor_copy(out=accv.t(hh)[:, 0:nq], in_=PTs.t(j)[:, 0:nq]))
                else:
                    V([PTs.c(j), accv.c(hh)], [accv.c(hh)], lambda e: e.tensor_tensor(
                        out=accv.t(hh)[:, 0:nq], in0=accv.t(hh)[:, 0:nq], in1=PTs.t(j)[:, 0:nq], op=ALU.add))
                if kb == nkb - 1:
                    P([accv.c(hh), ("ones32",)], [PA.c(1, 1)], lambda e: e.matmul(
                        dps, lhsT=ones32[:], rhs=accv.t(hh)[:, 0:nq], start=(nkb == 1), stop=True))
                    V([PA.c(1, 1)], [rden.c(hh)], lambda e: e.reciprocal(out=rden.t(hh)[:, 0:nq], in_=dps))
                    V([PB.c(hh), rden.c(hh)], [yat.c(gi, h)], lambda e: e.tensor_tensor(
                        out=yat.t(gi)[:, 0:len(gt), h, :], in0=PB.t(hh)[:, 0:nq].rearrange("p (t q) -> p t q", q=128),
                        in1=rden.t(hh)[:, 0:nq].rearrange("p (t q) -> p t q", q=128), op=ALU.mult))
                    if h == 7:
                        for ti, i in enumerate(gt):
                            STO([yat.c(gi, hx) for hx in range(8)], [], lambda e, ti=ti, i=i: e.dma_start(
                                out=rows(YAT, i), in_=yat.t(gi)[:, ti, :, :].rearrange("p a b -> p (a b)")))

            for gi in range(ngrp):
                gt = tiles[gi * 4:(gi + 1) * 4]
                nq = len(gt) * 128
                q_prep(gi, gt)
                qcells = [qTg.c(gi, ti) for ti in range(len(gt))]
                for h in range(8):
                    g = h // 4
                    hh = hc[0]; hc[0] += 1
                    for kb in range(nkb):
                        j = jc[0]; jc[0] += 1
                        pi, half = SSL[j % 3]
                        sps = PA.t(pi)[:, half * 512:half * 512 + nq]
                        P([KT.c(0, kb)] + qcells, [PA.c(pi, half)], lambda e, kb=kb, g=g, h=h, sps=sps, gi=gi, nq=nq: e.matmul(
                            sps, lhsT=KT.t()[:, g, kb * 128:(kb + 1) * 128], rhs=qTg.t(gi)[:, h, 0:nq], start=True, stop=True))
                        pending.append((gi, gt, h, kb, j, hh))
                        if len(pending) > LA:
                            rest(*pending.pop(0))
            while pending:
                rest(*pending.pop(0))
            st.close()
            if stop_after is not None and stop_after == (lt + "t%d" % s):
                tr.barrier()
                return nc, tr

        st = Stage(nc, tr, lt + "c")
        wro = load_w(st, "wro", ret_wo[l], 8, D)
        wso = load_w(st, "wso", sg_wo[l], 8, D)
        wao = load_w(st, "wao", att_wo[l], 8, D)
        wout = load_w(st, "wout", w_out[l], 8, D)
        lw0 = load_bcast(st, "lw0", ln_w[l, 0:1, :], D)
        lb0 = load_bcast(st, "lb0", ln_b[l, 0:1, :], D)
        yr_t = st.sb("yr", [128, D], BF16, 2)
        ys_t = st.sb("ys", [128, D], BF16, 2)
        yaT = st.sb("yaT", [128, 8, 128], BF16, 2)
        gt_t = st.sb("gt", [128, 3 * D], BF16, 2)
        xr = st.sb("xr", [128, D], F32, 2)
        yrT = st.sb("yrT", [128, 8, 128], BF16, 2)
        ysT = st.sb("ysT", [128, 8, 128], BF16, 2)
        mg = st.sb("mg", [128, D], F32, 2)
        tmpm = st.sb("tmpm", [128, D], F32, 2)
        mgb = st.sb("mgb", [128, D], BF16, 2)
        mT = st.sb("mT", [128, 8, 128], BF16, 2)
        x1 = st.sb("x1", [128, D], F32, 2)
        lbf = ln_bufs(st, "ln")
        def body(i):
            LD([], [yr_t.c(i)], lambda e, i=i: e.dma_start(out=yr_t.t(i)[:], in_=rows(YR, i)))
            LD([], [ys_t.c(i)], lambda e, i=i: e.dma_start(out=ys_t.t(i)[:], in_=rows(YS, i)))
            LD([], [yaT.c(i)], lambda e, i=i: e.dma_start(out=yaT.t(i)[:].rearrange("p a b -> p (a b)"), in_=rows(YAT, i)))
            LD([], [gt_t.c(i)], lambda e, i=i: e.dma_start(out=gt_t.t(i)[:], in_=rows(GT, i)))
            LD([], [xr.c(i)], lambda e, i=i: e.dma_start(out=xr.t(i)[:], in_=rows(X, i)))
            to_T(yr_t, i, 8, yrT, i, copy_eng="act")
            to_T(ys_t, i, 8, ysT, i, copy_eng="dve")
            yield
            for bi, (srcT, wbuf) in enumerate(((yrT, wro), (ysT, wso), (yaT, wao))):
                p = next_pa()
                for cbk in range(2):
                    gemm(lambda k: srcT.t(i)[:, k, :], 8, wbuf, cbk * 512, 512,
                         PA.t(p)[:, cbk * 512:(cbk + 1) * 512], [srcT.c(i)], [PA.c(p, cbk)])
                gsl = gt_t.t(i)[:, bi * 1024:(bi + 1) * 1024]
                if bi == 0:
                    V([PA.c(p, 0), PA.c(p, 1), gt_t.c(i)], [mg.c(i)], lambda e, p=p, gsl=gsl, i=i: e.tensor_tensor(
                        out=mg.t(i)[:], in0=PA.t(p)[:], in1=gsl, op=ALU.mult))
                else:
                    V([PA.c(p, 0), PA.c(p, 1), gt_t.c(i)], [tmpm.c(i)], lambda e, p=p, gsl=gsl, i=i: e.tensor_tensor(
                        out=tmpm.t(i)[:], in0=PA.t(p)[:], in1=gsl, op=ALU.mult))
                    if bi == 1:
                        G([mg.c(i), tmpm.c(i)], [mg.c(i)], lambda e, i=i: e.tensor_tensor(
                            out=mg.t(i)[:], in0=mg.t(i)[:], in1=tmpm.t(i)[:], op=ALU.add))
                    else:
                        G([mg.c(i), tmpm.c(i)], [mgb.c(i)], lambda e, i=i: e.tensor_tensor(
                            out=mgb.t(i)[:], in0=mg.t(i)[:], in1=tmpm.t(i)[:], op=ALU.add))
                yield
            to_T(mgb, i, 8, mT, i, copy_eng="act")
            yield
            p = next_pa()
            for cbk in range(2):
                gemm(lambda k: mT.t(i)[:, k, :], 8, wout, cbk * 512, 512,
                     PA.t(p)[:, cbk * 512:(cbk + 1) * 512], [mT.c(i)], [PA.c(p, cbk)])
            V([PA.c(p, 0), PA.c(p, 1), xr.c(i)], [x1.c(i)], lambda e, p=p, i=i: e.scalar_tensor_tensor(
                out=x1.t(i)[:], in0=xr.t(i)[:], scalar=ALPHA, in1=PA.t(p)[:], op0=ALU.mult, op1=ALU.add))
            yield from layer_norm(x1.t(i)[:], [x1.c(i)], lbf, i, lw0, lb0, x1.t(i)[:], [x1.c(i)])
            STO([x1.c(i)], [], lambda e, i=i: e.dma_start(out=rows(X1, i), in_=x1.t(i)[:]))
        run_tiles(body, NT)
        st.close()
        if stop_after is not None and stop_after == (lt + "c"):
            tr.barrier()
            return nc, tr

        st = Stage(nc, tr, lt + "x")
        wxq = load_w(st, "wxq", xa_wq[l], 8, D)
        wxo = load_w(st, "wxo", xa_wo[l], 8, D)
        lw1 = load_bcast(st, "lw1", ln_w[l, 1:2, :], D)
        lb1 = load_bcast(st, "lb1", ln_b[l, 1:2, :], D)
        x1 = st.sb("x1", [128, D], F32, 2)
        x1b = st.sb("x1b", [128, D], BF16, 2)
        x1T = st.sb("x1T", [128, 8, 128], BF16, 2)
        qxT = st.sb("qxT", [128, 8, 128], BF16, 2)
        mk_t = st.sb("mk", [128, 2, 8, 128], BF16, 2)
        mv_t = st.sb("mvv", [128, 2, D], BF16, 2)
        pex = st.sb("pex", [128, 8, 128], BF16, 2)
        rdx = st.sb("rdx", [128, 4, 128], F32, 2)
        oT = st.sb("oT", [128, 8, 128], BF16, 2)
        x2 = st.sb("x2", [128, D], F32, 2)
        lbf = ln_bufs(st, "ln")
        cur_seg = [-1]
        segc = [0]
        def body(i):
            s = cfg.tile_seg(i)
            if s != cur_seg[0]:
                cur_seg[0] = s
                segc[0] += 1
                sc = segc[0]
                for kb in range(2):
                    LD([], [mk_t.c(sc, kb)], lambda e, kb=kb, sc=sc, s=s: e.dma_start(
                        out=mk_t.t(sc)[:, kb, :, :].rearrange("p a b -> p (a b)"), in_=rows(MK, s * 2 + kb)))
                    LD([], [mv_t.c(sc, kb)], lambda e, kb=kb, sc=sc, s=s: e.dma_start(out=mv_t.t(sc)[:, kb, :], in_=rows(MV, s * 2 + kb)))
            sc = segc[0]
            LD([], [x1.c(i)], lambda e, i=i: e.dma_start(out=x1.t(i)[:], in_=rows(X1, i)))
            A([x1.c(i)], [x1b.c(i)], lambda e, i=i: e.copy(out=x1b.t(i)[:], in_=x1.t(i)[:]))
            to_T(x1b, i, 8, x1T, i, copy_eng="dve")
            yield
            p = next_pa()

            def fq(e, p=p, i=i):
                ins = None
                for j in range(8):
                    for k in range(8):
                        ins = e.matmul(PA.t(p)[:, j * 128:(j + 1) * 128], lhsT=wxq.t()[:, k, j * 128:(j + 1) * 128],
                                       rhs=x1T.t(i)[:, k, :], start=(k == 0), stop=(k == 7))
                return ins
            P([wxq.c(), x1T.c(i)], [PA.c(p, 0), PA.c(p, 1)], fq)
            A([PA.c(p, 0), PA.c(p, 1)], [qxT.c(i)], lambda e, p=p, i=i: e.copy(
                out=qxT.t(i)[:].rearrange("p a b -> p (a b)"), in_=PA.t(p)[:]))
            yield
            p = next_pa()

            def fsx(e, p=p, i=i, sc=sc):
                ins = None
                for h in range(4):
                    for kb in range(2):
                        o = PA.t(p)[:, (h * 2 + kb) * 128:(h * 2 + kb + 1) * 128]
                        for c2 in range(2):
                            ins = e.matmul(o, lhsT=mk_t.t(sc)[:, kb, h * 2 + c2, :], rhs=qxT.t(i)[:, h * 2 + c2, :],
                                           start=(c2 == 0), stop=(c2 == 1))
                return ins
            P([mk_t.c(sc, 0), mk_t.c(sc, 1), qxT.c(i)], [PA.c(p, 0), PA.c(p, 1)], fsx)
            A([PA.c(p, 0), PA.c(p, 1)], [pex.c(i)], lambda e, p=p, i=i: e.activation(
                out=pex.t(i)[:].rearrange("p a b -> p (a b)"), in_=PA.t(p)[:], func=AF.Exp, scale=256.0 ** -0.5))
            yield
            pbi = i % 2

            def fden(e, i=i, pbi=pbi):
                ins = None
                for h in range(4):
                    for kb in range(2):
                        ins = e.matmul(PB.t(pbi)[:, h * 128:(h + 1) * 128], lhsT=ones[:], rhs=pex.t(i)[:, h * 2 + kb, :],
                                       start=(kb == 0), stop=(kb == 1))
                return ins
            P([pex.c(i), ("ones",)], [PB.c(pbi)], fden)
            p = next_pa()

            def fo2(e, p=p, i=i, sc=sc):
                ins = None
                for h in range(4):
                    for c2 in range(2):
                        o = PA.t(p)[:, (h * 2 + c2) * 128:(h * 2 + c2 + 1) * 128]
                        for kb in range(2):
                            ins = e.matmul(o, lhsT=mv_t.t(sc)[:, kb, (h * 2 + c2) * 128:(h * 2 + c2 + 1) * 128],
                                           rhs=pex.t(i)[:, h * 2 + kb, :], start=(kb == 0), stop=(kb == 1))
                return ins
            P([pex.c(i), mv_t.c(sc, 0), mv_t.c(sc, 1)], [PA.c(p, 0), PA.c(p, 1)], fo2)
            V([PB.c(pbi)], [rdx.c(i)], lambda e, i=i, pbi=pbi: e.reciprocal(
                out=rdx.t(i)[:].rearrange("p a b -> p (a b)"), in_=PB.t(pbi)[:]))
            V([PA.c(p, 0), PA.c(p, 1), rdx.c(i)], [oT.c(i)], lambda e, p=p, i=i: e.tensor_tensor(
                out=oT.t(i)[:].rearrange("p (h c) q -> p h c q", c=2),
                in0=PA.t(p)[:].rearrange("p (h c q) -> p h c q", h=4, c=2),
                in1=rdx.t(i)[:].unsqueeze(2).to_broadcast([128, 4, 2, 128]), op=ALU.mult))
            yield
            p = next_pa()
            for cbk in range(2):
                gemm(lambda k: oT.t(i)[:, k, :], 8, wxo, cbk * 512, 512,
                     PA.t(p)[:, cbk * 512:(cbk + 1) * 512], [oT.c(i)], [PA.c(p, cbk)])
            V([PA.c(p, 0), PA.c(p, 1), x1.c(i)], [x2.c(i)], lambda e, p=p, i=i: e.scalar_tensor_tensor(
                out=x2.t(i)[:], in0=x1.t(i)[:], scalar=ALPHA, in1=PA.t(p)[:], op0=ALU.mult, op1=ALU.add))
            yield from layer_norm(x2.t(i)[:], [x2.c(i)], lbf, i, lw1, lb1, x2.t(i)[:], [x2.c(i)])
            STO([x2.c(i)], [], lambda e, i=i: e.dma_start(out=rows(X2, i), in_=x2.t(i)[:]))
        run_tiles(body, NT)
        st.close()
        if stop_after is not None and stop_after == (lt + "x"):
            tr.barrier()
            return nc, tr

        st = Stage(nc, tr, lt + "f")
        wfi = load_w(st, "wfi", ffn_w_in[l], 8, 2 * D_FF)
        wfo = load_w(st, "wfo", ffn_w_out[l], 22, D)
        lw2 = load_bcast(st, "lw2", ln_w[l, 2:3, :], D)
        lb2 = load_bcast(st, "lb2", ln_b[l, 2:3, :], D)
        xin = st.sb("xin", [128, D], F32, 2)
        xb = st.sb("xb", [128, D], BF16, 2)
        xT = st.sb("xT", [128, 8, 128], BF16, 2)
        sa = st.sb("sa", [128, 512], F32, 2)
        act = st.sb("act", [128, D_FF], BF16, 2)
        actT = st.sb("actT", [128, 22, 128], BF16, 2)
        lbf = ln_bufs(st, "ln")
        Xdst = X if l < L - 1 else y_out
        cc_ = [0]
        def body(i):
            load_xT(st, X2, i, xin, xb, xT)
            yield
            blocks = [(c0, 512) for c0 in range(0, 2560, 512)] + [(2560, 256)]
            for (c0, w_) in blocks:
                p = next_pa()
                gemm(lambda k: xT.t(i)[:, k, :], 8, wfi, c0, w_, PA.t(p)[:, 0:w_], [xT.c(i)], [PA.c(p, 0)])
                gemm(lambda k: xT.t(i)[:, k, :], 8, wfi, D_FF + c0, w_, PA.t(p)[:, 512:512 + w_], [xT.c(i)], [PA.c(p, 1)])
                j = cc_[0]; cc_[0] += 1
                A([PA.c(p, 0)], [sa.c(j)], lambda e, p=p, j=j, w_=w_: e.activation(
                    out=sa.t(j)[:, 0:w_], in_=PA.t(p)[:, 0:w_], func=AF.Silu))
                V([sa.c(j), PA.c(p, 1)], [act.c(i, c0)], lambda e, p=p, j=j, w_=w_, c0=c0, i=i: e.tensor_tensor(
                    out=act.t(i)[:, c0:c0 + w_], in0=sa.t(j)[:, 0:w_], in1=PA.t(p)[:, 512:512 + w_], op=ALU.mult))
                yield
            acells = [act.c(i, c0) for (c0, _) in blocks]
            for r in range(3):
                nb = min(8, 22 - r * 8)
                pq = next_pt()
                transposes(lambda j, r=r: act.t(i)[:, (r * 8 + j) * 128:(r * 8 + j + 1) * 128], nb, pq, acells)
                eng = "act" if r % 2 == 0 else "dve"
                if eng == "act":
                    A([PT.c(pq)], [actT.c(i, r)], lambda e, pq=pq, r=r, nb=nb, i=i: e.copy(
                        out=actT.t(i)[:, r * 8:r * 8 + nb, :].rearrange("p a b -> p (a b)"), in_=PT.t(pq)[:, 0:nb * 128]))
                else:
                    V([PT.c(pq)], [actT.c(i, r)], lambda e, pq=pq, r=r, nb=nb, i=i: e.tensor_copy(
                        out=actT.t(i)[:, r * 8:r * 8 + nb, :].rearrange("p a b -> p (a b)"), in_=PT.t(pq)[:, 0:nb * 128]))
            p = next_pa()
            for cbk in range(2):
                gemm(lambda k: actT.t(i)[:, k, :], 22, wfo, cbk * 512, 512,
                     PA.t(p)[:, cbk * 512:(cbk + 1) * 512], [actT.c(i, r) for r in range(3)], [PA.c(p, cbk)])
            V([PA.c(p, 0), PA.c(p, 1), xin.c(i)], [xin.c(i)], lambda e, p=p, i=i: e.scalar_tensor_tensor(
                out=xin.t(i)[:], in0=xin.t(i)[:], scalar=ALPHA, in1=PA.t(p)[:], op0=ALU.mult, op1=ALU.add))
            yield from layer_norm(xin.t(i)[:], [xin.c(i)], lbf, i, lw2, lb2, xin.t(i)[:], [xin.c(i)])
            STO([xin.c(i)], [], lambda e, i=i: e.dma_start(out=rows(Xdst, i), in_=xin.t(i)[:]))
        run_tiles(body, NT)
        st.close()
        if stop_after is not None and stop_after == (lt + "f"):
            tr.barrier()
            return nc, tr
        ls.close()

    tr.barrier()
    gs.close()
    return nc, tr


def _rope_tables(cfg, core):
    def cs(pos, dim):
        inv = (ROPE_BASE ** (-np.arange(0, dim, 2, dtype=np.float32) / np.float32(dim))).astype(np.float32)
        ang = pos.astype(np.float32)[:, None] * inv[None, :]
        return np.cos(ang).astype(np.float32), np.sin(ang).astype(np.float32)
    pos = np.concatenate([np.arange(cfg.SEQ), core * cfg.SSEG + np.arange(cfg.SSEG)]).astype(np.int64)
    c, s = cs(pos, 128)
    C = np.concatenate([c, c], 1)
    S = np.concatenate([-s, s], 1)
    sc = np.float32(128.0 ** -0.5)
    tab_r = np.concatenate([C, S, C * sc, S * sc], 1).astype(np.float32)
    cr, sr = cs(pos // GRID_W, 64)
    cc, s2 = cs(pos % GRID_W, 64)
    Ca = np.concatenate([cr, cr, cc, cc], 1)
    Sa = np.concatenate([-sr, sr, -s2, s2], 1)
    tab_a = np.concatenate([Ca, Sa], 1).astype(np.float32)
    return tab_r, tab_a


def _ctab():
    m = np.arange(128, dtype=np.float32)[:, None]
    c = np.arange(128, dtype=np.float32)[None, :]
    t = np.zeros((128, 1024), np.float32)
    t[:, 0:128] = np.maximum(c - m, 0)
    t[:, 128:256] = (c >= m)
    t[:, 256:384] = np.maximum(m - c, 0)
    t[:, 384:512] = (m > c)
    t[:, 512:640] = c + 1
    t[:, 640:768] = 128 - c
    t[:, 768] = 127 - m[:, 0]
    t[:, 769] = m[:, 0]
    for j in range(4):
        t[j, 772 + j] = 1.0
    return t


def _cctab(cfg, core):
    t = np.zeros((128, 32), np.float32)
    for c2 in range(cfg.NC):
        if c2 < core:
            t[:, c2] = cfg.SSEG * (core - 1 - c2)
            t[:, 16 + c2] = 1.0
        if c2 > core:
            t[:, 8 + c2] = cfg.SSEG * (c2 - core - 1)
            t[:, 24 + c2] = 1.0
    return t


def make_in_maps(cfg, inp):
    f = lambda a: np.ascontiguousarray(np.asarray(a, dtype=np.float32))
    shared = {
        "in_ln_w": f(inp["in_ln_w"]).reshape(1, D), "in_ln_b": f(inp["in_ln_b"]).reshape(1, D),
        "w_in": f(inp["w_in"]), "b_gate": f(inp["b_gate"]),
        "ret_decay": f(np.concatenate([inp["ret_decay_f"], inp["ret_decay_b"]], axis=1)),
        "ret_gn_w": f(inp["ret_gn_w"]), "ret_wo": f(inp["ret_wo"]),
        "sg_ln_w": f(inp["sg_ln_w"]), "sg_ln_b": f(inp["sg_ln_b"]), "sg_ws": f(inp["sg_ws"]), "sg_b": f(inp["sg_b"]),
        "sg_wo": f(inp["sg_wo"]), "att_qn_w": f(inp["att_qn_w"]), "att_kn_w": f(inp["att_kn_w"]),
        "att_wo": f(inp["att_wo"]), "w_out": f(inp["w_out"]), "ln_w": f(inp["ln_w"]), "ln_b": f(inp["ln_b"]),
        "xa_wq": f(inp["xa_wq"]), "xa_wkv": f(inp["xa_wkv"]), "xa_wo": f(inp["xa_wo"]),
        "ffn_w_in": f(inp["ffn_w_in"]), "ffn_w_out": f(inp["ffn_w_out"]),
        "ctab": _ctab(), "ident": np.eye(128, dtype=np.float32),
    }
    xp, xs = f(inp["x_prompt"]), f(inp["x_sample"])
    mp, ms = f(inp["mem_prompt"]), f(inp["mem_sample"])
    maps = []
    for c in range(cfg.NC):
        m = dict(shared)
        xpc = xp[c * cfg.NP:(c + 1) * cfg.NP].reshape(cfg.NP * cfg.SEQ, D)
        xsc = xs[0, c * cfg.SSEG:(c + 1) * cfg.SSEG]
        m["x"] = np.ascontiguousarray(np.concatenate([xpc, xsc], 0))
        m["mem"] = np.ascontiguousarray(np.concatenate([mp[c * cfg.NP:(c + 1) * cfg.NP].reshape(cfg.NP * N_MEM, D), ms[0]], 0))
        tr_, ta_ = _rope_tables(cfg, c)
        m["tab_r"], m["tab_a"] = tr_, ta_
        m["cctab"] = _cctab(cfg, c)
        maps.append(m)
    return maps


def run(cfg, inp, debug=()):
    nc, tr = build(cfg, debug)
    maps = make_in_maps(cfg, inp)
    res = run_bass_kernel_spmd(nc, maps, core_ids=list(range(cfg.NC)))
    return res


def assemble(cfg, res):
    yp = np.zeros((cfg.NP * cfg.NC, cfg.SEQ, D), np.float32)
    ys = np.zeros((1, cfg.DSEQ, D), np.float32)
    for c in range(cfg.NC):
        y = res.results[c]["y"]
        yp[c * cfg.NP:(c + 1) * cfg.NP] = y[:cfg.NP * cfg.SEQ].reshape(cfg.NP, cfg.SEQ, D)
        ys[0, c * cfg.SSEG:(c + 1) * cfg.SSEG] = y[cfg.NP * cfg.SEQ:]
    return yp, ys


def kernel(**inputs):
    cfg = Cfg()
    res = run(cfg, inputs)
    return assemble(cfg, res)
```

```python
import math
from contextlib import ExitStack

import numpy as np
import concourse.bass as bass
import concourse.mybir as mybir
from concourse.bass_utils import run_bass_kernel_spmd

F32 = mybir.dt.float32
BF16 = mybir.dt.bfloat16
AF = mybir.ActivationFunctionType
ALU = mybir.AluOpType
AX = mybir.AxisListType

D = 1024
N_MEM = 256
GRID_W = 64
D_FF = 2816
D_IN = 10752
LN_EPS = 1e-5
RMS_EPS = 1e-6
ROPE_BASE = 10000.0


class Cfg:
    def __init__(self, nseg_p=4, seq=2048, sseg=2048, depth=4, ncores=8):
        self.NP = nseg_p
        self.SEQ = seq
        self.SSEG = sseg
        self.DEPTH = depth
        self.NC = ncores
        self.TP = seq // 128
        self.TS = sseg // 128
        self.NT = nseg_p * self.TP + self.TS
        self.T = self.NT * 128
        self.DSEQ = sseg * ncores
        self.ALPHA = (2 * depth) ** 0.25
        self.NSEG = nseg_p + 1

    def seg_tiles(self, s):
        if s < self.NP:
            return list(range(s * self.TP, (s + 1) * self.TP))
        return list(range(self.NP * self.TP, self.NT))

    def tile_seg(self, i):
        return min(i // self.TP, self.NP) if self.TP > 0 else self.NP

    def tab_row(self, i):
        if i < self.NP * self.TP:
            return i % self.TP
        return self.TP + (i - self.NP * self.TP)


class Tracker:
    NQ = 12

    def __init__(self, nc):
        self.nc = nc
        self.engs = {"pe": nc.tensor, "act": nc.scalar, "dve": nc.vector, "pool": nc.gpsimd, "sp": nc.sync}
        self.semh = {}
        self.ccnt = {}
        self.opidx = {}
        for e in ("pe", "act", "dve", "pool"):
            self.semh[("c", e)] = nc.alloc_semaphore("c_" + e)
            self.ccnt[e] = 0
            self.opidx[e] = 0
        self.dcnt = {}
        self.drr = {}
        for q in ("sp", "pool"):
            self.dcnt[q] = [0] * self.NQ
            self.drr[q] = 0
            for k in range(self.NQ):
                self.semh[("d", q, k)] = nc.alloc_semaphore("d_%s%d" % (q, k))
        self.semh[("cc",)] = nc.alloc_semaphore("ccsem")
        self.cccnt = 0
        self.waited = {e: {} for e in self.engs}
        self.lastw = {}
        self.readers = {}
        self.ninstr = 0

    def _wait(self, eng, t):
        semkey, val, teng, tidx = t
        if val <= 0:
            return
        if teng is not None and teng == eng:
            if eng == "pe":
                return
        if self.waited[eng].get(semkey, 0) >= val:
            return
        self.engs[eng].wait_ge(self.semh[semkey], val)
        self.ninstr += 1
        self.waited[eng][semkey] = val

    def _deps(self, eng, reads, writes):
        for c in reads:
            t = self.lastw.get(c)
            if t is not None:
                self._wait(eng, t)
        for c in writes:
            t = self.lastw.get(c)
            if t is not None:
                self._wait(eng, t)
            rd = self.readers.get(c)
            if rd:
                for t in rd.values():
                    self._wait(eng, t)

    def _commit(self, t, reads, writes):
        for c in reads:
            self.readers.setdefault(c, {})[t[0]] = t
        for c in writes:
            self.lastw[c] = t
            self.readers[c] = {}

    def op(self, eng, reads, writes, fn):
        ps = [c for c in reads if c[0] in ("PA", "PT", "PB")]
        if ps:
            reads = [c for c in reads if c[0] not in ("PA", "PT", "PB")]
            writes = list(writes) + ps
        self._deps(eng, reads, writes)
        ins = fn(self.engs[eng])
        self.ccnt[eng] += 1
        ins.then_inc(self.semh[("c", eng)], 1)
        t = (("c", eng), self.ccnt[eng], eng, self.opidx[eng])
        self.opidx[eng] += 1
        self.ninstr += 1
        self._commit(t, reads, writes)

    def dma(self, q, reads, writes, fn):
        self._deps(q, reads, writes)
        k = self.drr[q]
        self.drr[q] = (k + 1) % self.NQ
        semkey = ("d", q, k)
        self._wait(q, (semkey, self.dcnt[q][k], None, 0))
        inss = fn(self.engs[q])
        if not isinstance(inss, (list, tuple)):
            inss = [inss]
        for ins in inss:
            ins.then_inc(self.semh[semkey], 16)
        self.dcnt[q][k] += 16 * len(inss)
        self.ninstr += len(inss)
        t = (semkey, self.dcnt[q][k], None, 0)
        self._commit(t, reads, writes)

    def collective(self, reads, writes, fn):
        self._deps("pool", reads, writes)
        ins = fn(self.engs["pool"])
        ins.then_inc(self.semh[("cc",)], 1)
        self.cccnt += 1
        t = (("cc",), self.cccnt, None, 0)
        self._commit(t, reads, writes)

    def barrier(self):
        for e in self.engs:
            for ce in ("pe", "act", "dve", "pool"):
                self._wait(e, (("c", ce), self.ccnt[ce], None, 0))
            for q in ("sp", "pool"):
                for k in range(self.NQ):
                    self._wait(e, (("d", q, k), self.dcnt[q][k], None, 0))
            self._wait(e, (("cc",), self.cccnt, None, 0))
        self.lastw = {}
        self.readers = {}


class Buf:
    def __init__(self, tensors, name):
        self.ts = tensors
        self.name = name
        self.n = len(tensors)

    def t(self, i=0):
        return self.ts[i % self.n]

    def c(self, i=0, sub=0):
        return (self.name, i % self.n, sub)


class Stage:
    def __init__(self, nc, tr, tag):
        self.nc = nc
        self.tr = tr
        self.tag = tag
        self.es = ExitStack()

    def sb(self, name, shape, dtype, nslots=1):
        ts = [self.es.enter_context(self.nc.sbuf_tensor("%s_%s_%d" % (self.tag, name, k), list(shape), dtype))
              for k in range(nslots)]
        return Buf(ts, self.tag + name)

    def close(self):
        self.tr.barrier()
        self.es.close()


def build(cfg, debug=(), stop_after=None):
    nc = bass.Bass("TRN2", target_bir_lowering=False)
    tr = Tracker(nc)
    T, NT, L = cfg.T, cfg.NT, cfg.DEPTH
    NSEG = cfg.NSEG
    ALPHA = cfg.ALPHA

    def din(name, shape, dt=F32):
        return nc.dram_tensor(name, list(shape), dt, kind="ExternalInput").ap()

    def dscr(name, shape, dt):
        kind = "ExternalOutput" if name in debug else "Internal"
        return nc.dram_tensor(name, list(shape), dt, kind=kind).ap()

    x_in = din("x", [T, D])
    mem_in = din("mem", [NSEG * N_MEM, D])
    in_ln_w = din("in_ln_w", [1, D])
    in_ln_b = din("in_ln_b", [1, D])
    w_in = din("w_in", [L, D, D_IN])
    b_gate = din("b_gate", [L, 3 * D])
    ret_decay = din("ret_decay", [L, 16])
    ret_gn_w = din("ret_gn_w", [L, D])
    ret_wo = din("ret_wo", [L, D, D])
    sg_ln_w = din("sg_ln_w", [L, D])
    sg_ln_b = din("sg_ln_b", [L, D])
    sg_ws = din("sg_ws", [L, 4, 128, 128])
    sg_b = din("sg_b", [L, 4, 128])
    sg_wo = din("sg_wo", [L, D, D])
    att_qn_w = din("att_qn_w", [L, 128])
    att_kn_w = din("att_kn_w", [L, 128])
    att_wo = din("att_wo", [L, D, D])
    w_out = din("w_out", [L, D, D])
    ln_w = din("ln_w", [L, 3, D])
    ln_b = din("ln_b", [L, 3, D])
    xa_wq = din("xa_wq", [L, D, D])
    xa_wkv = din("xa_wkv", [L, D, 2 * D])
    xa_wo = din("xa_wo", [L, D, D])
    ffn_w_in = din("ffn_w_in", [L, D, 2 * D_FF])
    ffn_w_out = din("ffn_w_out", [L, D_FF, D])
    NTAB = cfg.TP + cfg.TS
    tab_r = din("tab_r", [NTAB * 128, 512])
    tab_a = din("tab_a", [NTAB * 128, 256])
    ctab = din("ctab", [128, 1024])
    cctab = din("cctab", [128, 32])
    ident_in = din("ident", [128, 128])

    y_out = nc.dram_tensor("y", [T, D], F32, kind="ExternalOutput").ap()

    X = dscr("X", [T, D], F32)
    X1 = dscr("X1", [T, D], F32)
    X2 = dscr("X2", [T, D], F32)
    RQ = dscr("RQ", [T, D], BF16)
    RK = dscr("RK", [T, D], BF16)
    RV = dscr("RV", [T, D], BF16)
    RG = dscr("RG", [T, D], BF16)
    SU = dscr("SU", [T, D], BF16)
    SV = dscr("SV", [T, D], BF16)
    AQ = dscr("AQ", [T, D], BF16)
    AKV = dscr("AKV", [T, 512], BF16)
    GT = dscr("GT", [T, 3 * D], BF16)
    SF = dscr("STF", [NT * 128, D], BF16)
    SB = dscr("STB", [NT * 128, D], BF16)
    YR = dscr("YR", [T, D], BF16)
    YS = dscr("YS", [T, D], BF16)
    YAT = dscr("YAT", [NT * 128, D], BF16)
    MK = dscr("MK", [NSEG * 2 * 128, D], BF16)
    MV = dscr("MV", [NSEG * 2 * 128, D], BF16)
    CR = max(cfg.SSEG, 1024)
    CCin = nc.dram_tensor("CCin", [CR, 256], F32)
    CCout = nc.dram_tensor("CCout", [CR * cfg.NC, 256], F32)
    CCin_bf = CCin.ap().bitcast(BF16)
    CCout_bf = CCout.ap().bitcast(BF16)
    Eloc_v = CCin.ap()[0:1024, :].rearrange("(j q) c -> j (q c)", q=4)

    def Eall_v(c2):
        return CCout.ap()[c2 * CR:c2 * CR + 1024, :].rearrange("(j q) c -> j (q c)", q=4)

    def allgather():
        tr.collective([("ccin",)], [("ccout",)], lambda e: e.collective_compute(
            "AllGather", ALU.bypass, replica_groups=[list(range(cfg.NC))],
            ins=[CCin.ap().opt()], outs=[CCout.ap().opt()]))

    gs = ExitStack()

    def gsb(name, shape, dt):
        return gs.enter_context(nc.sbuf_tensor(name, list(shape), dt))

    ident = gsb("identb", [128, 128], BF16)
    ones = gsb("onesb", [128, 128], BF16)
    ones32 = gsb("ones32", [128, 128], F32)
    ctb = gsb("ctb", [128, 1024], F32)
    cct = gsb("cct", [128, 32], F32)
    pa = [gs.enter_context(nc.psum_tensor("pa%d" % k, [128, 1024], F32)) for k in range(2)]
    pt = [gs.enter_context(nc.psum_tensor("pt%d" % k, [128, 1024], BF16)) for k in range(2)]
    pb = [gs.enter_context(nc.psum_tensor("pb%d" % k, [128, 512], F32)) for k in range(2)]
    PA = Buf(pa, "PA")
    PT = Buf(pt, "PT")
    PB = Buf(pb, "PB")
    C_PF, C_MF, C_PB, C_MB, C_RP1, C_RM = 0, 128, 256, 384, 512, 640
    C_Z = 768
    C_I4 = 772

    tr.dma("pool", [], [("ident",)], lambda e: e.dma_start(out=ident[:], in_=ident_in))
    tr.dma("sp", [], [("ctb",)], lambda e: e.dma_start(out=ctb[:], in_=ctab))
    tr.dma("sp", [], [("cct",)], lambda e: e.dma_start(out=cct[:], in_=cctab))
    tr.op("dve", [], [("ones",)], lambda e: e.memset(ones[:], 1.0))
    tr.op("dve", [], [("ones32",)], lambda e: e.memset(ones32[:], 1.0))
    tr.barrier()

    def V(reads, writes, fn):
        tr.op("dve", reads, writes, fn)

    def A(reads, writes, fn):
        tr.op("act", reads, writes, fn)

    def G(reads, writes, fn):
        tr.op("pool", reads, writes, fn)

    def P(reads, writes, fn):
        tr.op("pe", reads, writes, fn)

    def LD(reads, writes, fn):
        tr.dma("sp", reads, writes, fn)

    def STO(reads, writes, fn):
        tr.dma("sp", reads, writes, fn)

    def rows(ap2d, i):
        return ap2d[i * 128:(i + 1) * 128, :]

    def load_w(st, name, src2d, nk, ncols, col0=0):
        buf = st.sb(name, [128, nk, ncols], BF16)
        w = buf.t()
        step = max(1, 4096 // ncols)
        k = 0
        while k < nk:
            k2 = min(nk, k + step)
            tr.dma("pool", [], [buf.c()],
                   lambda e, k=k, k2=k2: e.dma_start(
                       out=w[:, k:k2, :],
                       in_=src2d[k * 128:k2 * 128, col0:col0 + ncols].rearrange("(k p) c -> p k c", p=128)))
            k = k2
        return buf

    def load_bcast(st, name, src_row, n, dt=F32):
        buf = st.sb(name, [128, n], dt)
        q = "sp" if dt == F32 else "pool"
        tr.dma(q, [], [buf.c()], lambda e: e.dma_start(out=buf.t()[:], in_=src_row.to_broadcast([128, n])))
        return buf

    def transposes(src_fn, n, ptslot, reads):
        def f(e):
            ins = None
            for j in range(n):
                ins = e.transpose(PT.t(ptslot)[:, j * 128:(j + 1) * 128], src_fn(j), ident[:])
            return ins
        P(reads + [("ident",)], [PT.c(ptslot)], f)

    def gemm(lhs_fn, nk, wbuf, col0, ncols, out_ap, reads, writes, extra=None):
        def f(e):
            ins = None
            for k in range(nk):
                ins = e.matmul(out_ap, lhsT=lhs_fn(k), rhs=wbuf.t()[:, k, col0:col0 + ncols],
                               start=(k == 0), stop=(k == nk - 1 and extra is None))
            if extra is not None:
                ins = e.matmul(out_ap, lhsT=extra[0], rhs=extra[1], start=False, stop=True)
            return ins
        P(reads + [wbuf.c()], writes, f)

    pt_rr = [0]
    pa_rr = [0]

    def next_pt():
        pt_rr[0] += 1
        return pt_rr[0]

    def next_pa():
        pa_rr[0] += 1
        return pa_rr[0]

    def to_T(src_buf, slot, nblk, dst_buf, dslot, copy_eng="dve"):
        p = next_pt()
        src = src_buf.t(slot)
        transposes(lambda j: src[:, j * 128:(j + 1) * 128], nblk, p, [src_buf.c(slot)])
        dst = dst_buf.t(dslot)
        fn = lambda e: e.tensor_copy(out=dst[:].rearrange("p a b -> p (a b)")[:, 0:nblk * 128],
                                     in_=PT.t(p)[:, 0:nblk * 128])
        if copy_eng == "act":
            fn = lambda e: e.copy(out=dst[:].rearrange("p a b -> p (a b)")[:, 0:nblk * 128],
                                  in_=PT.t(p)[:, 0:nblk * 128])
        tr.op(copy_eng, [PT.c(p)], [dst_buf.c(dslot)], fn)

    def rsqrt(ap, cells, premul, eps):
        V(cells, cells, lambda e: e.tensor_scalar(out=ap, in0=ap, scalar1=premul, scalar2=eps, op0=ALU.mult, op1=ALU.add))
        yield
        A(cells, cells, lambda e: e.activation(out=ap, in_=ap, func=AF.Sqrt))
        yield
        V(cells, cells, lambda e: e.reciprocal(out=ap, in_=ap))

    def layer_norm(src_ap, src_cells, st_bufs, slot, wrep, brep, dst_ap, dst_cells, tmp_ap=None, tmp_cells=None):
        stats, mv, rs = st_bufs
        s_t, mv_t, rs_t = stats.t(slot), mv.t(slot), rs.t(slot)
        V(src_cells, [stats.c(slot)], lambda e: (
            e.bn_stats(out=s_t[:, 0:6], in_=src_ap[:, 0:512]),
            e.bn_stats(out=s_t[:, 6:12], in_=src_ap[:, 512:1024]))[-1])
        V([stats.c(slot)], [mv.c(slot)], lambda e: e.bn_aggr(out=mv_t[:, 0:2], in_=s_t[:, 0:12]))
        V([mv.c(slot)], [rs.c(slot)], lambda e: e.tensor_copy(out=rs_t[:, 0:1], in_=mv_t[:, 1:2]))
        yield from rsqrt(rs_t[:, 0:1], [rs.c(slot)], 1.0, LN_EPS)
        t_ap = tmp_ap if tmp_ap is not None else src_ap
        t_cells = tmp_cells if tmp_cells is not None else src_cells
        V(src_cells + [mv.c(slot), rs.c(slot)], t_cells, lambda e: e.tensor_scalar(
            out=t_ap, in0=src_ap, scalar1=mv_t[:, 0:1], scalar2=rs_t[:, 0:1], op0=ALU.subtract, op1=ALU.mult))
        yield
        G(t_cells + [wrep.c()], t_cells, lambda e: e.tensor_tensor(out=t_ap, in0=t_ap, in1=wrep.t()[:], op=ALU.mult))
        G(t_cells + [brep.c()], dst_cells, lambda e: e.tensor_tensor(out=dst_ap, in0=t_ap, in1=brep.t()[:], op=ALU.add))

    def ln_bufs(st, tag, nslots=2):
        return (st.sb(tag + "st", [128, 12], F32, nslots), st.sb(tag + "mv", [128, 2], F32, nslots),
                st.sb(tag + "rs", [128, 1], F32, nslots))

    def run_tiles(body, n, width=2):
        nxt = 0
        active = []
        while nxt < n or active:
            while len(active) < width and nxt < n:
                active.append(body(nxt))
                nxt += 1
            for g in list(active):
                try:
                    next(g)
                except StopIteration:
                    active.remove(g)

    st = Stage(nc, tr, "l0")
    wrep = load_bcast(st, "w", in_ln_w, D)
    brep = load_bcast(st, "b", in_ln_b, D)
    xin = st.sb("xin", [128, D], F32, 4)
    xo = st.sb("xo", [128, D], F32, 2)
    lb = ln_bufs(st, "ln")
    for i in range(NT):
        LD([], [xin.c(i)], lambda e, i=i: e.dma_start(out=xin.t(i)[:], in_=rows(x_in, i)))
        for _ in layer_norm(xin.t(i)[:], [xin.c(i)], lb, i, wrep, brep, xo.t(i)[:], [xo.c(i)]):
            pass
        STO([xo.c(i)], [], lambda e, i=i: e.dma_start(out=rows(X, i), in_=xo.t(i)[:]))
    st.close()
    if stop_after is not None and stop_after == ("l0"):
        tr.barrier()
        return nc, tr

    def load_xT(st, Xsrc, i, xin, xb, xT, do_load=True):
        if do_load:
            LD([], [xin.c(i)], lambda e: e.dma_start(out=xin.t(i)[:], in_=rows(Xsrc, i)))
        A([xin.c(i)], [xb.c(i)], lambda e: e.copy(out=xb.t(i)[:], in_=xin.t(i)[:]))
        to_T(xb, i, 8, xT, i)

    for l in range(L):
        lt = "y%d" % l
        ls = Stage(nc, tr, lt + "p")
        lgt = ls.sb("lg", [128, 16], F32)
        tmp16 = ls.sb("t16", [128, 16], F32)
        DTb = ls.sb("DT", [128, 8, 128], F32)
        XIF = ls.sb("XIF", [128, 8, 128], F32)
        XIB = ls.sb("XIB", [128, 8, 128], F32)
        ZT = ls.sb("ZT", [128, 16], F32)
        CDt = ls.sb("CD", [128, 16], F32)
        CF = ls.sb("CF", [128, 8, 8], F32)
        CB = ls.sb("CB", [128, 8, 8], F32)
        wq_rep = ls.sb("wqr", [128, 128], F32)
        wk_rep = ls.sb("wkr", [128, 128], F32)
        mx = ls.sb("mx", [128, 4], F32)
        wq_sw = ls.sb("wqs", [128, 128], F32)
        wk_sw = ls.sb("wks", [128, 128], F32)
        lsd = Stage(nc, tr, lt + "q")
        dtmp = lsd.sb("dtmp", [128, 8, 128], F32)
        tr.dma("sp", [], [tmp16.c()], lambda e: e.dma_start(
            out=tmp16.t()[:], in_=ret_decay[l:l + 1, :].to_broadcast([128, 16])))
        A([tmp16.c()], [lgt.c()], lambda e: e.activation(out=lgt.t()[:], in_=tmp16.t()[:], func=AF.Exp, scale=-1.0))
        V([lgt.c()], [tmp16.c()], lambda e: e.tensor_scalar(
            out=tmp16.t()[:], in0=lgt.t()[:], scalar1=1.0, scalar2=None, op0=ALU.add))
        A([tmp16.c()], [lgt.c()], lambda e: e.activation(out=lgt.t()[:], in_=tmp16.t()[:], func=AF.Ln))
        V([lgt.c()], [lgt.c()], lambda e: e.tensor_scalar(
            out=lgt.t()[:], in0=lgt.t()[:], scalar1=-1.0, scalar2=None, op0=ALU.mult))
        lg = lgt.t()
        for h in range(8):
            A([lgt.c(), ("ctb",)], [DTb.c()], lambda e, h=h: e.activation(
                out=DTb.t()[:, h, :], in_=ctb[:, C_PF:C_PF + 128], func=AF.Exp, scale=lg[:, h:h + 1]))
            A([lgt.c(), ("ctb",)], [dtmp.c()], lambda e, h=h: e.activation(
                out=dtmp.t()[:, h, :], in_=ctb[:, C_PB:C_PB + 128], func=AF.Exp, scale=lg[:, 8 + h:9 + h]))
        V([DTb.c(), ("ctb",)], [DTb.c()], lambda e: e.tensor_tensor(
            out=DTb.t()[:], in0=DTb.t()[:], in1=ctb[:, C_MF:C_MF + 128].unsqueeze(1).to_broadcast([128, 8, 128]),
            op=ALU.mult))
        V([dtmp.c(), ("ctb",)], [dtmp.c()], lambda e: e.tensor_tensor(
            out=dtmp.t()[:], in0=dtmp.t()[:], in1=ctb[:, C_MB:C_MB + 128].unsqueeze(1).to_broadcast([128, 8, 128]),
            op=ALU.mult))
        V([DTb.c(), dtmp.c()], [DTb.c()], lambda e: e.tensor_tensor(
            out=DTb.t()[:], in0=DTb.t()[:], in1=dtmp.t()[:], op=ALU.add))
        for h in range(8):
            A([lgt.c(), ("ctb",)], [XIF.c()], lambda e, h=h: e.activation(
                out=XIF.t()[:, h, :], in_=ctb[:, C_RP1:C_RP1 + 128], func=AF.Exp, scale=lg[:, h:h + 1]))
            A([lgt.c(), ("ctb",)], [XIB.c()], lambda e, h=h: e.activation(
                out=XIB.t()[:, h, :], in_=ctb[:, C_RM:C_RM + 128], func=AF.Exp, scale=lg[:, 8 + h:9 + h]))
        A([lgt.c(), ("ctb",)], [ZT.c()], lambda e: e.activation(
            out=ZT.t()[:, 0:8], in_=lg[:, 0:8], func=AF.Exp, scale=ctb[:, C_Z:C_Z + 1]))
        A([lgt.c(), ("ctb",)], [ZT.c()], lambda e: e.activation(
            out=ZT.t()[:, 8:16], in_=lg[:, 8:16], func=AF.Exp, scale=ctb[:, C_Z + 1:C_Z + 2]))
        A([lgt.c()], [CDt.c()], lambda e: e.activation(out=CDt.t()[:], in_=lg[:], func=AF.Exp, scale=128.0))
        for c2 in range(cfg.NC):
            A([lgt.c(), ("cct",)], [CF.c()], lambda e, c2=c2: e.activation(
                out=CF.t()[:, c2, :], in_=lg[:, 0:8], func=AF.Exp, scale=cct[:, c2:c2 + 1]))
            A([lgt.c(), ("cct",)], [CB.c()], lambda e, c2=c2: e.activation(
                out=CB.t()[:, c2, :], in_=lg[:, 8:16], func=AF.Exp, scale=cct[:, 8 + c2:9 + c2]))
        V([CF.c(), ("cct",)], [CF.c()], lambda e: e.tensor_tensor(
            out=CF.t()[:], in0=CF.t()[:], in1=cct[:, 16:24].unsqueeze(2).to_broadcast([128, 8, 8]), op=ALU.mult))
        V([CB.c(), ("cct",)], [CB.c()], lambda e: e.tensor_tensor(
            out=CB.t()[:], in0=CB.t()[:], in1=cct[:, 24:32].unsqueeze(2).to_broadcast([128, 8, 8]), op=ALU.mult))
        tr.dma("sp", [], [wq_rep.c()], lambda e: e.dma_start(
            out=wq_rep.t()[:], in_=att_qn_w[l:l + 1, :].to_broadcast([128, 128])))
        tr.dma("sp", [], [wk_rep.c()], lambda e: e.dma_start(
            out=wk_rep.t()[:], in_=att_kn_w[l:l + 1, :].to_broadcast([128, 128])))
        V([wq_rep.c()], [mx.c(0, 1)], lambda e: e.tensor_reduce(
            out=mx.t()[:, 0:1], in_=wq_rep.t()[:], axis=AX.X, op=ALU.max, apply_absolute_value=True))
        V([wk_rep.c()], [mx.c(0, 2)], lambda e: e.tensor_reduce(
            out=mx.t()[:, 1:2], in_=wk_rep.t()[:], axis=AX.X, op=ALU.max, apply_absolute_value=True))
        V([mx.c(0, 1), mx.c(0, 2)], [mx.c(0, 3)], lambda e: e.tensor_scalar(
            out=mx.t()[:, 2:3], in0=mx.t()[:, 0:1], scalar1=mx.t()[:, 1:2], scalar2=-math.sqrt(128.0),
            op0=ALU.mult, op1=ALU.mult))
        negM = mx.t()[:, 2:3]
        negM_c = mx.c(0, 3)
        for (src, dst) in ((wq_rep, wq_sw), (wk_rep, wk_sw)):
            s4 = src.t()[:].rearrange("p (a h b) -> p a h b", a=2, h=2)
            d4 = dst.t()[:].rearrange("p (a h b) -> p a h b", a=2, h=2)
            V([src.c()], [dst.c()], lambda e, s4=s4, d4=d4: (
                e.tensor_copy(out=d4[:, :, 0, :], in_=s4[:, :, 1, :]),
                e.tensor_copy(out=d4[:, :, 1, :], in_=s4[:, :, 0, :]))[-1])
        lsd.close()

        st = Stage(nc, tr, lt + "m")
        wkv = load_w(st, "wkv", xa_wkv[l], 8, 2 * D)
        xin = st.sb("xin", [128, D], F32, 2)
        xb = st.sb("xb", [128, D], BF16, 2)
        xT = st.sb("xT", [128, 8, 128], BF16, 2)
        kb_ = st.sb("kb", [128, D], BF16, 2)
        kT = st.sb("kT", [128, 8, 128], BF16, 2)
        vb_ = st.sb("vb", [128, D], BF16, 2)
        for i in range(NSEG * 2):
            load_xT(st, mem_in, i, xin, xb, xT)
            for half, (dst, eng) in enumerate(((kb_, "act"), (vb_, "dve"))):
                p = next_pa()
                for cbk in range(2):
                    gemm(lambda k: xT.t(i)[:, k, :], 8, wkv, half * D + cbk * 512, 512,
                         PA.t(p)[:, cbk * 512:(cbk + 1) * 512], [xT.c(i)], [PA.c(p, cbk)])
                if eng == "act":
                    A([PA.c(p, 0), PA.c(p, 1)], [dst.c(i)], lambda e, p=p, dst=dst: e.copy(out=dst.t(i)[:], in_=PA.t(p)[:]))
                else:
                    V([PA.c(p, 0), PA.c(p, 1)], [dst.c(i)], lambda e, p=p, dst=dst: e.tensor_copy(out=dst.t(i)[:], in_=PA.t(p)[:]))
            to_T(kb_, i, 8, kT, i, copy_eng="act")
            STO([kT.c(i)], [], lambda e, i=i: e.dma_start(out=rows(MK, i), in_=kT.t(i)[:].rearrange("p a b -> p (a b)")))
            STO([vb_.c(i)], [], lambda e, i=i: e.dma_start(out=rows(MV, i), in_=vb_.t(i)[:]))
        st.close()
        if stop_after is not None and stop_after == (lt + "m"):
            tr.barrier()
            return nc, tr

        st = Stage(nc, tr, lt + "a")
        wa = load_w(st, "w", w_in[l], 8, 4096, 0)
        xin = st.sb("xin", [128, D], F32, 4)
        xb = st.sb("xb", [128, D], BF16, 2)
        xT = st.sb("xT", [128, 8, 128], BF16, 2)
        tabr = st.sb("tabr", [128, 512], F32, 4)
        xs = st.sb("xs", [128, 8, 128], F32, 2)
        t1 = st.sb("t1", [128, 8, 128], F32, 2)
        u = st.sb("u", [128, 8, 128], F32, 2)
        ob = st.sb("ob", [128, D], BF16, 4)
        oc = [0]

        def rope_ret(i, p, tab_off, dstD):
            k = oc[0]
            oc[0] += 1
            tb = tabr.t(i)
            Cb = tb[:, tab_off:tab_off + 128].unsqueeze(1).to_broadcast([128, 8, 128])
            Slo = tb[:, tab_off + 128:tab_off + 192].unsqueeze(1).to_broadcast([128, 8, 64])
            Shi = tb[:, tab_off + 192:tab_off + 256].unsqueeze(1).to_broadcast([128, 8, 64])
            A([PA.c(p, 0), PA.c(p, 1)], [xs.c(k)], lambda e: e.copy(
                out=xs.t(k)[:].rearrange("p a b -> p (a b)"), in_=PA.t(p)[:]))
            V([xs.c(k), tabr.c(i)], [t1.c(k)], lambda e: e.tensor_tensor(out=t1.t(k)[:], in0=xs.t(k)[:], in1=Cb, op=ALU.mult))
            G([xs.c(k), tabr.c(i)], [u.c(k)], lambda e: (
                e.tensor_tensor(out=u.t(k)[:, :, 0:64], in0=xs.t(k)[:, :, 64:128], in1=Slo, op=ALU.mult),
                e.tensor_tensor(out=u.t(k)[:, :, 64:128], in0=xs.t(k)[:, :, 0:64], in1=Shi, op=ALU.mult))[-1])
            V([t1.c(k), u.c(k)], [ob.c(k)], lambda e: e.tensor_tensor(
                out=ob.t(k)[:].rearrange("p (a b) -> p a b", a=8), in0=t1.t(k)[:], in1=u.t(k)[:], op=ALU.add))
            STO([ob.c(k)], [], lambda e: e.dma_start(out=rows(dstD, i), in_=ob.t(k)[:]))

        def pre(t):
            LD([], [xin.c(t)], lambda e: e.dma_start(out=xin.t(t)[:], in_=rows(X, t)))
            LD([], [tabr.c(t)], lambda e: e.dma_start(out=tabr.t(t)[:], in_=rows(tab_r, cfg.tab_row(t))))
        for t in range(min(2, NT)):
            pre(t)

        def body(i):
            if i + 2 < NT:
                pre(i + 2)
            load_xT(st, X, i, xin, xb, xT, do_load=False)
            yield
            for grp in range(4):
                p = next_pa()
                for cbk in range(2):
                    gemm(lambda k: xT.t(i)[:, k, :], 8, wa, grp * 1024 + cbk * 512, 512,
                         PA.t(p)[:, cbk * 512:(cbk + 1) * 512], [xT.c(i)], [PA.c(p, cbk)])
                if grp == 0:
                    rope_ret(i, p, 0, RQ)
                elif grp == 1:
                    rope_ret(i, p, 256, RK)
                else:
                    k = oc[0]
                    oc[0] += 1
                    fn = AF.Copy if grp == 2 else AF.Silu
                    A([PA.c(p, 0), PA.c(p, 1)], [ob.c(k)], lambda e, p=p, k=k, fn=fn: e.activation(
                        out=ob.t(k)[:], in_=PA.t(p)[:], func=fn))
                    dstD = RV if grp == 2 else RG
                    STO([ob.c(k)], [], lambda e, k=k, dstD=dstD, i=i: e.dma_start(out=rows(dstD, i), in_=ob.t(k)[:]))
                yield
        run_tiles(body, NT)
        st.close()
        if stop_after is not None and stop_after == (lt + "a"):
            tr.barrier()
            return nc, tr

        st = Stage(nc, tr, lt + "b")
        wb = load_w(st, "w", w_in[l], 8, 3584, 4096)
        sgw = load_bcast(st, "sgw", sg_ln_w[l:l + 1, :], D)
        sgb = load_bcast(st, "sgb", sg_ln_b[l:l + 1, :], D)
        xin = st.sb("xin", [128, D], F32, 4)
        xb = st.sb("xb", [128, D], BF16, 2)
        xT = st.sb("xT", [128, 8, 128], BF16, 2)
        taba = st.sb("taba", [128, 256], F32, 4)
        cw = st.sb("cw", [128, 4, 128], F32, 2)
        xs = st.sb("xs", [128, 8, 128], F32, 4)
        sq = st.sb("sq", [128, 8, 128], F32, 4)
        t1 = st.sb("t1", [128, 8, 128], F32, 4)
        u = st.sb("u", [128, 8, 128], F32, 4)
        ss = st.sb("ss", [128, 16], F32, 4)
        svf = st.sb("svf", [128, D], F32, 2)
        ob = st.sb("ob", [128, D], BF16, 4)
        okv = st.sb("okv", [128, 512], BF16, 2)
        lb = ln_bufs(st, "ln")
        oc = [0]

        def axial(i, k, nh, Ci, Si, src3, dst3, rst, rs_off, cells_in, cells_out):
            cwt = cw.t(i)
            G(cells_in, [sq.c(k)], lambda e: e.tensor_tensor(out=sq.t(k)[:, 0:nh, :], in0=src3, in1=src3, op=ALU.mult))
            V([sq.c(k)], [ss.c(k, rs_off)], lambda e: e.tensor_reduce(
                out=rst[:, rs_off:rs_off + nh], in_=sq.t(k)[:, 0:nh, :], axis=AX.X, op=ALU.add))
            yield from rsqrt(rst[:, rs_off:rs_off + nh], [ss.c(k, rs_off)], 1.0 / 128.0, RMS_EPS)
            Cb = cwt[:, Ci, :].unsqueeze(1).to_broadcast([128, nh, 128])
            V(cells_in + [cw.c(i)], [t1.c(k)], lambda e: e.tensor_tensor(out=t1.t(k)[:, 0:nh, :], in0=src3, in1=Cb, op=ALU.mult))
            s5 = src3.rearrange("p h (a c b) -> p h a c b", a=2, c=2)
            u5 = u.t(k)[:, 0:nh, :].rearrange("p h (a c b) -> p h a c b", a=2, c=2)
            S4 = cwt[:, Si, :].rearrange("p (a c b) -> p a c b", a=2, c=2)

            def fu2(e):
                ins = None
                for a in range(2):
                    for c in range(2):
                        ins = e.tensor_tensor(out=u5[:, :, a, c, :], in0=s5[:, :, a, 1 - c, :],
                                              in1=S4[:, a, c, :].unsqueeze(1).to_broadcast([128, nh, 32]), op=ALU.mult)
                return ins
            G(cells_in + [cw.c(i)], [u.c(k)], fu2)
            V([t1.c(k), u.c(k)], [t1.c(k)], lambda e: e.tensor_tensor(
                out=t1.t(k)[:, 0:nh, :], in0=t1.t(k)[:, 0:nh, :], in1=u.t(k)[:, 0:nh, :], op=ALU.add))
            V([t1.c(k), ss.c(k, rs_off)], cells_out, lambda e: e.tensor_tensor(
                out=dst3, in0=t1.t(k)[:, 0:nh, :],
                in1=rst[:, rs_off:rs_off + nh].unsqueeze(2).to_broadcast([128, nh, 128]), op=ALU.mult))

        def pre(t):
            LD([], [xin.c(t)], lambda e: e.dma_start(out=xin.t(t)[:], in_=rows(X, t)))
            LD([], [taba.c(t)], lambda e: e.dma_start(out=taba.t(t)[:], in_=rows(tab_a, cfg.tab_row(t))))
        for t in range(min(2, NT)):
            pre(t)

        def body(i):
            if i + 2 < NT:
                pre(i + 2)
            load_xT(st, X, i, xin, xb, xT, do_load=False)
            ta = taba.t(i)
            G([taba.c(i), wq_rep.c(), wq_sw.c(), wk_rep.c(), wk_sw.c()], [cw.c(i)], lambda e, i=i, ta=ta: (
                e.tensor_tensor(out=cw.t(i)[:, 0, :], in0=ta[:, 0:128], in1=wq_rep.t()[:], op=ALU.mult),
                e.tensor_tensor(out=cw.t(i)[:, 1, :], in0=ta[:, 128:256], in1=wq_sw.t()[:], op=ALU.mult),
                e.tensor_tensor(out=cw.t(i)[:, 2, :], in0=ta[:, 0:128], in1=wk_rep.t()[:], op=ALU.mult),
                e.tensor_tensor(out=cw.t(i)[:, 3, :], in0=ta[:, 128:256], in1=wk_sw.t()[:], op=ALU.mult))[-1])
            yield
            p = next_pa()
            for cbk in range(2):
                gemm(lambda k: xT.t(i)[:, k, :], 8, wb, cbk * 512, 512,
                     PA.t(p)[:, cbk * 512:(cbk + 1) * 512], [xT.c(i)], [PA.c(p, cbk)])
            k = oc[0]; oc[0] += 1
            A([PA.c(p, 0), PA.c(p, 1)], [ob.c(k)], lambda e, p=p, k=k: e.activation(
                out=ob.t(k)[:], in_=PA.t(p)[:], func=AF.Gelu_apprx_tanh))
            STO([ob.c(k)], [], lambda e, k=k, i=i: e.dma_start(out=rows(SU, i), in_=ob.t(k)[:]))
            yield
            p = next_pa()
            for cbk in range(2):
                gemm(lambda k: xT.t(i)[:, k, :], 8, wb, 1024 + cbk * 512, 512,
                     PA.t(p)[:, cbk * 512:(cbk + 1) * 512], [xT.c(i)], [PA.c(p, cbk)])
            A([PA.c(p, 0), PA.c(p, 1)], [svf.c(i)], lambda e, p=p, i=i: e.activation(
                out=svf.t(i)[:], in_=PA.t(p)[:], func=AF.Gelu_apprx_tanh))
            k = oc[0]; oc[0] += 1
            yield from layer_norm(svf.t(i)[:], [svf.c(i)], lb, i, sgw, sgb, ob.t(k)[:], [ob.c(k)])
            STO([ob.c(k)], [], lambda e, k=k, i=i: e.dma_start(out=rows(SV, i), in_=ob.t(k)[:]))
            yield
            p = next_pa()
            for cbk in range(2):
                gemm(lambda k: xT.t(i)[:, k, :], 8, wb, 2048 + cbk * 512, 512,
                     PA.t(p)[:, cbk * 512:(cbk + 1) * 512], [xT.c(i)], [PA.c(p, cbk)])
            k = oc[0]; oc[0] += 1
            kx = (i % 2) * 2
            A([PA.c(p, 0), PA.c(p, 1)], [xs.c(kx)], lambda e, p=p, kx=kx: e.copy(
                out=xs.t(kx)[:].rearrange("p a b -> p (a b)"), in_=PA.t(p)[:]))
            yield from axial(i, kx, 8, 0, 1, xs.t(kx)[:], ob.t(k)[:].rearrange("p (a b) -> p a b", a=8), ss.t(kx), 0,
                  [xs.c(kx)], [ob.c(k)])
            STO([ob.c(k)], [], lambda e, k=k, i=i: e.dma_start(out=rows(AQ, i), in_=ob.t(k)[:]))
            yield
            p = next_pa()
            gemm(lambda k: xT.t(i)[:, k, :], 8, wb, 3072, 512, PA.t(p)[:, 0:512], [xT.c(i)], [PA.c(p, 0)])
            k2 = (i % 2) * 2 + 1
            A([PA.c(p, 0)], [xs.c(k2)], lambda e, p=p, k2=k2: e.copy(
                out=xs.t(k2)[:].rearrange("p a b -> p (a b)")[:, 0:512], in_=PA.t(p)[:, 0:512]))
            yield from axial(i, k2, 2, 2, 3, xs.t(k2)[:, 0:2, :], okv.t(i)[:, 0:256].rearrange("p (a b) -> p a b", a=2), ss.t(k2), 8,
                  [xs.c(k2)], [okv.c(i, 0)])
            V([xs.c(k2)], [okv.c(i, 1)], lambda e, k2=k2, i=i: e.tensor_copy(
                out=okv.t(i)[:, 256:512], in_=xs.t(k2)[:, 2:4, :].rearrange("p a b -> p (a b)")))
            STO([okv.c(i, 0), okv.c(i, 1)], [], lambda e, i=i: e.dma_start(out=rows(AKV, i), in_=okv.t(i)[:]))
            yield
        run_tiles(body, NT)
        st.close()
        if stop_after is not None and stop_after == (lt + "b"):
            tr.barrier()
            return nc, tr

        st = Stage(nc, tr, lt + "g")
        wg = load_w(st, "w", w_in[l], 8, 3 * D, 4096 + 3584)
        bgr = st.sb("bg", [1, 3 * D], BF16)
        tr.dma("pool", [], [bgr.c()], lambda e: e.dma_start(out=bgr.t()[:], in_=b_gate[l:l + 1, :]))
        xin = st.sb("xin", [128, D], F32, 4)
        xb = st.sb("xb", [128, D], BF16, 2)
        xT = st.sb("xT", [128, 8, 128], BF16, 2)
        og = st.sb("og", [128, 3 * D], BF16, 2)
        def pre(t):
            LD([], [xin.c(t)], lambda e: e.dma_start(out=xin.t(t)[:], in_=rows(X, t)))
        for t in range(min(2, NT)):
            pre(t)

        def body(i):
            if i + 2 < NT:
                pre(i + 2)
            load_xT(st, X, i, xin, xb, xT, do_load=False)
            for g3 in range(3):
                p = next_pa()
                for cbk in range(2):
                    c0 = g3 * 1024 + cbk * 512
                    gemm(lambda k: xT.t(i)[:, k, :], 8, wg, c0, 512,
                         PA.t(p)[:, cbk * 512:(cbk + 1) * 512], [xT.c(i), bgr.c(), ("ones",)], [PA.c(p, cbk)],
                         extra=(ones[0:1, :], bgr.t()[0:1, c0:c0 + 512]))
                A([PA.c(p, 0), PA.c(p, 1)], [og.c(i, g3)], lambda e, p=p, i=i, g3=g3: e.activation(
                    out=og.t(i)[:, g3 * 1024:(g3 + 1) * 1024], in_=PA.t(p)[:], func=AF.Sigmoid))
                yield
            STO([og.c(i, 0), og.c(i, 1), og.c(i, 2)], [], lambda e, i=i: e.dma_start(out=rows(GT, i), in_=og.t(i)[:]))
        run_tiles(body, NT)
        st.close()
        if stop_after is not None and stop_after == (lt + "g"):
            tr.barrier()
            return nc, tr

        st = Stage(nc, tr, lt + "s")
        rk_t = st.sb("rk", [128, 8, 128], BF16, 4)
        rv_t = st.sb("rv", [128, 8, 128], BF16, 4)
        kz = st.sb("kz", [128, 8, 128], BF16, 4)
        S = st.sb("S", [128, 8, 128], F32, 4)
        Sb16 = st.sb("Sb", [128, 8, 128], BF16, 4)
        eall = st.sb("eall", [128, 8, 128], F32, 2)
        it = [0]

        def run_gens(gens, width):
            gens = list(gens)
            active = []
            while gens or active:
                while len(active) < width and gens:
                    active.append(gens.pop(0))
                for g in list(active):
                    try:
                        next(g)
                    except StopIteration:
                        active.remove(g)

        def scan(tiles, direction, store, sl, zero=True):
            order = tiles if direction == 0 else tiles[::-1]
            zoff = 8 * direction
            Sd = SF if direction == 0 else SB
            S3 = S.t(sl)[:]
            if zero:
                V([], [S.c(sl)], lambda e: e.memset(S3.rearrange("p a b -> p (a b)"), 0.0))
            for i in order:
                j = it[0]; it[0] += 1
                LD([], [rk_t.c(j)], lambda e: e.dma_start(out=rk_t.t(j)[:].rearrange("p a b -> p (a b)"), in_=rows(RK, i)))
                LD([], [rv_t.c(j)], lambda e: e.dma_start(out=rv_t.t(j)[:].rearrange("p a b -> p (a b)"), in_=rows(RV, i)))
                if store:
                    A([S.c(sl)], [Sb16.c(j)], lambda e: e.copy(out=Sb16.t(j)[:], in_=S3))
                    STO([Sb16.c(j)], [], lambda e: e.dma_start(out=rows(Sd, i), in_=Sb16.t(j)[:].rearrange("p a b -> p (a b)")))
                V([rk_t.c(j), ZT.c()], [kz.c(j)], lambda e: e.tensor_tensor(
                    out=kz.t(j)[:], in0=rk_t.t(j)[:],
                    in1=ZT.t()[:, zoff:zoff + 8].unsqueeze(2).to_broadcast([128, 8, 128]), op=ALU.mult))
                V([S.c(sl), CDt.c()], [S.c(sl)], lambda e: e.tensor_tensor(
                    out=S3, in0=S3, in1=CDt.t()[:, zoff:zoff + 8].unsqueeze(2).to_broadcast([128, 8, 128]), op=ALU.mult))
                yield
                p = next_pa()

                def f(e):
                    ins = None
                    for h in range(8):
                        ins = e.matmul(PA.t(p)[:, h * 128:(h + 1) * 128], lhsT=kz.t(j)[:, h, :], rhs=rv_t.t(j)[:, h, :],
                                       start=True, stop=True)
                    return ins
                P([kz.c(j), rv_t.c(j)], [PA.c(p, 0), PA.c(p, 1)], f)
                V([S.c(sl), PA.c(p, 0), PA.c(p, 1)], [S.c(sl)], lambda e: e.tensor_tensor(
                    out=S3, in0=S3, in1=PA.t(p)[:].rearrange("p (a b) -> p a b", a=8), op=ALU.add))
                yield

        run_gens([scan(cfg.seg_tiles(s_), d, True, (2 * s_ + d) % 4) for s_ in range(cfg.NP) for d in range(2)], 4)
        stl = cfg.seg_tiles(cfg.NP)
        run_gens([scan(stl, d, False, d) for d in range(2)], 2)
        for d in range(2):
            tr.dma("sp", [S.c(d)], [("ccin",)], lambda e, d=d: e.dma_start(
                out=Eloc_v[d * 128:(d + 1) * 128, :], in_=S.t(d)[:].rearrange("p a b -> p (a b)")))
        allgather()
        for d in range(2):
            S3 = S.t(d)[:]
            V([], [S.c(d)], lambda e: e.memset(S3.rearrange("p a b -> p (a b)"), 0.0))
            CO = CF if d == 0 else CB
            for c2 in range(cfg.NC):
                j = it[0]; it[0] += 1
                LD([("ccout",)], [eall.c(j)], lambda e, c2=c2, j=j, d=d: e.dma_start(
                    out=eall.t(j)[:].rearrange("p a b -> p (a b)"),
                    in_=Eall_v(c2)[d * 128:(d + 1) * 128, :]))
                V([eall.c(j), CO.c()], [eall.c(j)], lambda e, c2=c2, j=j, CO=CO: e.tensor_tensor(
                    out=eall.t(j)[:], in0=eall.t(j)[:],
                    in1=CO.t()[:, c2, :].unsqueeze(2).to_broadcast([128, 8, 128]), op=ALU.mult))
                V([eall.c(j), S.c(d)], [S.c(d)], lambda e, j=j, S3=S3: e.tensor_tensor(out=S3, in0=S3, in1=eall.t(j)[:], op=ALU.add))
        run_gens([scan(stl, d, True, d, zero=False) for d in range(2)], 2)
        st.close()
        if stop_after is not None and stop_after == (lt + "s"):
            tr.barrier()
            return nc, tr

        s0 = cfg.NP * cfg.TP * 128
        tr.dma("sp", [], [("ccin",)], lambda e: e.dma_start(out=CCin_bf[0:cfg.SSEG, :], in_=AKV[s0:s0 + cfg.SSEG, :]))
        allgather()
        tr.barrier()

        st = Stage(nc, tr, lt + "r")
        gnw = load_bcast(st, "gnw", ret_gn_w[l:l + 1, :], D)
        wsT = st.sb("wsT", [128, 4, 128], BF16)
        wsl = st.sb("wsl", [128, 4, 128], BF16)
        tr.dma("pool", [], [wsl.c()], lambda e: e.dma_start(out=wsl.t()[:], in_=sg_ws[l].rearrange("g c m -> c g m")))
        pq = next_pt()
        transposes(lambda j: wsl.t()[:, j, :], 4, pq, [wsl.c()])
        V([PT.c(pq)], [wsT.c()], lambda e: e.tensor_copy(out=wsT.t()[:].rearrange("p a b -> p (a b)"), in_=PT.t(pq)[:, 0:512]))
        sgbt = st.sb("sgbt", [128, 4], F32)
        sgb4 = st.sb("sgb4", [4, 128], F32)
        tr.dma("sp", [], [sgb4.c()], lambda e: e.dma_start(out=sgb4.t()[:], in_=sg_b[l]))
        P([sgb4.c(), ("ctb",)], [PB.c(0)], lambda e: e.matmul(
            PB.t(0)[:, 0:4], lhsT=sgb4.t()[0:4, :], rhs=ctb[0:4, C_I4:C_I4 + 4], start=True, stop=True))
        V([PB.c(0)], [sgbt.c()], lambda e: e.tensor_copy(out=sgbt.t()[:], in_=PB.t(0)[:, 0:4]))
        q_t = st.sb("q", [128, D], BF16, 2)
        k_t = st.sb("k", [128, D], BF16, 2)
        v_t = st.sb("v", [128, 8, 128], BF16, 2)
        g_t = st.sb("g", [128, D], BF16, 2)
        su_t = st.sb("su", [128, D], BF16, 2)
        sv_t = st.sb("sv", [128, D], BF16, 2)
        sf_t = st.sb("sf", [128, 8, 128], BF16, 2)
        sb_t = st.sb("sbb", [128, 8, 128], BF16, 2)
        qT = st.sb("qT", [128, 8, 128], BF16, 2)
        kT = st.sb("kT", [128, 8, 128], BF16, 2)
        qTf = st.sb("qTf", [128, 8, 128], BF16, 2)
        qTb = st.sb("qTb", [128, 8, 128], BF16, 2)
        Pm = st.sb("Pm", [128, 8, 128], BF16, 2)
        ro = st.sb("ro", [128, 8, 128], F32, 2)
        rsq = st.sb("rsq", [128, 8, 128], F32, 2)
        gst = st.sb("gst", [128, 32], F32, 2)
        yr = st.sb("yr", [128, D], BF16, 2)
        ys = st.sb("ys", [128, D], BF16, 2)
        def body(i):
            for (buf, src) in ((q_t, RQ), (k_t, RK), (g_t, RG), (su_t, SU), (sv_t, SV)):
                LD([], [buf.c(i)], lambda e, buf=buf, src=src, i=i: e.dma_start(out=buf.t(i)[:], in_=rows(src, i)))
            LD([], [v_t.c(i)], lambda e, i=i: e.dma_start(out=v_t.t(i)[:].rearrange("p a b -> p (a b)"), in_=rows(RV, i)))
            LD([], [sf_t.c(i)], lambda e, i=i: e.dma_start(out=sf_t.t(i)[:].rearrange("p a b -> p (a b)"), in_=rows(SF, i)))
            LD([], [sb_t.c(i)], lambda e, i=i: e.dma_start(out=sb_t.t(i)[:].rearrange("p a b -> p (a b)"), in_=rows(SB, i)))
            yield
            pq = next_pt()
            transposes(lambda j: q_t.t(i)[:, j * 128:(j + 1) * 128], 8, pq, [q_t.c(i)])
            ptq = PT.t(pq)[:].rearrange("p (a b) -> p a b", a=8)
            A([PT.c(pq)], [qT.c(i)], lambda e, ptq=ptq, i=i: e.copy(out=qT.t(i)[:], in_=ptq))
            V([PT.c(pq), XIF.c()], [qTf.c(i)], lambda e, ptq=ptq, i=i: e.tensor_tensor(
                out=qTf.t(i)[:], in0=ptq, in1=XIF.t()[:], op=ALU.mult))
            V([PT.c(pq), XIB.c()], [qTb.c(i)], lambda e, ptq=ptq, i=i: e.tensor_tensor(
                out=qTb.t(i)[:], in0=ptq, in1=XIB.t()[:], op=ALU.mult))
            to_T(k_t, i, 8, kT, i, copy_eng="act")
            yield
            p = next_pa()

            def fa(e, p=p, i=i):
                ins = None
                for h in range(8):
                    ins = e.matmul(PA.t(p)[:, h * 128:(h + 1) * 128], lhsT=kT.t(i)[:, h, :], rhs=qT.t(i)[:, h, :],
                                   start=True, stop=True)
                return ins
            P([kT.c(i), qT.c(i)], [PA.c(p, 0), PA.c(p, 1)], fa)
            V([PA.c(p, 0), PA.c(p, 1), DTb.c()], [Pm.c(i)], lambda e, p=p, i=i: e.tensor_tensor(
                out=Pm.t(i)[:], in0=PA.t(p)[:].rearrange("p (a b) -> p a b", a=8), in1=DTb.t()[:], op=ALU.mult))
            yield
            p2 = next_pa()

            def fo(e, p2=p2, i=i):
                ins = None
                for h in range(8):
                    o = PA.t(p2)[:, h * 128:(h + 1) * 128]
                    e.matmul(o, lhsT=Pm.t(i)[:, h, :], rhs=v_t.t(i)[:, h, :], start=True, stop=False)
                    e.matmul(o, lhsT=qTf.t(i)[:, h, :], rhs=sf_t.t(i)[:, h, :], start=False, stop=False)
                    ins = e.matmul(o, lhsT=qTb.t(i)[:, h, :], rhs=sb_t.t(i)[:, h, :], start=False, stop=True)
                return ins
            P([Pm.c(i), v_t.c(i), qTf.c(i), qTb.c(i), sf_t.c(i), sb_t.c(i)], [PA.c(p2, 0), PA.c(p2, 1)], fo)
            A([PA.c(p2, 0), PA.c(p2, 1)], [ro.c(i)], lambda e, p2=p2, i=i: e.copy(
                out=ro.t(i)[:].rearrange("p a b -> p (a b)"), in_=PA.t(p2)[:]))
            yield
            gs_ = gst.t(i)
            V([ro.c(i)], [gst.c(i, 0)], lambda e, i=i, gs_=gs_: e.tensor_reduce(
                out=gs_[:, 0:8], in_=ro.t(i)[:], axis=AX.X, op=ALU.add))
            G([ro.c(i)], [rsq.c(i)], lambda e, i=i: e.tensor_tensor(out=rsq.t(i)[:], in0=ro.t(i)[:], in1=ro.t(i)[:], op=ALU.mult))
            V([rsq.c(i)], [gst.c(i, 1)], lambda e, i=i, gs_=gs_: e.tensor_reduce(
                out=gs_[:, 8:16], in_=rsq.t(i)[:], axis=AX.X, op=ALU.add))
            V([gst.c(i, 0)], [gst.c(i, 0)], lambda e, gs_=gs_: e.tensor_scalar(
                out=gs_[:, 0:8], in0=gs_[:, 0:8], scalar1=1.0 / 128.0, scalar2=None, op0=ALU.mult))
            V([gst.c(i, 0)], [gst.c(i, 2)], lambda e, gs_=gs_: e.tensor_tensor(
                out=gs_[:, 16:24], in0=gs_[:, 0:8], in1=gs_[:, 0:8], op=ALU.mult))
            V([gst.c(i, 1), gst.c(i, 2)], [gst.c(i, 1)], lambda e, gs_=gs_: e.scalar_tensor_tensor(
                out=gs_[:, 8:16], in0=gs_[:, 8:16], scalar=1.0 / 128.0, in1=gs_[:, 16:24], op0=ALU.mult, op1=ALU.subtract))
            yield from rsqrt(gs_[:, 8:16], [gst.c(i, 1)], 1.0, LN_EPS)
            V([ro.c(i), gst.c(i, 0)], [ro.c(i)], lambda e, i=i, gs_=gs_: e.tensor_tensor(
                out=ro.t(i)[:], in0=ro.t(i)[:], in1=gs_[:, 0:8].unsqueeze(2).to_broadcast([128, 8, 128]), op=ALU.subtract))
            V([ro.c(i), gst.c(i, 1)], [ro.c(i)], lambda e, i=i, gs_=gs_: e.tensor_tensor(
                out=ro.t(i)[:], in0=ro.t(i)[:], in1=gs_[:, 8:16].unsqueeze(2).to_broadcast([128, 8, 128]), op=ALU.mult))
            G([ro.c(i), gnw.c()], [ro.c(i)], lambda e, i=i: e.tensor_tensor(
                out=ro.t(i)[:].rearrange("p a b -> p (a b)"), in0=ro.t(i)[:].rearrange("p a b -> p (a b)"),
                in1=gnw.t()[:], op=ALU.mult))
            G([ro.c(i), g_t.c(i)], [yr.c(i)], lambda e, i=i: e.tensor_tensor(
                out=yr.t(i)[:], in0=ro.t(i)[:].rearrange("p a b -> p (a b)"), in1=g_t.t(i)[:], op=ALU.mult))
            STO([yr.c(i)], [], lambda e, i=i: e.dma_start(out=rows(YR, i), in_=yr.t(i)[:]))
            yield
            p3 = next_pa()

            def fs(e, p3=p3, i=i):
                ins = None
                for g in range(4):
                    ins = e.matmul(PA.t(p3)[:, g * 256:(g + 1) * 256], lhsT=wsT.t()[:, g, :],
                                   rhs=sv_t.t(i)[:, g * 256:(g + 1) * 256], start=True, stop=True)
                return ins
            P([wsT.c(), sv_t.c(i)], [PA.c(p3, 0), PA.c(p3, 1)], fs)

            def fy(e, p3=p3, i=i):
                ins = None
                for g in range(4):
                    ins = e.scalar_tensor_tensor(out=ys.t(i)[:, g * 256:(g + 1) * 256], in0=PA.t(p3)[:, g * 256:(g + 1) * 256],
                                                 scalar=sgbt.t()[:, g:g + 1], in1=su_t.t(i)[:, g * 256:(g + 1) * 256],
                                                 op0=ALU.add, op1=ALU.mult)
                return ins
            V([PA.c(p3, 0), PA.c(p3, 1), sgbt.c(), su_t.c(i)], [ys.c(i)], fy)
            STO([ys.c(i)], [], lambda e, i=i: e.dma_start(out=rows(YS, i), in_=ys.t(i)[:]))
        run_tiles(body, NT)
        st.close()
        if stop_after is not None and stop_after == (lt + "r"):
            tr.barrier()
            return nc, tr

        for s in range(NSEG):
            tiles = cfg.seg_tiles(s)
            is_samp = (s == cfg.NP)
            nkb = (cfg.DSEQ // 128) if is_samp else cfg.TP
            st = Stage(nc, tr, lt + "t%d" % s)
            KT = st.sb("KT", [128, 2, nkb * 128], BF16)
            Vt = st.sb("Vt", [128, nkb, 256], BF16)
            kld = st.sb("kld", [128, 256], BF16, 3)
            if is_samp:
                ksrc = CCout_bf
                koff = 0
            else:
                ksrc = AKV
                koff = tiles[0] * 128
            for kb in range(nkb):
                r0 = koff + kb * 128
                if is_samp:
                    r0 = ((kb * 128) // cfg.SSEG) * CR + (kb * 128) % cfg.SSEG
                LD([], [Vt.c(0, kb)], lambda e, kb=kb, r0=r0: e.dma_start(out=Vt.t()[:, kb, :], in_=ksrc[r0:r0 + 128, 256:512]))
                LD([], [kld.c(kb)], lambda e, kb=kb, r0=r0: e.dma_start(out=kld.t(kb)[:], in_=ksrc[r0:r0 + 128, 0:256]))
                pq = next_pt()
                transposes(lambda j: kld.t(kb)[:, j * 128:(j + 1) * 128], 2, pq, [kld.c(kb)])
                V([PT.c(pq)], [KT.c(0, kb)], lambda e, kb=kb, pq=pq: e.tensor_copy(
                    out=KT.t()[:, :, kb * 128:(kb + 1) * 128], in_=PT.t(pq)[:, 0:256].rearrange("p (a b) -> p a b", a=2)))
            aq_t = st.sb("aq", [128, D], BF16, 3)
            qTg = st.sb("qTg", [128, 8, 512], BF16, 2)
            PTs = st.sb("PTs", [128, 512], BF16, 4)
            accv = st.sb("accv", [128, 512], F32, 2)
            rden = st.sb("rden", [128, 512], F32, 2)
            yat = st.sb("yat", [128, 4, 8, 128], BF16, 2)
            ngrp = (len(tiles) + 3) // 4
            LA = 2
            pending = []
            jc = [0]
            hc = [0]
            SSL = [(0, 0), (0, 1), (1, 0)]

            def q_prep(gi, gt):
                for ti, i in enumerate(gt):
                    LD([], [aq_t.c(i)], lambda e, i=i: e.dma_start(out=aq_t.t(i)[:], in_=rows(AQ, i)))
                    pq = next_pt()
                    transposes(lambda j: aq_t.t(i)[:, j * 128:(j + 1) * 128], 8, pq, [aq_t.c(i)])
                    V([PT.c(pq)], [qTg.c(gi, ti)], lambda e, pq=pq, ti=ti, gi=gi: e.tensor_copy(
                        out=qTg.t(gi)[:, :, ti * 128:(ti + 1) * 128], in_=PT.t(pq)[:].rearrange("p (a b) -> p a b", a=8)))

            def rest(gi, gt, h, kb, j, hh):
                nq = len(gt) * 128
                g = h // 4
                pi, half = SSL[j % 3]
                sps = PA.t(pi)[:, half * 512:half * 512 + nq]
                A([PA.c(pi, half), negM_c], [PTs.c(j)], lambda e: e.activation(
                    out=PTs.t(j)[:, 0:nq], in_=sps, func=AF.Exp, bias=negM, scale=128.0 ** -0.5))
                P([PTs.c(j), Vt.c(0, kb)], [PB.c(hh)], lambda e: e.matmul(
                    PB.t(hh)[:, 0:nq], lhsT=Vt.t()[:, kb, g * 128:(g + 1) * 128], rhs=PTs.t(j)[:, 0:nq],
                    start=(kb == 0), stop=(kb == nkb - 1)))
                dps = PA.t(1)[:, 512:512 + nq]
                if kb % 2 == 1:
                    P([PTs.c(j), ("ones",)], [PA.c(1, 1)], lambda e: e.matmul(
                        dps, lhsT=ones[:], rhs=PTs.t(j)[:, 0:nq], start=(kb == 1), stop=False))
                elif kb == 0:
                    V([PTs.c(j)], [accv.c(hh)], lambda e: e.tensor_copy(out=accv.t(hh)[:, 0:nq], in_=PTs.t(j)[:, 0:nq]))
                else:
                    V([PTs.c(j), accv.c(hh)], [accv.c(hh)], lambda e: e.tensor_tensor(
                        out=accv.t(hh)[:, 0:nq], in0=accv.t(hh)[:, 0:nq], in1=PTs.t(j)[:, 0:nq], op=ALU.add))
                if kb == nkb - 1:
                    P([accv.c(hh), ("ones32",)], [PA.c(1, 1)], lambda e: e.matmul(
                        dps, lhsT=ones32[:], rhs=accv.t(hh)[:, 0:nq], start=(nkb == 1), stop=True))
                    V([PA.c(1, 1)], [rden.c(hh)], lambda e: e.reciprocal(out=rden.t(hh)[:, 0:nq], in_=dps))
                    V([PB.c(hh), rden.c(hh)], [yat.c(gi, h)], lambda e: e.tensor_tensor(
                        out=yat.t(gi)[:, 0:len(gt), h, :], in0=PB.t(hh)[:, 0:nq].rearrange("p (t q) -> p t q", q=128),
                        in1=rden.t(hh)[:, 0:nq].rearrange("p (t q) -> p t q", q=128), op=ALU.mult))
                    if h == 7:
                        for ti, i in enumerate(gt):
                            STO([yat.c(gi, hx) for hx in range(8)], [], lambda e, ti=ti, i=i: e.dma_start(
                                out=rows(YAT, i), in_=yat.t(gi)[:, ti, :, :].rearrange("p a b -> p (a b)")))

            for gi in range(ngrp):
                gt = tiles[gi * 4:(gi + 1) * 4]
                nq = len(gt) * 128
                q_prep(gi, gt)
                qcells = [qTg.c(gi, ti) for ti in range(len(gt))]
                for h in range(8):
                    g = h // 4
                    hh = hc[0]; hc[0] += 1
                    for kb in range(nkb):
                        j = jc[0]; jc[0] += 1
                        pi, half = SSL[j % 3]
                        sps = PA.t(pi)[:, half * 512:half * 512 + nq]
                        P([KT.c(0, kb)] + qcells, [PA.c(pi, half)], lambda e, kb=kb, g=g, h=h, sps=sps, gi=gi, nq=nq: e.matmul(
                            sps, lhsT=KT.t()[:, g, kb * 128:(kb + 1) * 128], rhs=qTg.t(gi)[:, h, 0:nq], start=True, stop=True))
                        pending.append((gi, gt, h, kb, j, hh))
                        if len(pending) > LA:
                            rest(*pending.pop(0))
            while pending:
                rest(*pending.pop(0))
            st.close()
            if stop_after is not None and stop_after == (lt + "t%d" % s):
                tr.barrier()
                return nc, tr

        st = Stage(nc, tr, lt + "c")
        wro = load_w(st, "wro", ret_wo[l], 8, D)
        wso = load_w(st, "wso", sg_wo[l], 8, D)
        wao = load_w(st, "wao", att_wo[l], 8, D)
        wout = load_w(st, "wout", w_out[l], 8, D)
        lw0 = load_bcast(st, "lw0", ln_w[l, 0:1, :], D)
        lb0 = load_bcast(st, "lb0", ln_b[l, 0:1, :], D)
        yr_t = st.sb("yr", [128, D], BF16, 2)
        ys_t = st.sb("ys", [128, D], BF16, 2)
        yaT = st.sb("yaT", [128, 8, 128], BF16, 2)
        gt_t = st.sb("gt", [128, 3 * D], BF16, 2)
        xr = st.sb("xr", [128, D], F32, 2)
        yrT = st.sb("yrT", [128, 8, 128], BF16, 2)
        ysT = st.sb("ysT", [128, 8, 128], BF16, 2)
        mg = st.sb("mg", [128, D], F32, 2)
        tmpm = st.sb("tmpm", [128, D], F32, 2)
        mgb = st.sb("mgb", [128, D], BF16, 2)
        mT = st.sb("mT", [128, 8, 128], BF16, 2)
        x1 = st.sb("x1", [128, D], F32, 2)
        lbf = ln_bufs(st, "ln")
        def body(i):
            LD([], [yr_t.c(i)], lambda e, i=i: e.dma_start(out=yr_t.t(i)[:], in_=rows(YR, i)))
            LD([], [ys_t.c(i)], lambda e, i=i: e.dma_start(out=ys_t.t(i)[:], in_=rows(YS, i)))
            LD([], [yaT.c(i)], lambda e, i=i: e.dma_start(out=yaT.t(i)[:].rearrange("p a b -> p (a b)"), in_=rows(YAT, i)))
            LD([], [gt_t.c(i)], lambda e, i=i: e.dma_start(out=gt_t.t(i)[:], in_=rows(GT, i)))
            LD([], [xr.c(i)], lambda e, i=i: e.dma_start(out=xr.t(i)[:], in_=rows(X, i)))
            to_T(yr_t, i, 8, yrT, i, copy_eng="act")
            to_T(ys_t, i, 8, ysT, i, copy_eng="dve")
            yield
            for bi, (srcT, wbuf) in enumerate(((yrT, wro), (ysT, wso), (yaT, wao))):
                p = next_pa()
                for cbk in range(2):
                    gemm(lambda k: srcT.t(i)[:, k, :], 8, wbuf, cbk * 512, 512,
                         PA.t(p)[:, cbk * 512:(cbk + 1) * 512], [srcT.c(i)], [PA.c(p, cbk)])
                gsl = gt_t.t(i)[:, bi * 1024:(bi + 1) * 1024]
                if bi == 0:
                    V([PA.c(p, 0), PA.c(p, 1), gt_t.c(i)], [mg.c(i)], lambda e, p=p, gsl=gsl, i=i: e.tensor_tensor(
                        out=mg.t(i)[:], in0=PA.t(p)[:], in1=gsl, op=ALU.mult))
                else:
                    V([PA.c(p, 0), PA.c(p, 1), gt_t.c(i)], [tmpm.c(i)], lambda e, p=p, gsl=gsl, i=i: e.tensor_tensor(
                        out=tmpm.t(i)[:], in0=PA.t(p)[:], in1=gsl, op=ALU.mult))
                    if bi == 1:
                        G([mg.c(i), tmpm.c(i)], [mg.c(i)], lambda e, i=i: e.tensor_tensor(
                            out=mg.t(i)[:], in0=mg.t(i)[:], in1=tmpm.t(i)[:], op=ALU.add))
                    else:
                        G([mg.c(i), tmpm.c(i)], [mgb.c(i)], lambda e, i=i: e.tensor_tensor(
                            out=mgb.t(i)[:], in0=mg.t(i)[:], in1=tmpm.t(i)[:], op=ALU.add))
                yield
            to_T(mgb, i, 8, mT, i, copy_eng="act")
            yield
            p = next_pa()
            for cbk in range(2):
                gemm(lambda k: mT.t(i)[:, k, :], 8, wout, cbk * 512, 512,
                     PA.t(p)[:, cbk * 512:(cbk + 1) * 512], [mT.c(i)], [PA.c(p, cbk)])
            V([PA.c(p, 0), PA.c(p, 1), xr.c(i)], [x1.c(i)], lambda e, p=p, i=i: e.scalar_tensor_tensor(
                out=x1.t(i)[:], in0=xr.t(i)[:], scalar=ALPHA, in1=PA.t(p)[:], op0=ALU.mult, op1=ALU.add))
            yield from layer_norm(x1.t(i)[:], [x1.c(i)], lbf, i, lw0, lb0, x1.t(i)[:], [x1.c(i)])
            STO([x1.c(i)], [], lambda e, i=i: e.dma_start(out=rows(X1, i), in_=x1.t(i)[:]))
        run_tiles(body, NT)
        st.close()
        if stop_after is not None and stop_after == (lt + "c"):
            tr.barrier()
            return nc, tr

        st = Stage(nc, tr, lt + "x")
        wxq = load_w(st, "wxq", xa_wq[l], 8, D)
        wxo = load_w(st, "wxo", xa_wo[l], 8, D)
        lw1 = load_bcast(st, "lw1", ln_w[l, 1:2, :], D)
        lb1 = load_bcast(st, "lb1", ln_b[l, 1:2, :], D)
        x1 = st.sb("x1", [128, D], F32, 2)
        x1b = st.sb("x1b", [128, D], BF16, 2)
        x1T = st.sb("x1T", [128, 8, 128], BF16, 2)
        qxT = st.sb("qxT", [128, 8, 128], BF16, 2)
        mk_t = st.sb("mk", [128, 2, 8, 128], BF16, 2)
        mv_t = st.sb("mvv", [128, 2, D], BF16, 2)
        pex = st.sb("pex", [128, 8, 128], BF16, 2)
        rdx = st.sb("rdx", [128, 4, 128], F32, 2)
        oT = st.sb("oT", [128, 8, 128], BF16, 2)
        x2 = st.sb("x2", [128, D], F32, 2)
        lbf = ln_bufs(st, "ln")
        cur_seg = [-1]
        segc = [0]
        def body(i):
            s = cfg.tile_seg(i)
            if s != cur_seg[0]:
                cur_seg[0] = s
                segc[0] += 1
                sc = segc[0]
                for kb in range(2):
                    LD([], [mk_t.c(sc, kb)], lambda e, kb=kb, sc=sc, s=s: e.dma_start(
                        out=mk_t.t(sc)[:, kb, :, :].rearrange("p a b -> p (a b)"), in_=rows(MK, s * 2 + kb)))
                    LD([], [mv_t.c(sc, kb)], lambda e, kb=kb, sc=sc, s=s: e.dma_start(out=mv_t.t(sc)[:, kb, :], in_=rows(MV, s * 2 + kb)))
            sc = segc[0]
            LD([], [x1.c(i)], lambda e, i=i: e.dma_start(out=x1.t(i)[:], in_=rows(X1, i)))
            A([x1.c(i)], [x1b.c(i)], lambda e, i=i: e.copy(out=x1b.t(i)[:], in_=x1.t(i)[:]))
            to_T(x1b, i, 8, x1T, i, copy_eng="dve")
            yield
            p = next_pa()

            def fq(e, p=p, i=i):
                ins = None
                for j in range(8):
                    for k in range(8):
                        ins = e.matmul(PA.t(p)[:, j * 128:(j + 1) * 128], lhsT=wxq.t()[:, k, j * 128:(j + 1) * 128],
                                       rhs=x1T.t(i)[:, k, :], start=(k == 0), stop=(k == 7))
                return ins
            P([wxq.c(), x1T.c(i)], [PA.c(p, 0), PA.c(p, 1)], fq)
            A([PA.c(p, 0), PA.c(p, 1)], [qxT.c(i)], lambda e, p=p, i=i: e.copy(
                out=qxT.t(i)[:].rearrange("p a b -> p (a b)"), in_=PA.t(p)[:]))
            yield
            p = next_pa()

            def fsx(e, p=p, i=i, sc=sc):
                ins = None
                for h in range(4):
                    for kb in range(2):
                        o = PA.t(p)[:, (h * 2 + kb) * 128:(h * 2 + kb + 1) * 128]
                        for c2 in range(2):
                            ins = e.matmul(o, lhsT=mk_t.t(sc)[:, kb, h * 2 + c2, :], rhs=qxT.t(i)[:, h * 2 + c2, :],
                                           start=(c2 == 0), stop=(c2 == 1))
                return ins
            P([mk_t.c(sc, 0), mk_t.c(sc, 1), qxT.c(i)], [PA.c(p, 0), PA.c(p, 1)], fsx)
            A([PA.c(p, 0), PA.c(p, 1)], [pex.c(i)], lambda e, p=p, i=i: e.activation(
                out=pex.t(i)[:].rearrange("p a b -> p (a b)"), in_=PA.t(p)[:], func=AF.Exp, scale=256.0 ** -0.5))
            yield
            pbi = i % 2

            def fden(e, i=i, pbi=pbi):
                ins = None
                for h in range(4):
                    for kb in range(2):
                        ins = e.matmul(PB.t(pbi)[:, h * 128:(h + 1) * 128], lhsT=ones[:], rhs=pex.t(i)[:, h * 2 + kb, :],
                                       start=(kb == 0), stop=(kb == 1))
                return ins
            P([pex.c(i), ("ones",)], [PB.c(pbi)], fden)
            p = next_pa()

            def fo2(e, p=p, i=i, sc=sc):
                ins = None
                for h in range(4):
                    for c2 in range(2):
                        o = PA.t(p)[:, (h * 2 + c2) * 128:(h * 2 + c2 + 1) * 128]
                        for kb in range(2):
                            ins = e.matmul(o, lhsT=mv_t.t(sc)[:, kb, (h * 2 + c2) * 128:(h * 2 + c2 + 1) * 128],
                                           rhs=pex.t(i)[:, h * 2 + kb, :], start=(kb == 0), stop=(kb == 1))
                return ins
            P([pex.c(i), mv_t.c(sc, 0), mv_t.c(sc, 1)], [PA.c(p, 0), PA.c(p, 1)], fo2)
            V([PB.c(pbi)], [rdx.c(i)], lambda e, i=i, pbi=pbi: e.reciprocal(
                out=rdx.t(i)[:].rearrange("p a b -> p (a b)"), in_=PB.t(pbi)[:]))
            V([PA.c(p, 0), PA.c(p, 1), rdx.c(i)], [oT.c(i)], lambda e, p=p, i=i: e.tensor_tensor(
                out=oT.t(i)[:].rearrange("p (h c) q -> p h c q", c=2),
                in0=PA.t(p)[:].rearrange("p (h c q) -> p h c q", h=4, c=2),
                in1=rdx.t(i)[:].unsqueeze(2).to_broadcast([128, 4, 2, 128]), op=ALU.mult))
            yield
            p = next_pa()
            for cbk in range(2):
                gemm(lambda k: oT.t(i)[:, k, :], 8, wxo, cbk * 512, 512,
                     PA.t(p)[:, cbk * 512:(cbk + 1) * 512], [oT.c(i)], [PA.c(p, cbk)])
            V([PA.c(p, 0), PA.c(p, 1), x1.c(i)], [x2.c(i)], lambda e, p=p, i=i: e.scalar_tensor_tensor(
                out=x2.t(i)[:], in0=x1.t(i)[:], scalar=ALPHA, in1=PA.t(p)[:], op0=ALU.mult, op1=ALU.add))
            yield from layer_norm(x2.t(i)[:], [x2.c(i)], lbf, i, lw1, lb1, x2.t(i)[:], [x2.c(i)])
            STO([x2.c(i)], [], lambda e, i=i: e.dma_start(out=rows(X2, i), in_=x2.t(i)[:]))
        run_tiles(body, NT)
        st.close()
        if stop_after is not None and stop_after == (lt + "x"):
            tr.barrier()
            return nc, tr

        st = Stage(nc, tr, lt + "f")
        wfi = load_w(st, "wfi", ffn_w_in[l], 8, 2 * D_FF)
        wfo = load_w(st, "wfo", ffn_w_out[l], 22, D)
        lw2 = load_bcast(st, "lw2", ln_w[l, 2:3, :], D)
        lb2 = load_bcast(st, "lb2", ln_b[l, 2:3, :], D)
        xin = st.sb("xin", [128, D], F32, 2)
        xb = st.sb("xb", [128, D], BF16, 2)
        xT = st.sb("xT", [128, 8, 128], BF16, 2)
        sa = st.sb("sa", [128, 512], F32, 2)
        act = st.sb("act", [128, D_FF], BF16, 2)
        actT = st.sb("actT", [128, 22, 128], BF16, 2)
        lbf = ln_bufs(st, "ln")
        Xdst = X if l < L - 1 else y_out
        cc_ = [0]
        def body(i):
            load_xT(st, X2, i, xin, xb, xT)
            yield
            blocks = [(c0, 512) for c0 in range(0, 2560, 512)] + [(2560, 256)]
            for (c0, w_) in blocks:
                p = next_pa()
                gemm(lambda k: xT.t(i)[:, k, :], 8, wfi, c0, w_, PA.t(p)[:, 0:w_], [xT.c(i)], [PA.c(p, 0)])
                gemm(lambda k: xT.t(i)[:, k, :], 8, wfi, D_FF + c0, w_, PA.t(p)[:, 512:512 + w_], [xT.c(i)], [PA.c(p, 1)])
                j = cc_[0]; cc_[0] += 1
                A([PA.c(p, 0)], [sa.c(j)], lambda e, p=p, j=j, w_=w_: e.activation(
                    out=sa.t(j)[:, 0:w_], in_=PA.t(p)[:, 0:w_], func=AF.Silu))
                V([sa.c(j), PA.c(p, 1)], [act.c(i, c0)], lambda e, p=p, j=j, w_=w_, c0=c0, i=i: e.tensor_tensor(
                    out=act.t(i)[:, c0:c0 + w_], in0=sa.t(j)[:, 0:w_], in1=PA.t(p)[:, 512:512 + w_], op=ALU.mult))
                yield
            acells = [act.c(i, c0) for (c0, _) in blocks]
            for r in range(3):
                nb = min(8, 22 - r * 8)
                pq = next_pt()
                transposes(lambda j, r=r: act.t(i)[:, (r * 8 + j) * 128:(r * 8 + j + 1) * 128], nb, pq, acells)
                eng = "act" if r % 2 == 0 else "dve"
                if eng == "act":
                    A([PT.c(pq)], [actT.c(i, r)], lambda e, pq=pq, r=r, nb=nb, i=i: e.copy(
                        out=actT.t(i)[:, r * 8:r * 8 + nb, :].rearrange("p a b -> p (a b)"), in_=PT.t(pq)[:, 0:nb * 128]))
                else:
                    V([PT.c(pq)], [actT.c(i, r)], lambda e, pq=pq, r=r, nb=nb, i=i: e.tensor_copy(
                        out=actT.t(i)[:, r * 8:r * 8 + nb, :].rearrange("p a b -> p (a b)"), in_=PT.t(pq)[:, 0:nb * 128]))
            p = next_pa()
            for cbk in range(2):
                gemm(lambda k: actT.t(i)[:, k, :], 22, wfo, cbk * 512, 512,
                     PA.t(p)[:, cbk * 512:(cbk + 1) * 512], [actT.c(i, r) for r in range(3)], [PA.c(p, cbk)])
            V([PA.c(p, 0), PA.c(p, 1), xin.c(i)], [xin.c(i)], lambda e, p=p, i=i: e.scalar_tensor_tensor(
                out=xin.t(i)[:], in0=xin.t(i)[:], scalar=ALPHA, in1=PA.t(p)[:], op0=ALU.mult, op1=ALU.add))
            yield from layer_norm(xin.t(i)[:], [xin.c(i)], lbf, i, lw2, lb2, xin.t(i)[:], [xin.c(i)])
            STO([xin.c(i)], [], lambda e, i=i: e.dma_start(out=rows(Xdst, i), in_=xin.t(i)[:]))
        run_tiles(body, NT)
        st.close()
        if stop_after is not None and stop_after == (lt + "f"):
            tr.barrier()
            return nc, tr
        ls.close()

    tr.barrier()
    gs.close()
    return nc, tr


def _rope_tables(cfg, core):
    def cs(pos, dim):
        inv = (ROPE_BASE ** (-np.arange(0, dim, 2, dtype=np.float32) / np.float32(dim))).astype(np.float32)
        ang = pos.astype(np.float32)[:, None] * inv[None, :]
        return np.cos(ang).astype(np.float32), np.sin(ang).astype(np.float32)
    pos = np.concatenate([np.arange(cfg.SEQ), core * cfg.SSEG + np.arange(cfg.SSEG)]).astype(np.int64)
    c, s = cs(pos, 128)
    C = np.concatenate([c, c], 1)
    S = np.concatenate([-s, s], 1)
    sc = np.float32(128.0 ** -0.5)
    tab_r = np.concatenate([C, S, C * sc, S * sc], 1).astype(np.float32)
    cr, sr = cs(pos // GRID_W, 64)
    cc, s2 = cs(pos % GRID_W, 64)
    Ca = np.concatenate([cr, cr, cc, cc], 1)
    Sa = np.concatenate([-sr, sr, -s2, s2], 1)
    tab_a = np.concatenate([Ca, Sa], 1).astype(np.float32)
    return tab_r, tab_a


def _ctab():
    m = np.arange(128, dtype=np.float32)[:, None]
    c = np.arange(128, dtype=np.float32)[None, :]
    t = np.zeros((128, 1024), np.float32)
    t[:, 0:128] = np.maximum(c - m, 0)
    t[:, 128:256] = (c >= m)
    t[:, 256:384] = np.maximum(m - c, 0)
    t[:, 384:512] = (m > c)
    t[:, 512:640] = c + 1
    t[:, 640:768] = 128 - c
    t[:, 768] = 127 - m[:, 0]
    t[:, 769] = m[:, 0]
    for j in range(4):
        t[j, 772 + j] = 1.0
    return t


def _cctab(cfg, core):
    t = np.zeros((128, 32), np.float32)
    for c2 in range(cfg.NC):
        if c2 < core:
            t[:, c2] = cfg.SSEG * (core - 1 - c2)
            t[:, 16 + c2] = 1.0
        if c2 > core:
            t[:, 8 + c2] = cfg.SSEG * (c2 - core - 1)
            t[:, 24 + c2] = 1.0
    return t


def make_in_maps(cfg, inp):
    f = lambda a: np.ascontiguousarray(np.asarray(a, dtype=np.float32))
    shared = {
        "in_ln_w": f(inp["in_ln_w"]).reshape(1, D), "in_ln_b": f(inp["in_ln_b"]).reshape(1, D),
        "w_in": f(inp["w_in"]), "b_gate": f(inp["b_gate"]),
        "ret_decay": f(np.concatenate([inp["ret_decay_f"], inp["ret_decay_b"]], axis=1)),
        "ret_gn_w": f(inp["ret_gn_w"]), "ret_wo": f(inp["ret_wo"]),
        "sg_ln_w": f(inp["sg_ln_w"]), "sg_ln_b": f(inp["sg_ln_b"]), "sg_ws": f(inp["sg_ws"]), "sg_b": f(inp["sg_b"]),
        "sg_wo": f(inp["sg_wo"]), "att_qn_w": f(inp["att_qn_w"]), "att_kn_w": f(inp["att_kn_w"]),
        "att_wo": f(inp["att_wo"]), "w_out": f(inp["w_out"]), "ln_w": f(inp["ln_w"]), "ln_b": f(inp["ln_b"]),
        "xa_wq": f(inp["xa_wq"]), "xa_wkv": f(inp["xa_wkv"]), "xa_wo": f(inp["xa_wo"]),
        "ffn_w_in": f(inp["ffn_w_in"]), "ffn_w_out": f(inp["ffn_w_out"]),
        "ctab": _ctab(), "ident": np.eye(128, dtype=np.float32),
    }
    xp, xs = f(inp["x_prompt"]), f(inp["x_sample"])
    mp, ms = f(inp["mem_prompt"]), f(inp["mem_sample"])
    maps = []
    for c in range(cfg.NC):
        m = dict(shared)
        xpc = xp[c * cfg.NP:(c + 1) * cfg.NP].reshape(cfg.NP * cfg.SEQ, D)
        xsc = xs[0, c * cfg.SSEG:(c + 1) * cfg.SSEG]
        m["x"] = np.ascontiguousarray(np.concatenate([xpc, xsc], 0))
        m["mem"] = np.ascontiguousarray(np.concatenate([mp[c * cfg.NP:(c + 1) * cfg.NP].reshape(cfg.NP * N_MEM, D), ms[0]], 0))
        tr_, ta_ = _rope_tables(cfg, c)
        m["tab_r"], m["tab_a"] = tr_, ta_
        m["cctab"] = _cctab(cfg, c)
        maps.append(m)
    return maps


def run(cfg, inp, debug=()):
    nc, tr = build(cfg, debug)
    maps = make_in_maps(cfg, inp)
    res = run_bass_kernel_spmd(nc, maps, core_ids=list(range(cfg.NC)))
    return res


def assemble(cfg, res):
    yp = np.zeros((cfg.NP * cfg.NC, cfg.SEQ, D), np.float32)
    ys = np.zeros((1, cfg.DSEQ, D), np.float32)
    for c in range(cfg.NC):
        y = res.results[c]["y"]
        yp[c * cfg.NP:(c + 1) * cfg.NP] = y[:cfg.NP * cfg.SEQ].reshape(cfg.NP, cfg.SEQ, D)
        ys[0, c * cfg.SSEG:(c + 1) * cfg.SSEG] = y[cfg.NP * cfg.SEQ:]
    return yp, ys


def kernel(**inputs):
    cfg = Cfg()
    res = run(cfg, inputs)
    return assemble(cfg, res)
```
